# Optimizing a Trainium2 kernel written in Bass

```python
import jax, jax.numpy as jnp
from jax import lax
import numpy as np

D_MODEL = 4096
BATCH = 4
SEQ = 4096
DEPTH = 1
DEC_BATCH = 2
DEC_SEQ = 4096
PAST_LEN = 128

SSD_HEAD_DIM = 64
SSD_INNER = D_MODEL
SSD_HEADS = SSD_INNER // SSD_HEAD_DIM
SSD_GROUPS = 8
SSD_STATE = 128
SSD_CONV = 5
SSD_CHUNK = 128
SSD_CONV_DIM = SSD_INNER + 2 * SSD_GROUPS * SSD_STATE
SSD_NORM_GROUP = SSD_INNER // SSD_GROUPS
POOL_WINDOWS = (2, 4, 8, 16)
POOL_GROUPS = 4
POOL_GROUP_DIM = D_MODEL // 8
POOL_DIM = POOL_GROUPS * POOL_GROUP_DIM
D_FF = 4 * D_MODEL
PLE_DIM = 256
N_BRANCH = 2
EPS = 1e-6

IN_COLS = SSD_INNER + SSD_CONV_DIM + 2 * SSD_HEADS + POOL_DIM + N_BRANCH * D_MODEL
SPLITS = (
    SSD_INNER,
    SSD_INNER + SSD_CONV_DIM,
    SSD_INNER + SSD_CONV_DIM + SSD_HEADS,
    SSD_INNER + SSD_CONV_DIM + 2 * SSD_HEADS,
    SSD_INNER + SSD_CONV_DIM + 2 * SSD_HEADS + POOL_DIM,
    SSD_INNER + SSD_CONV_DIM + 2 * SSD_HEADS + POOL_DIM + D_MODEL,
)

kernel_name = "hybrid_ssd_pool_encoder"


def rms_norm(x, g):
    xf = x.astype(jnp.float32)
    xf = xf * lax.rsqrt(jnp.mean(xf * xf, axis=-1, keepdims=True) + EPS)
    return xf.astype(x.dtype) * g


def depthwise_centred_conv(u, w, b):
    out = lax.conv_general_dilated(
        u, w[:, None, :].astype(u.dtype), window_strides=(1,),
        padding=[(SSD_CONV // 2, SSD_CONV // 2)],
        dimension_numbers=('NWC', 'WIO', 'NWC'),
        feature_group_count=u.shape[-1])
    return out + b


def ssd_scan(x, dt, A, B, C):
    b, s, h, p = x.shape
    g, n = B.shape[2], B.shape[3]
    e = h // g
    nc = s // SSD_CHUNK
    L = SSD_CHUNK
    xc = (x * dt[..., None]).reshape(b, nc, L, g, e, p)
    a = (dt * A).reshape(b, nc, L, g, e)
    a_cum = jnp.cumsum(a, axis=2)
    Bc = B.reshape(b, nc, L, g, n)
    Cc = C.reshape(b, nc, L, g, n)
    seg = a_cum[:, :, :, None] - a_cum[:, :, None, :]
    mask = jnp.tril(jnp.ones((L, L), dtype=bool))[None, None, :, :, None, None]
    decay = jnp.exp(jnp.where(mask, seg, -jnp.inf))
    cb = jnp.einsum('bclgn,bcsgn->bclsg', Cc, Bc)
    y_diag = jnp.einsum('bclsg,bclsge,bcsgep->bclgep', cb, decay, xc)
    decay_to_end = jnp.exp(a_cum[:, :, -1:] - a_cum)
    states = jnp.einsum('bclgn,bclge,bclgep->bcgepn', Bc, decay_to_end, xc)
    chunk_decay = jnp.exp(a_cum[:, :, -1])

    def step(h_prev, inp):
        st, dec = inp
        h_new = dec[..., None, None] * h_prev + st
        return h_new, h_prev

    h0 = jnp.zeros((b, g, e, p, n), dtype=states.dtype)
    _, prev = lax.scan(step, h0, (jnp.swapaxes(states, 0, 1), jnp.swapaxes(chunk_decay, 0, 1)))
    prev = jnp.swapaxes(prev, 0, 1)
    y_off = jnp.einsum('bclgn,bcgepn,bclge->bclgep', Cc, prev, jnp.exp(a_cum))
    return (y_diag + y_off).reshape(b, s, h, p).astype(x.dtype)


def multiscale_pool(u, pool_w, pool_scale):
    b, s, _ = u.shape
    uf = u.astype(jnp.float32).reshape(b, s, POOL_GROUPS, POOL_GROUP_DIM)
    cs = jnp.concatenate([jnp.zeros((b, 1, POOL_GROUPS, POOL_GROUP_DIM), jnp.float32),
                          jnp.cumsum(uf, axis=1)], axis=1)
    t = jnp.arange(s)
    outs = []
    for gi, w in enumerate(POOL_WINDOWS):
        lo = jnp.clip(t - w // 2, 0, s)
        hi = jnp.clip(t + (w - w // 2), 0, s)
        csg = cs[:, :, gi]
        mean = (csg[:, hi] - csg[:, lo]) / (hi - lo).astype(jnp.float32)[None, :, None]
        outs.append(mean - uf[:, :, gi])
    pooled = jnp.stack(outs, axis=2).astype(u.dtype)
    y = jnp.einsum('bsgc,gcd->bsgd', pooled, pool_w).reshape(b, s, POOL_DIM)
    return y * pool_scale


def encoder_layer(x, p_l, norm_mix_g, w_in, conv_w, conv_b, dt_bias_f, dt_bias_b,
                  a_log_f, a_log_b, d_skip, ssd_norm_g, w_ssd_up, pool_w, pool_scale,
                  w_pool_up, w_out, norm_mlp_g, w_ff1, w_ff2, norm_ple_g, w_ple_gate,
                  w_ple_proj):
    b, s, _ = x.shape
    h = rms_norm(x, norm_mix_g)
    proj = h @ w_in
    z, xbc, dt_f, dt_b, u_pool, g_a, g_b = jnp.split(proj, SPLITS, axis=-1)

    xbc = jax.nn.silu(depthwise_centred_conv(xbc, conv_w, conv_b))
    xs, Bm, Cm = jnp.split(xbc, (SSD_INNER, SSD_INNER + SSD_GROUPS * SSD_STATE), axis=-1)
    xs = xs.reshape(b, s, SSD_HEADS, SSD_HEAD_DIM)
    Bm = Bm.reshape(b, s, SSD_GROUPS, SSD_STATE)
    Cm = Cm.reshape(b, s, SSD_GROUPS, SSD_STATE)
    dtf = jax.nn.softplus((dt_f + dt_bias_f).astype(jnp.float32))
    dtb = jax.nn.softplus((dt_b + dt_bias_b).astype(jnp.float32))
    A_f = -jnp.exp(a_log_f.astype(jnp.float32))
    A_b = -jnp.exp(a_log_b.astype(jnp.float32))
    y_f = ssd_scan(xs, dtf, A_f, Bm, Cm)
    y_b = jnp.flip(ssd_scan(jnp.flip(xs, 1), jnp.flip(dtb, 1), A_b,
                            jnp.flip(Bm, 1), jnp.flip(Cm, 1)), 1)
    y = y_f + y_b + d_skip[:, None] * xs
    y = y.reshape(b, s, SSD_INNER) * jax.nn.silu(z)
    y = rms_norm(y.reshape(b, s, SSD_GROUPS, SSD_NORM_GROUP), 1.0).reshape(b, s, SSD_INNER) * ssd_norm_g
    branch_a = y @ w_ssd_up

    branch_b = multiscale_pool(u_pool, pool_w, pool_scale) @ w_pool_up

    mixed = jax.nn.sigmoid(g_a) * branch_a + jax.nn.sigmoid(g_b) * branch_b
    x = x + mixed @ w_out

    h2 = rms_norm(x, norm_mlp_g)
    x = x + jnp.square(jax.nn.relu(h2 @ w_ff1)) @ w_ff2

    gate = jax.nn.sigmoid(rms_norm(x, norm_ple_g) @ w_ple_gate)
    x = x + gate * (p_l @ w_ple_proj)
    return x


def run_trunk(x, p, layer_params, norm_final_g):
    for i in range(DEPTH):
        x = encoder_layer(x, p[i], *[w[i] for w in layer_params])
    return rms_norm(x, norm_final_g)


def setup_inputs(seed: int = 0) -> dict:
    key = jax.random.key(seed)
    ks = jax.random.split(key, 32)
    f32 = jnp.float32

    def nrm(k, shape, scale):
        return jax.random.normal(k, shape, f32) * scale

    def gain(k, shape):
        return 1.0 + 0.02 * jax.random.normal(k, shape, f32)

    dt_init_f = jnp.exp(jax.random.uniform(ks[8], (DEPTH, SSD_HEADS), f32, np.log(1e-3), np.log(1e-1)))
    dt_init_b = jnp.exp(jax.random.uniform(ks[9], (DEPTH, SSD_HEADS), f32, np.log(1e-3), np.log(1e-1)))
    return {
        "x_prompt": nrm(ks[0], (BATCH, SEQ, D_MODEL), 1.0),
        "x_sample": nrm(ks[1], (DEC_BATCH, DEC_SEQ, D_MODEL), 1.0),
        "p_prompt": nrm(ks[2], (DEPTH, BATCH, SEQ, PLE_DIM), 1.0),
        "p_sample": nrm(ks[3], (DEPTH, DEC_BATCH, DEC_SEQ, PLE_DIM), 1.0),
        "norm_mix_g": gain(ks[4], (DEPTH, D_MODEL)),
        "w_in": nrm(ks[5], (DEPTH, D_MODEL, IN_COLS), D_MODEL ** -0.5),
        "conv_w": nrm(ks[6], (DEPTH, SSD_CONV, SSD_CONV_DIM), SSD_CONV ** -0.5),
        "conv_b": nrm(ks[7], (DEPTH, SSD_CONV_DIM), 0.02),
        "dt_bias_f": dt_init_f + jnp.log(-jnp.expm1(-dt_init_f)),
        "dt_bias_b": dt_init_b + jnp.log(-jnp.expm1(-dt_init_b)),
        "a_log_f": jnp.log(jax.random.uniform(ks[10], (DEPTH, SSD_HEADS), f32, 1.0, 16.0)),
        "a_log_b": jnp.log(jax.random.uniform(ks[11], (DEPTH, SSD_HEADS), f32, 1.0, 16.0)),
        "d_skip": gain(ks[12], (DEPTH, SSD_HEADS)),
        "ssd_norm_g": gain(ks[13], (DEPTH, SSD_INNER)),
        "w_ssd_up": nrm(ks[14], (DEPTH, SSD_INNER, D_MODEL), SSD_INNER ** -0.5),
        "pool_w": nrm(ks[15], (DEPTH, POOL_GROUPS, POOL_GROUP_DIM, POOL_GROUP_DIM), POOL_GROUP_DIM ** -0.5),
        "pool_scale": gain(ks[16], (DEPTH, POOL_DIM)),
        "w_pool_up": nrm(ks[17], (DEPTH, POOL_DIM, D_MODEL), POOL_DIM ** -0.5),
        "w_out": nrm(ks[18], (DEPTH, D_MODEL, D_MODEL), D_MODEL ** -0.5),
        "norm_mlp_g": gain(ks[19], (DEPTH, D_MODEL)),
        "w_ff1": nrm(ks[20], (DEPTH, D_MODEL, D_FF), D_MODEL ** -0.5),
        "w_ff2": nrm(ks[21], (DEPTH, D_FF, D_MODEL), D_FF ** -0.5),
        "norm_ple_g": gain(ks[22], (DEPTH, D_MODEL)),
        "w_ple_gate": nrm(ks[23], (DEPTH, D_MODEL, D_MODEL), D_MODEL ** -0.5),
        "w_ple_proj": nrm(ks[24], (DEPTH, PLE_DIM, D_MODEL), PLE_DIM ** -0.5),
        "norm_final_g": gain(ks[25], (D_MODEL,)),
    }


def reference(x_prompt, x_sample, p_prompt, p_sample, norm_mix_g, w_in, conv_w, conv_b,
              dt_bias_f, dt_bias_b, a_log_f, a_log_b, d_skip, ssd_norm_g, w_ssd_up,
              pool_w, pool_scale, w_pool_up, w_out, norm_mlp_g, w_ff1, w_ff2,
              norm_ple_g, w_ple_gate, w_ple_proj, norm_final_g):
    layer_params = (norm_mix_g, w_in, conv_w, conv_b, dt_bias_f, dt_bias_b, a_log_f, a_log_b,
                    d_skip, ssd_norm_g, w_ssd_up, pool_w, pool_scale, w_pool_up, w_out,
                    norm_mlp_g, w_ff1, w_ff2, norm_ple_g, w_ple_gate, w_ple_proj)
    y_prompt = run_trunk(x_prompt, p_prompt, layer_params, norm_final_g)
    y_sample = run_trunk(x_sample, p_sample, layer_params, norm_final_g)
    return (y_prompt, y_sample)
```

```python
import numpy as np
from contextlib import ExitStack
import concourse.bass as bass
import concourse.mybir as mybir
from concourse.bass_utils import run_bass_kernel_spmd

F32 = mybir.dt.float32
BF16 = mybir.dt.bfloat16
AF = mybir.ActivationFunctionType
ALU = mybir.AluOpType

D = 4096
NH = 64
HD = 64
NG = 8
XBC = 6144
POOLD = 2048
DFF = 16384
PLE = 256
IN_COLS = 20608
C_Z, C_XBC, C_DT, C_UP, C_GA, C_GB = 0, 4096, 10240, 10368, 12416, 16512
EPS = 1e-6
KC = D // 128
SEQ = 4096


class Buf:
    __slots__ = ("name", "lw", "rd", "lsem", "ssem")

    def __init__(self, name):
        self.name = name
        self.lw = None
        self.rd = {}
        self.lsem = None
        self.ssem = None


class Sem:
    __slots__ = ("h", "cnt", "idx")

    def __init__(self, h, idx):
        self.h = h
        self.cnt = 0
        self.idx = idx


class Eng:
    def __init__(self, h, sem):
        self.h = h
        self.sem = sem
        self.seen = {}


class KB:
    def __init__(self, nc, es, nsem=100):
        self.nc = nc
        self.sems = [Sem(es.enter_context(nc.semaphore(f"s{i}")), i) for i in range(nsem)]
        self.pe = Eng(nc.tensor, self.sems[0])
        self.act = Eng(nc.scalar, self.sems[1])
        self.dve = Eng(nc.vector, self.sems[2])
        self.pool = Eng(nc.gpsimd, self.sems[3])
        self.sp = Eng(nc.sync, None)
        self.engs = [self.pe, self.act, self.dve, self.pool, self.sp]
        self.free = list(self.sems[4:])
        self.used = []
        self.bufs = []

    def buf(self, name):
        b = Buf(name)
        self.bufs.append(b)
        return b

    def _dsem(self):
        s = self.free.pop()
        self.used.append(s)
        return s

    def _deps(self, reads, writes):
        deps = {}

        def add(s, v):
            if s.idx not in deps or deps[s.idx][1] < v:
                deps[s.idx] = (s, v)

        for b in reads:
            if b.lw is not None:
                add(*b.lw)
        for b in writes:
            if b.lw is not None:
                add(*b.lw)
            for s, v in b.rd.values():
                add(s, v)
        return deps

    def _wait(self, eng, deps):
        for s, v in deps.values():
            if eng.seen.get(s.idx, 0) < v:
                eng.h.wait_ge(s.h, v)
                eng.seen[s.idx] = v

    def _record(self, s, v, reads, writes):
        for b in writes:
            b.lw = (s, v)
            b.rd = {}
        for b in reads:
            if b.rd.get(s.idx, (None, 0))[1] < v:
                b.rd[s.idx] = (s, v)

    def op(self, eng, fn, reads=(), writes=()):
        self._wait(eng, self._deps(reads, writes))
        ins = fn()
        eng.sem.cnt += 1
        ins.then_inc(eng.sem.h, 1)
        self._record(eng.sem, eng.sem.cnt, reads, writes)

    def grp(self, eng, fns, reads=(), writes=()):
        self._wait(eng, self._deps(reads, writes))
        ins = None
        for fn in fns:
            ins = fn()
        eng.sem.cnt += 1
        ins.then_inc(eng.sem.h, 1)
        self._record(eng.sem, eng.sem.cnt, reads, writes)

    def dma(self, eng, out, in_, reads=(), writes=()):
        self._wait(eng, self._deps(reads, writes))
        ins = eng.h.dma_start(out=out, in_=in_)
        if writes:
            b = writes[0]
            if b.lsem is None:
                b.lsem = self._dsem()
            s = b.lsem
        else:
            b = reads[0]
            if b.ssem is None:
                b.ssem = self._dsem()
            s = b.ssem
        s.cnt += 16
        ins.then_inc(s.h, 16)
        self._record(s, s.cnt, reads, writes)

    def barrier(self):
        allsems = [e.sem for e in self.engs if e.sem is not None] + self.used
        for e in self.engs:
            for s in allsems:
                if s.cnt > 0 and e.seen.get(s.idx, 0) < s.cnt:
                    e.h.wait_ge(s.h, s.cnt)
                    e.seen[s.idx] = s.cnt
        for b in self.bufs:
            b.lw = None
            b.rd = {}
            b.lsem = None
            b.ssem = None
        self.bufs = []
        self.free.extend(self.used)
        self.used = []


def build_program(S, dbg=False, phases=99):
    nc = bass.Bass("TRN2", target_bir_lowering=False)
    NCH = S // 128
    skind = "ExternalOutput" if dbg else "Internal"

    def din(name, shape, dt=F32):
        return nc.dram_tensor(name, shape, dt, kind="ExternalInput").ap()

    def dscr(name, shape, dt):
        return nc.dram_tensor(name, shape, dt, kind=skind).ap()

    x = din("x", [S, D])
    pin = din("p", [S, PLE])
    w_in = din("w_in", [D, IN_COLS])
    w_ssd_up = din("w_ssd_up", [D, D])
    pool_w = din("pool_w", [2048, 512])
    w_pool_up = din("w_pool_up", [POOLD, D])
    w_out = din("w_out", [D, D])
    w_ff1 = din("w_ff1", [D, DFF])
    w_ff2 = din("w_ff2", [DFF, D])
    w_ple_gate = din("w_ple_gate", [D, D])
    w_ple_proj = din("w_ple_proj", [PLE, D])
    gains = din("gains", [128, 4, KC])
    g_final = din("g_final", [1, D])
    conv_wb = din("conv_wb", [128, 48, 6])
    dtb = din("dtb", [1, 128])
    alog = din("alog", [1, 128])
    dskip = din("dskip", [1, NH])
    pscale = din("pscale", [128, 16])
    out = nc.dram_tensor("out", [S, D], F32, kind="ExternalOutput").ap()

    z_s = dscr("z_s", [S, D], F32)
    xbcT_s = dscr("xbcT_s", [XBC, S], F32)
    dt_s = dscr("dt_s", [S, 128], F32)
    upT_s = dscr("upT_s", [POOLD, S], F32)
    gT_s = dscr("gT_s", [2 * D, S], BF16)
    xs_s = dscr("xs_s", [S, D], F32)
    btok_s = dscr("btok_s", [S, 1024], BF16)
    bT_s = dscr("bT_s", [NCH, 128, 8, 128], BF16)
    cT_s = dscr("cT_s", [NCH, 128, 8, 128], BF16)
    yd_s = [dscr("yf_s", [S, D], F32), dscr("yb_s", [S, D], F32)]
    yT_s = dscr("yT_s", [D, S], BF16)
    pbT_s = dscr("pbT_s", [POOLD, S], BF16)
    mixT_s = dscr("mixT_s", [D, S], BF16)
    x1_s = dscr("x1_s", [S, D], F32)
    uT_s = dscr("uT_s", [DFF, S], BF16)
    x2_s = dscr("x2_s", [S, D], F32)
    x3_s = dscr("x3_s", [S, D], F32)

    with ExitStack() as es:
        kb = KB(nc, es)
        pe, act, dve, pool, sp = kb.pe, kb.act, kb.dve, kb.pool, kb.sp
        V, A_, G, T_ = nc.vector, nc.scalar, nc.gpsimd, nc.tensor

        def sb(st, name, shape, dt):
            return st.enter_context(nc.sbuf_tensor(name, shape, dt))

        ps = [es.enter_context(nc.psum_tensor(f"ps{i}", [128, 512], F32)) for i in range(8)]
        psb = [Buf(f"ps{i}") for i in range(8)]
        pctr = [0]

        def nextbank():
            i = pctr[0] % 8
            pctr[0] += 1
            return ps[i], psb[i]

        ident_b = sb(es, "ident_b", [128, 128], BF16)
        ident_f = sb(es, "ident_f", [128, 128], F32)
        triU = sb(es, "triU", [128, 128], F32)
        triLs = sb(es, "triLs", [128, 128], F32)
        triL = sb(es, "triL", [128, 128], F32)
        triUs = sb(es, "triUs", [128, 128], F32)
        ones = sb(es, "ones", [128, 128], F32)
        gains_t = sb(es, "gains_t", [128, 4, KC], F32)
        cwb_t = sb(es, "cwb_t", [128, 48, 6], F32)
        pscale_t = sb(es, "pscale_t", [128, 16], F32)

        def mk(t, cmp, sgn=1):
            b = kb.buf("c")
            kb.op(pool, lambda: G.memset(t[:], 1.0), writes=[b])
            if cmp is not None:
                kb.op(pool, lambda: G.affine_select(out=t[:], in_=t[:], pattern=[[-sgn, 128]], compare_op=cmp,
                                                    fill=0.0, base=0, channel_multiplier=sgn), writes=[b])

        mk(ident_b, ALU.is_equal)
        mk(ident_f, ALU.is_equal)
        mk(triU, ALU.is_ge, -1)
        mk(triLs, ALU.is_gt, 1)
        mk(triL, ALU.is_ge, 1)
        mk(triUs, ALU.is_gt, -1)
        mk(ones, None)
        kb.dma(sp, gains_t[:], gains, writes=[kb.buf("c1")])
        kb.dma(sp, cwb_t[:], conv_wb, writes=[kb.buf("c2")])
        kb.dma(sp, pscale_t[:], pscale, writes=[kb.buf("c3")])
        kb.barrier()

        class WStream:
            def __init__(self, st, name, kc, ncols, nbuf, blocks):
                self.t = [sb(st, f"{name}{i}", [128, kc, ncols], BF16) for i in range(nbuf)]
                self.b = [kb.buf(f"{name}{i}") for i in range(nbuf)]
                self.blocks = blocks
                self.nbuf = nbuf
                self.issued = 0

            def prefetch(self, upto):
                while self.issued < min(upto, len(self.blocks)):
                    i = self.issued
                    src = self.blocks[i]
                    dst = self.t[i % self.nbuf]
                    kcn, ncn = src.shape[1], src.shape[2]
                    kb.dma(pool, dst[:, 0:kcn, 0:ncn], src, writes=[self.b[i % self.nbuf]])
                    self.issued += 1

            def get(self, i):
                self.prefetch(i + self.nbuf)
                return self.t[i % self.nbuf], self.b[i % self.nbuf]

        def wblk(w, r0, nk, c0, ncols):
            return w[r0:r0 + nk * 128, c0:c0 + ncols].rearrange("(kc p) n -> p kc n", p=128)

        class Ring:
            def __init__(self, st, name, shape, dt, n):
                self.t = [sb(st, f"{name}{i}", shape, dt) for i in range(n)]
                self.b = [kb.buf(f"{name}{i}") for i in range(n)]
                self.i = 0
                self.n = n

            def next(self):
                i = self.i % self.n
                self.i += 1
                return self.t[i], self.b[i]

        class NormT:
            def __init__(self, st, pref, nsub, nxl=2):
                self.xl = Ring(st, pref + "xl", [128, D], F32, nxl)
                self.xn = Ring(st, pref + "xn", [128, D], BF16, 1)
                self.ss = sb(st, pref + "ss", [128, 2 * nsub], F32)
                self.ssb = [kb.buf(f"ss{i}") for i in range(nsub)]
                self.nsub = nsub

            def run(self, src, row0, gidx, hT, hTb):
                nsub, ss = self.nsub, self.ss
                for ts in range(nsub):
                    r = row0 + ts * 128
                    xt, xtb = self.xl.next()
                    xo, xob = self.xn.next()
                    sb_ = self.ssb[ts]
                    kb.dma(sp, xt[:], src[r:r + 128, :], writes=[xtb])
                    kb.op(act, lambda xt=xt, xo=xo, ts=ts: A_.activation(out=xo[:], in_=xt[:], func=AF.Square,
                                                                         accum_out=ss[:, ts:ts + 1]),
                          reads=[xtb], writes=[xob, sb_])
                    kb.op(act, lambda ts=ts: A_.activation(out=ss[:, nsub + ts:nsub + ts + 1], in_=ss[:, ts:ts + 1],
                                                           func=AF.Sqrt, scale=1.0 / D, bias=EPS),
                          reads=[sb_], writes=[sb_])
                    kb.op(dve, lambda ts=ts: V.reciprocal(out=ss[:, nsub + ts:nsub + ts + 1],
                                                          in_=ss[:, nsub + ts:nsub + ts + 1]),
                          reads=[sb_], writes=[sb_])
                    kb.op(dve, lambda xt=xt, xo=xo, ts=ts: V.tensor_scalar(
                        out=xo[:], in0=xt[:], scalar1=ss[:, nsub + ts:nsub + ts + 1], scalar2=None,
                        op0=ALU.mult), reads=[xtb, sb_], writes=[xob])
                    for q in range(4):
                        pt, ptb = nextbank()
                        ptv = pt[:].bitcast(BF16)
                        fns = []
                        for k in range(8):
                            kc = q * 8 + k
                            fns.append(lambda k=k, kc=kc, ptv=ptv, xo=xo: T_.transpose(
                                out=ptv[:, k * 128:(k + 1) * 128], in_=xo[:, kc * 128:(kc + 1) * 128],
                                identity=ident_b[:]))
                        kb.grp(pe, fns, reads=[xob], writes=[ptb])
                        kb.op(dve, lambda q=q, ptv=ptv, ts=ts: V.tensor_tensor(
                            out=hT[:, q * 8:(q + 1) * 8, ts * 128:(ts + 1) * 128],
                            in0=ptv.rearrange("p (k t) -> p k t", k=8),
                            in1=gains_t[:, gidx, q * 8:(q + 1) * 8].unsqueeze(2).to_broadcast([128, 8, 128]),
                            op=ALU.mult), reads=[ptb], writes=[hTb])

        def mm_ws(wt, wtb, sub, klist, actT, actTb, tok0, ntok, extra_reads=()):
            pt, ptb = nextbank()
            fns = []
            nk = len(klist)
            for i, (wk, ak) in enumerate(klist):
                fns.append(lambda i=i, wk=wk, ak=ak, pt=pt: T_.matmul(
                    pt[:, 0:ntok], lhsT=wt[:, wk, sub * 128:(sub + 1) * 128], rhs=actT[:, ak, tok0:tok0 + ntok],
                    start=(i == 0), stop=(i == nk - 1)))
            kb.grp(pe, fns, reads=[wtb, actTb] + list(extra_reads), writes=[ptb])
            return pt, ptb

        def mm_as(wt, wtb, ncols, nk, actT, actTb, ts):
            pt, ptb = nextbank()
            fns = []
            for kc in range(nk):
                fns.append(lambda kc=kc, pt=pt: T_.matmul(
                    pt[:, 0:ncols], lhsT=actT[:, kc, ts * 128:(ts + 1) * 128], rhs=wt[:, kc, 0:ncols],
                    start=(kc == 0), stop=(kc == nk - 1)))
            kb.grp(pe, fns, reads=[wtb, actTb], writes=[ptb])
            return pt, ptb

        klist32 = [(k, k) for k in range(KC)]

        def phase1():
            T = min(1024, S)
            nsub = T // 128
            nth = T // 512
            with ExitStack() as st:
                hT = sb(st, "p1hT", [128, KC, T], BF16)
                hTb = kb.buf("hT")
                nt = NormT(st, "p1", nsub, 2)
                stg = Ring(st, "p1stg", [128, 512], F32, 6)
                stgb = Ring(st, "p1stgb", [128, 512], BF16, 4)
                sig = Ring(st, "p1sig", [128, 512], F32, 2)
                blocks, kinds = [], []
                for j in range(8):
                    blocks.append(wblk(w_in, 0, KC, C_Z + j * 512, 512)); kinds.append(("z", j))
                for j in range(12):
                    blocks.append(wblk(w_in, 0, KC, C_XBC + j * 512, 512)); kinds.append(("xbc", j))
                blocks.append(wblk(w_in, 0, KC, C_DT, 128)); kinds.append(("dt", 0))
                for j in range(4):
                    blocks.append(wblk(w_in, 0, KC, C_UP + j * 512, 512)); kinds.append(("up", j))
                for j in range(16):
                    blocks.append(wblk(w_in, 0, KC, C_GA + j * 512, 512)); kinds.append(("g", j))
                nblk = len(blocks)
                ntile = S // T
                ws = WStream(st, "p1w", KC, 512, 2, blocks * ntile)
                for tt in range(ntile):
                    t0 = tt * T
                    ws.prefetch(tt * nblk + 2)
                    nt.run(x, t0, 0, hT, hTb)
                    for bi in range(nblk):
                        wt, wtb = ws.get(tt * nblk + bi)
                        kind, j = kinds[bi]
                        if kind == "z":
                            for ts in range(nsub):
                                pt, ptb = mm_as(wt, wtb, 512, KC, hT, hTb, ts)
                                sg, sgb = sig.next()
                                kb.op(act, lambda pt=pt, sg=sg: A_.activation(out=sg[:], in_=pt[:], func=AF.Sigmoid),
                                      reads=[ptb], writes=[sgb])
                                so, sob = stg.next()
                                kb.op(dve, lambda pt=pt, sg=sg, so=so: V.tensor_tensor(out=so[:], in0=pt[:], in1=sg[:],
                                                                                       op=ALU.mult),
                                      reads=[ptb, sgb], writes=[sob])
                                r = t0 + ts * 128
                                kb.dma(sp, z_s[r:r + 128, j * 512:(j + 1) * 512], so[:], reads=[sob])
                        elif kind == "dt":
                            for ts in range(nsub):
                                pt, ptb = mm_as(wt, wtb, 128, KC, hT, hTb, ts)
                                so, sob = stg.next()
                                kb.op(dve, lambda pt=pt, so=so: V.tensor_copy(out=so[:, 0:128], in_=pt[:, 0:128]),
                                      reads=[ptb], writes=[sob])
                                r = t0 + ts * 128
                                kb.dma(sp, dt_s[r:r + 128, :], so[:, 0:128], reads=[sob])
                        else:
                            for sub in range(4):
                                for th in range(nth):
                                    pt, ptb = mm_ws(wt, wtb, sub, klist32, hT, hTb, th * 512, 512)
                                    tk = t0 + th * 512
                                    f0 = j * 512 + sub * 128
                                    if kind == "g":
                                        so, sob = stgb.next()
                                        kb.op(act, lambda pt=pt, so=so: A_.activation(out=so[:], in_=pt[:],
                                                                                      func=AF.Sigmoid),
                                              reads=[ptb], writes=[sob])
                                        kb.dma(sp, gT_s[f0:f0 + 128, tk:tk + 512], so[:], reads=[sob])
                                    else:
                                        so, sob = stg.next()
                                        kb.op(dve, lambda pt=pt, so=so: V.tensor_copy(out=so[:], in_=pt[:]),
                                              reads=[ptb], writes=[sob])
                                        dst = xbcT_s if kind == "xbc" else upT_s
                                        kb.dma(sp, dst[f0:f0 + 128, tk:tk + 512], so[:], reads=[sob])
                kb.barrier()

        def phase2():
            T2 = 512
            with ExitStack() as st:
                xin = Ring(st, "p2xin", [128, 12, T2 + 4], F32, 2)
                acc = Ring(st, "p2acc", [128, 12, T2], F32, 2)
                stg = Ring(st, "p2stg", [128, 512], F32, 6)
                stgb = Ring(st, "p2stgb", [128, 512], BF16, 4)
                stgT = Ring(st, "p2stgT", [128, 512], BF16, 4)
                ptmp = Ring(st, "p2ptmp", [128, 512], F32, 3)
                def p2load(tt, grp):
                    t0 = tt * T2
                    xi, xib = xin.next()
                    lo = max(t0 - 2, 0)
                    hi = min(t0 + T2 + 2, S)
                    if t0 == 0:
                        kb.op(pool, lambda xi=xi: G.memset(xi[:, :, 0:2], 0.0), writes=[xib])
                    if t0 + T2 == S:
                        kb.op(pool, lambda xi=xi: G.memset(xi[:, :, T2 + 2:T2 + 4], 0.0), writes=[xib])
                    src = xbcT_s[grp * 1536:(grp + 1) * 1536, lo:hi].rearrange("(c p) t -> p c t", p=128)
                    kb.dma(sp, xi[:, :, lo - (t0 - 2):hi - (t0 - 2)], src, writes=[xib])
                    return xi, xib

                units = [(tt, grp) for tt in range(S // T2) for grp in range(4)]
                loaded = {0: p2load(*units[0])}
                for ui, (tt, grp) in enumerate(units):
                    if True:
                        t0 = tt * T2
                        if ui + 1 < len(units):
                            loaded[ui + 1] = p2load(*units[ui + 1])
                        xi, xib = loaded.pop(ui)
                        ac, acb = acc.next()
                        for k in range(5):
                            for j in range(12):
                                cj = grp * 12 + j
                                useP = (j % 4 == 3)
                                e = pool if useP else dve
                                E_ = G if useP else V
                                if useP and k > 0:
                                    ptm, ptmb = ptmp.next()
                                    kb.op(pool, lambda j=j, cj=cj, k=k, xi=xi, ptm=ptm: G.tensor_scalar(
                                        out=ptm[:], in0=xi[:, j, k:k + T2], scalar1=cwb_t[:, cj, k:k + 1], scalar2=None,
                                        op0=ALU.mult), reads=[xib], writes=[ptmb])
                                    kb.op(pool, lambda j=j, ac=ac, ptm=ptm: G.tensor_tensor(
                                        out=ac[:, j, :], in0=ac[:, j, :], in1=ptm[:], op=ALU.add),
                                          reads=[ptmb], writes=[acb])
                                    continue
                                if k == 0:
                                    fn = lambda E_=E_, j=j, cj=cj, xi=xi, ac=ac: E_.tensor_scalar(
                                        out=ac[:, j, :], in0=xi[:, j, 0:T2], scalar1=cwb_t[:, cj, 0:1],
                                        scalar2=cwb_t[:, cj, 5:6], op0=ALU.mult, op1=ALU.add)
                                else:
                                    fn = lambda E_=E_, j=j, cj=cj, k=k, xi=xi, ac=ac: E_.scalar_tensor_tensor(
                                        out=ac[:, j, :], in0=xi[:, j, k:k + T2], scalar=cwb_t[:, cj, k:k + 1],
                                        in1=ac[:, j, :], op0=ALU.mult, op1=ALU.add)
                                kb.op(e, fn, reads=[xib], writes=[acb])
                        kb.op(act, lambda ac=ac: A_.activation(out=ac[:], in_=ac[:], func=AF.Silu),
                              reads=[], writes=[acb])
                        for q in range(3):
                            cj0 = grp * 12 + q * 4
                            if cj0 < 40:
                                for tc in range(T2 // 128):
                                    pt, ptb = nextbank()
                                    fns = []
                                    for k in range(4):
                                        fns.append(lambda k=k, pt=pt, ac=ac, q=q, tc=tc: T_.matmul(
                                            pt[:, k * 128:(k + 1) * 128], lhsT=ac[:, q * 4 + k, tc * 128:(tc + 1) * 128],
                                            rhs=ident_f[:], start=True, stop=True))
                                    kb.grp(pe, fns, reads=[acb], writes=[ptb])
                                    r = t0 + tc * 128
                                    if cj0 < 32:
                                        so, sob = stg.next()
                                        kb.op(act, lambda pt=pt, so=so: A_.copy(out=so[:], in_=pt[:]),
                                              reads=[ptb], writes=[sob])
                                        kb.dma(sp, xs_s[r:r + 128, cj0 * 128:cj0 * 128 + 512], so[:], reads=[sob])
                                    else:
                                        so, sob = stgb.next()
                                        kb.op(act, lambda pt=pt, so=so: A_.copy(out=so[:], in_=pt[:]),
                                              reads=[ptb], writes=[sob])
                                        c0 = (cj0 - 32) * 128
                                        kb.dma(sp, btok_s[r:r + 128, c0:c0 + 512], so[:], reads=[sob])
                            if cj0 >= 32:
                                for k in range(4):
                                    cj = cj0 + k
                                    so, sob = stgT.next()
                                    kb.op(dve, lambda so=so, ac=ac, q=q, k=k: V.tensor_copy(out=so[:],
                                                                                            in_=ac[:, q * 4 + k, :]),
                                          reads=[acb], writes=[sob])
                                    dstT = bT_s if cj < 40 else cT_s
                                    g = (cj - 32) % 8
                                    c0 = t0 // 128
                                    dst = dstT[c0:c0 + T2 // 128, :, g, :].rearrange("c p t -> p c t")
                                    kb.dma(sp, dst, so[:].rearrange("p (c t) -> p c t", t=128), reads=[sob])
                kb.barrier()

        def phase3():
            with ExitStack() as st:
                dtd = sb(st, "dtd", [128, NCH, 64], F32)
                a_d = sb(st, "a_d", [128, NCH, 64], F32)
                bias_t = sb(st, "bias_t", [128, 128], F32)
                A_t = sb(st, "A_t", [128, 128], F32)
                b_dt, b_a, b_bias, b_A = (kb.buf(n) for n in ("dt", "a", "bias", "A"))
                kb.dma(sp, bias_t[:], dtb.partition_broadcast(128).rearrange("p o h -> p (o h)"), writes=[b_bias])
                kb.dma(sp, A_t[:], alog.partition_broadcast(128).rearrange("p o h -> p (o h)"), writes=[b_A])
                kb.op(act, lambda: A_.activation(out=A_t[:], in_=A_t[:], func=AF.Exp), writes=[b_A])
                kb.op(dve, lambda: V.tensor_scalar(out=A_t[:], in0=A_t[:], scalar1=-1.0, scalar2=None, op0=ALU.mult),
                      writes=[b_A])

                Er = Ring(st, "p3E", [128, 192], F32, 3)
                ddr = Ring(st, "p3dd", [128, 64], F32, 3)
                aT = Ring(st, "p3aT", [128, 16, 128], F32, 2)
                xsr = Ring(st, "p3xs", [128, D], F32, 2)
                btr = Ring(st, "p3bt", [128, 1024], BF16, 2)
                bTr = Ring(st, "p3bT", [128, 8, 128], BF16, 2)
                cTr = Ring(st, "p3cT", [128, 8, 128], BF16, 2)
                cbm = Ring(st, "p3cbm", [128, 8, 128], BF16, 2)
                dec = Ring(st, "p3dec", [128, 512], F32, 6)
                MT = Ring(st, "p3MT", [128, 8, 128], BF16, 4)
                xdt = Ring(st, "p3xdt", [128, D], BF16, 2)
                xdte = Ring(st, "p3xdte", [128, D], BF16, 2)
                H = sb(st, "p3H", [128, 8, 512], F32)
                Hbf = sb(st, "p3Hbf", [128, 8, 512], BF16)
                Hb = [kb.buf(f"H{g}") for g in range(8)]
                Hbfb = [kb.buf(f"Hbf{g}") for g in range(8)]
                ystg = Ring(st, "p3y", [128, 512], F32, 4)
                tmp = Ring(st, "p3tmp", [128, 512], F32, 3)

                for d in (1, 0):
                    hs = slice(64 * d, 64 * d + 64)
                    tri_incl = triU if d == 0 else triL
                    tri_strict = triLs if d == 0 else triUs
                    ll_ = xsr.t[0][:, 0:NCH * 64].rearrange("p (c h) -> p c h", h=64)
                    b_l = xsr.b[0]
                    kb.dma(sp, dtd[:], dt_s[:, hs].rearrange("(c p) h -> p c h", p=128), writes=[b_dt])
                    bb = bias_t[:, hs].unsqueeze(1).to_broadcast([128, NCH, 64])
                    kb.op(dve, lambda bb=bb: V.tensor_tensor(out=a_d[:], in0=dtd[:], in1=bb, op=ALU.add),
                          reads=[b_dt, b_bias], writes=[b_a])
                    kb.op(act, lambda ll_=ll_: A_.activation(out=ll_, in_=a_d[:], func=AF.Abs),
                          reads=[b_a], writes=[b_l])
                    kb.op(act, lambda ll_=ll_: A_.activation(out=ll_, in_=ll_, func=AF.Exp, scale=-1.0), writes=[b_l])
                    kb.op(act, lambda ll_=ll_: A_.activation(out=ll_, in_=ll_, func=AF.Ln, bias=1.0), writes=[b_l])
                    kb.op(dve, lambda ll_=ll_: V.scalar_tensor_tensor(out=dtd[:], in0=a_d[:], scalar=0.0, in1=ll_,
                                                                      op0=ALU.max, op1=ALU.add),
                          reads=[b_a, b_l], writes=[b_dt])
                    ab = A_t[:, hs].unsqueeze(1).to_broadcast([128, NCH, 64])
                    kb.op(dve, lambda ab=ab: V.tensor_tensor(out=a_d[:], in0=dtd[:], in1=ab, op=ALU.mult),
                          reads=[b_dt, b_A], writes=[b_a])
                    for g in range(8):
                        kb.op(dve, lambda g=g: V.memset(H[:, g, :], 0.0), writes=[Hb[g]])
                        kb.op(dve, lambda g=g: V.memset(Hbf[:, g, :], 0.0), writes=[Hbfb[g]])
                    order = list(range(NCH)) if d == 0 else list(range(NCH - 1, -1, -1))

                    def prologue(c):
                        P = {"c": c}
                        r = c * 128
                        xs, xsb = xsr.next()
                        bt, btb = btr.next()
                        bTt, bTb = bTr.next()
                        cTt, cTb = cTr.next()
                        kb.dma(sp, xs[:], xs_s[r:r + 128, :], writes=[xsb])
                        kb.dma(sp, bt[:], btok_s[r:r + 128, :], writes=[btb])
                        kb.dma(sp, bTt[:], bT_s[c], writes=[bTb])
                        kb.dma(sp, cTt[:], cT_s[c], writes=[cTb])
                        E, Eb = Er.next()
                        dd, ddb = ddr.next()
                        pt, ptb = nextbank()
                        fns = []
                        for i, lt in enumerate((tri_incl, tri_strict, ones)):
                            fns.append(lambda i=i, lt=lt, pt=pt, c=c: T_.matmul(
                                pt[:, i * 64:(i + 1) * 64], lhsT=lt[:], rhs=a_d[:, c, :], start=True, stop=True))
                        kb.grp(pe, fns, reads=[b_a], writes=[ptb])
                        kb.op(act, lambda pt=pt, E=E: A_.activation(out=E[:], in_=pt[:, 0:192], func=AF.Exp),
                              reads=[ptb], writes=[Eb])
                        kb.op(dve, lambda dd=dd, E=E, c=c: V.tensor_tensor(out=dd[:], in0=dtd[:, c, :], in1=E[:, 64:128],
                                                                           op=ALU.mult),
                              reads=[b_dt, Eb], writes=[ddb])
                        xd, xdb = xdt.next()
                        xe, xeb = xdte.next()
                        kb.op(pool, lambda xs=xs, xd=xd, c=c: G.tensor_tensor(
                            out=xd[:].rearrange("p (h e) -> p h e", e=64), in0=xs[:].rearrange("p (h e) -> p h e", e=64),
                            in1=dtd[:, c, :].unsqueeze(2).to_broadcast([128, 64, 64]), op=ALU.mult),
                              reads=[xsb, b_dt], writes=[xdb])
                        kb.op(dve, lambda xs=xs, xe=xe, dd=dd: V.tensor_tensor(
                            out=xe[:].rearrange("p (h e) -> p h e", e=64), in0=xs[:].rearrange("p (h e) -> p h e", e=64),
                            in1=dd[:].unsqueeze(2).to_broadcast([128, 64, 64]), op=ALU.mult),
                              reads=[xsb, ddb], writes=[xeb])
                        cm, cmb = cbm.next()
                        for half in range(2):
                            pt, ptb = nextbank()
                            fns = []
                            for k in range(4):
                                g = half * 4 + k
                                fns.append(lambda k=k, g=g, pt=pt, bTt=bTt, cTt=cTt: T_.matmul(
                                    pt[:, k * 128:(k + 1) * 128], lhsT=bTt[:, g, :], rhs=cTt[:, g, :],
                                    start=True, stop=True))
                            kb.grp(pe, fns, reads=[bTb, cTb], writes=[ptb])
                            kb.op(dve, lambda pt=pt, cm=cm, half=half: V.tensor_tensor(
                                out=cm[:, half * 4:(half + 1) * 4, :], in0=pt[:].rearrange("p (g l) -> p g l", g=4),
                                in1=tri_incl[:].unsqueeze(1).to_broadcast([128, 4, 128]), op=ALU.mult),
                                  reads=[ptb], writes=[cmb])
                        P.update(xs=xs, xsb=xsb, bt=bt, btb=btb, cTt=cTt, cTb=cTb, E=E, Eb=Eb, xd=xd, xdb=xdb,
                                 xe=xe, xeb=xeb, cm=cm, cmb=cmb, at=None, atb=None, mt={}, mtb={})
                        return P

                    def stageA(P, g):
                        c = P["c"]
                        if g % 2 == 0:
                            at, atb = aT.next()
                            h0 = g * 8
                            kb.op(pool, lambda at=at, c=c, h0=h0: G.tensor_tensor(
                                out=at[:], in0=a_d[:, c, h0:h0 + 16].unsqueeze(2).to_broadcast([128, 16, 128]),
                                in1=tri_incl[:].unsqueeze(1).to_broadcast([128, 16, 128]), op=ALU.mult),
                                  reads=[b_a], writes=[atb])
                            P["at"], P["atb"] = at, atb
                        at, atb = P["at"], P["atb"]
                        cm, cmb = P["cm"], P["cmb"]
                        mt, mtb = MT.next()
                        for half in range(2):
                            pt, ptb = nextbank()
                            hh = (g % 2) * 8 + half * 4
                            kb.grp(pe, [lambda pt=pt, at=at, hh=hh: T_.matmul(
                                pt[:], lhsT=tri_strict[:], rhs=at[:, hh:hh + 4, :], start=True, stop=True)],
                                   reads=[atb], writes=[ptb])
                            dc, dcb = dec.next()
                            kb.op(act, lambda pt=pt, dc=dc: A_.activation(out=dc[:], in_=pt[:], func=AF.Exp),
                                  reads=[ptb], writes=[dcb])
                            kb.op(dve, lambda dc=dc, mt=mt, cm=cm, g=g, half=half: V.tensor_tensor(
                                out=mt[:, half * 4:(half + 1) * 4, :], in0=dc[:].rearrange("p (h l) -> p h l", h=4),
                                in1=cm[:, g, :].unsqueeze(1).to_broadcast([128, 4, 128]), op=ALU.mult),
                                  reads=[dcb, cmb], writes=[mtb])
                        P["mt"][g], P["mtb"][g] = mt, mtb

                    def stageB(P, g):
                        c = P["c"]
                        r = c * 128
                        mt, mtb = P["mt"][g], P["mtb"][g]
                        xd, xdb, xe, xeb = P["xd"], P["xdb"], P["xe"], P["xeb"]
                        cTt, cTb, bt, btb, E, Eb = P["cTt"], P["cTb"], P["bt"], P["btb"], P["E"], P["Eb"]
                        py, pyb = nextbank()
                        fns = []
                        for hl in range(8):
                            col = g * 512 + hl * 64
                            fns.append(lambda hl=hl, col=col, py=py, mt=mt, xd=xd: T_.matmul(
                                py[:, hl * 64:(hl + 1) * 64], lhsT=mt[:, hl, :], rhs=xd[:, col:col + 64],
                                start=True, stop=True))
                        kb.grp(pe, fns, reads=[mtb, xdb], writes=[pyb])
                        po, pob = nextbank()
                        kb.grp(pe, [lambda po=po, cTt=cTt, g=g: T_.matmul(po[:], lhsT=cTt[:, g, :], rhs=Hbf[:, g, :],
                                                                           start=True, stop=True)],
                               reads=[cTb, Hbfb[g]], writes=[pob])
                        pst, pstb = nextbank()
                        kb.grp(pe, [lambda pst=pst, bt=bt, xe=xe, g=g: T_.matmul(
                            pst[:], lhsT=bt[:, g * 128:(g + 1) * 128], rhs=xe[:, g * 512:(g + 1) * 512],
                            start=True, stop=True)], reads=[btb, xeb], writes=[pstb])
                        tm, tmb = tmp.next()
                        kb.op(dve, lambda po=po, tm=tm, E=E, g=g: V.tensor_tensor(
                            out=tm[:].rearrange("p (h e) -> p h e", e=64), in0=po[:].rearrange("p (h e) -> p h e", e=64),
                            in1=E[:, g * 8:(g + 1) * 8].unsqueeze(2).to_broadcast([128, 8, 64]), op=ALU.mult),
                              reads=[pob, Eb], writes=[tmb])
                        yo, yob = ystg.next()
                        kb.op(dve, lambda py=py, tm=tm, yo=yo: V.tensor_tensor(out=yo[:], in0=py[:], in1=tm[:],
                                                                               op=ALU.add),
                              reads=[pyb, tmb], writes=[yob])
                        kb.dma(sp, yd_s[d][r:r + 128, g * 512:(g + 1) * 512], yo[:], reads=[yob])
                        kb.op(pool, lambda E=E, g=g: G.tensor_tensor(
                            out=H[:, g, :].rearrange("p (h e) -> p h e", e=64),
                            in0=H[:, g, :].rearrange("p (h e) -> p h e", e=64),
                            in1=E[:, 128 + g * 8:128 + (g + 1) * 8].unsqueeze(2).to_broadcast([128, 8, 64]),
                            op=ALU.mult), reads=[Eb], writes=[Hb[g]])
                        kb.op(dve, lambda pst=pst, g=g: V.tensor_tensor(out=H[:, g, :], in0=H[:, g, :], in1=pst[:],
                                                                        op=ALU.add),
                              reads=[pstb], writes=[Hb[g]])
                        kb.op(act, lambda g=g: A_.copy(out=Hbf[:, g, :], in_=H[:, g, :]),
                              reads=[Hb[g]], writes=[Hbfb[g]])

                    steps = [(ci, g) for ci in range(len(order)) for g in range(8)]
                    Ps = {0: prologue(order[0])}
                    stageA(Ps[0], 0)
                    stageA(Ps[0], 1)
                    for k, (ci, g) in enumerate(steps):
                        if g == 2 and ci + 1 < len(order):
                            Ps[ci + 1] = prologue(order[ci + 1])
                        stageB(Ps[ci], g)
                        if k + 2 < len(steps):
                            ci2, g2 = steps[k + 2]
                            stageA(Ps[ci2], g2)
                        if g == 7:
                            del Ps[ci]
                kb.barrier()

        def phase3c():
            PW = 1024
            with ExitStack() as st:
                dsk = sb(st, "dsk", [128, NH], F32)
                b_dsk = kb.buf("dsk")
                kb.dma(sp, dsk[:], dskip.partition_broadcast(128).rearrange("p o h -> p (o h)"), writes=[b_dsk])
                yfr = Ring(st, "c_yf", [128, PW], F32, 2)
                ybr = Ring(st, "c_yb", [128, PW], F32, 2)
                xsr = Ring(st, "c_xs", [128, PW], F32, 2)
                zsr = Ring(st, "c_zs", [128, PW], F32, 2)
                junk = Ring(st, "c_junk", [128, 512], BF16, 2)
                ssr = Ring(st, "c_ss", [128, 4], F32, 4)
                ynr = Ring(st, "c_yn", [128, PW], BF16, 2)
                yTst = Ring(st, "c_yT", [128, KC, 512], BF16, 2)
                yts, ytb = None, None
                for c in range(NCH):
                    r = c * 128
                    if c % 4 == 0:
                        yts, ytb = yTst.next()
                    for pc in range(D // PW):
                        cs = slice(pc * PW, (pc + 1) * PW)
                        yf, yfb = yfr.next()
                        yb, ybb = ybr.next()
                        xs, xsb = xsr.next()
                        zs, zsb = zsr.next()
                        kb.dma(sp, yf[:], yd_s[0][r:r + 128, cs], writes=[yfb])
                        kb.dma(sp, yb[:], yd_s[1][r:r + 128, cs], writes=[ybb])
                        kb.dma(sp, xs[:], xs_s[r:r + 128, cs], writes=[xsb])
                        kb.dma(sp, zs[:], z_s[r:r + 128, cs], writes=[zsb])
                        nh = PW // 64
                        kb.op(pool, lambda yf=yf, yb=yb: G.tensor_tensor(out=yf[:], in0=yf[:], in1=yb[:], op=ALU.add),
                              reads=[ybb], writes=[yfb])
                        kb.op(pool, lambda xs=xs, pc=pc: G.tensor_tensor(
                            out=xs[:].rearrange("p (h e) -> p h e", e=64), in0=xs[:].rearrange("p (h e) -> p h e", e=64),
                            in1=dsk[:, pc * nh:(pc + 1) * nh].unsqueeze(2).to_broadcast([128, nh, 64]), op=ALU.mult),
                              reads=[b_dsk], writes=[xsb])
                        kb.op(dve, lambda yf=yf, xs=xs: V.tensor_tensor(out=yf[:], in0=yf[:], in1=xs[:], op=ALU.add),
                              reads=[xsb], writes=[yfb])
                        kb.op(dve, lambda yf=yf, zs=zs: V.tensor_tensor(out=yf[:], in0=yf[:], in1=zs[:], op=ALU.mult),
                              reads=[zsb], writes=[yfb])
                        ss, ssb = ssr.next()
                        for gq in range(PW // 512):
                            jk, jkb = junk.next()
                            kb.op(act, lambda yf=yf, jk=jk, ss=ss, gq=gq: A_.activation(
                                out=jk[:], in_=yf[:, gq * 512:(gq + 1) * 512], func=AF.Square,
                                accum_out=ss[:, gq:gq + 1]), reads=[yfb], writes=[jkb, ssb])
                        kb.op(act, lambda ss=ss: A_.activation(out=ss[:, 2:4], in_=ss[:, 0:2], func=AF.Sqrt,
                                                               scale=1.0 / 512, bias=EPS), writes=[ssb])
                        kb.op(dve, lambda ss=ss: V.reciprocal(out=ss[:, 2:4], in_=ss[:, 2:4]), writes=[ssb])
                        yn, ynb = ynr.next()
                        kb.op(dve, lambda yf=yf, yn=yn, ss=ss: V.tensor_tensor(
                            out=yn[:].rearrange("p (g e) -> p g e", e=512), in0=yf[:].rearrange("p (g e) -> p g e", e=512),
                            in1=ss[:, 2:4].unsqueeze(2).to_broadcast([128, 2, 512]),
                            op=ALU.mult), reads=[yfb, ssb], writes=[ynb])
                        pt, ptb = nextbank()
                        ptv = pt[:].bitcast(BF16)
                        fns = []
                        for k in range(8):
                            fns.append(lambda k=k, ptv=ptv, yn=yn: T_.transpose(
                                out=ptv[:, k * 128:(k + 1) * 128], in_=yn[:, k * 128:(k + 1) * 128], identity=ident_b[:]))
                        kb.grp(pe, fns, reads=[ynb], writes=[ptb])
                        kb.op(dve, lambda ptv=ptv, yts=yts, pc=pc, c=c: V.tensor_tensor(
                            out=yts[:, pc * 8:(pc + 1) * 8, (c % 4) * 128:(c % 4 + 1) * 128],
                            in0=ptv.rearrange("p (k t) -> p k t", k=8),
                            in1=gains_t[:, 3, pc * 8:(pc + 1) * 8].unsqueeze(2).to_broadcast([128, 8, 128]),
                            op=ALU.mult), reads=[ptb], writes=[ytb])
                    if c % 4 == 3:
                        t0 = (c - 3) * 128
                        kb.dma(sp, yT_s[:, t0:t0 + 512].rearrange("(kc p) t -> p kc t", p=128), yts[:], reads=[ytb])
                kb.barrier()

        def phase4():
            T4 = 512
            HL = 8
            W4 = T4 + 2 * HL
            with ExitStack() as st:
                pw = sb(st, "p4pw", [128, 16, 512], BF16)
                b_pw = kb.buf("pw")
                kb.dma(pool, pw[:], pool_w.rearrange("(kc p) n -> p kc n", p=128), writes=[b_pw])
                ur = Ring(st, "p4u", [128, 16, W4], F32, 2)
                w1 = sb(st, "p4w1", [128, 4, W4], F32)
                w2 = sb(st, "p4w2", [128, 4, W4], F32)
                b_w1, b_w2 = kb.buf("w1"), kb.buf("w2")
                pdr = Ring(st, "p4pd", [128, 16, T4], BF16, 2)
                stgb = Ring(st, "p4stg", [128, 512], BF16, 4)
                def p4load(tt):
                    t0 = tt * T4
                    u, ub = ur.next()
                    lo = max(t0 - HL, 0)
                    hi = min(t0 + T4 + HL, S)
                    if t0 == 0:
                        kb.op(pool, lambda u=u: G.memset(u[:, :, 0:HL], 0.0), writes=[ub])
                    if t0 + T4 == S:
                        kb.op(pool, lambda u=u: G.memset(u[:, :, T4 + HL:W4], 0.0), writes=[ub])
                    kb.dma(sp, u[:, :, lo - (t0 - HL):hi - (t0 - HL)],
                           upT_s[:, lo:hi].rearrange("(c p) t -> p c t", p=128), writes=[ub])
                    return u, ub

                uld = {0: p4load(0)}
                for tt in range(S // T4):
                    t0 = tt * T4
                    if tt + 1 < S // T4:
                        uld[tt + 1] = p4load(tt + 1)
                    u, ub = uld.pop(tt)
                    pd, pdb = pdr.next()
                    for gi, w in enumerate((2, 4, 8, 16)):
                        us = u[:, gi * 4:(gi + 1) * 4, :]
                        kb.op(dve, lambda us=us: V.tensor_tensor(out=w1[:, :, 1:W4], in0=us[:, :, 0:W4 - 1],
                                                                 in1=us[:, :, 1:W4], op=ALU.add),
                              reads=[ub], writes=[b_w1])
                        cur, curb, oth, othb = w1, b_w1, w2, b_w2
                        lo_v, hi_v = 1, W4
                        sh = 1
                        ww = 2
                        while ww < w:
                            nlo, nhi = lo_v + sh, hi_v - sh
                            kb.op(dve, lambda cur=cur, oth=oth, nlo=nlo, nhi=nhi, sh=sh: V.tensor_tensor(
                                out=oth[:, :, nlo:nhi], in0=cur[:, :, nlo - sh:nhi - sh], in1=cur[:, :, nlo + sh:nhi + sh],
                                op=ALU.add), reads=[curb], writes=[othb])
                            cur, curb, oth, othb = oth, othb, cur, curb
                            lo_v, hi_v = nlo, nhi
                            sh *= 2
                            ww *= 2
                        h = w // 2
                        if t0 == 0:
                            for t in range(h):
                                cnt = (t + (w - h)) - 0
                                kb.op(dve, lambda cur=cur, t=t, cnt=cnt, w=w: V.tensor_scalar(
                                    out=cur[:, :, HL + t:HL + t + 1], in0=cur[:, :, HL + t:HL + t + 1],
                                    scalar1=float(w) / cnt, scalar2=None, op0=ALU.mult), writes=[curb])
                        if t0 + T4 == S:
                            for t in range(S - (w - h) + 1, S):
                                cnt = S - (t - h)
                                jj = HL + (t - t0)
                                kb.op(dve, lambda cur=cur, jj=jj, cnt=cnt, w=w: V.tensor_scalar(
                                    out=cur[:, :, jj:jj + 1], in0=cur[:, :, jj:jj + 1],
                                    scalar1=float(w) / cnt, scalar2=None, op0=ALU.mult), writes=[curb])
                        kb.op(dve, lambda cur=cur, us=us, pd=pd, gi=gi, w=w: V.scalar_tensor_tensor(
                            out=pd[:, gi * 4:(gi + 1) * 4, :], in0=cur[:, :, HL:HL + T4], scalar=1.0 / w,
                            in1=us[:, :, HL:HL + T4], op0=ALU.mult, op1=ALU.subtract),
                              reads=[curb, ub], writes=[pdb])
                    for gi in range(4):
                        for sub in range(4):
                            klist = [(gi * 4 + k, gi * 4 + k) for k in range(4)]
                            pt, ptb = mm_ws(pw, b_pw, sub, klist, pd, pdb, 0, T4)
                            so, sob = stgb.next()
                            fi = gi * 4 + sub
                            kb.op(act, lambda pt=pt, so=so, fi=fi: A_.activation(out=so[:], in_=pt[:], func=AF.Identity,
                                                                                 scale=pscale_t[:, fi:fi + 1]),
                                  reads=[ptb], writes=[sob])
                            kb.dma(sp, pbT_s[fi * 128:(fi + 1) * 128, t0:t0 + T4], so[:], reads=[sob])
                kb.barrier()

        def phase5():
            T = min(1024, S)
            BW = 256
            NB = D // BW
            nsb = BW // 128
            nth = T // 512
            with ExitStack() as st:
                yT = Ring(st, "p5yT", [128, KC, T], BF16, 1)
                pbT = Ring(st, "p5pbT", [128, 16, T], BF16, 1)
                gar = Ring(st, "p5ga", [128, nsb, T], BF16, 2)
                gbr = Ring(st, "p5gb", [128, nsb, T], BF16, 2)
                m1r = Ring(st, "p5m1", [128, 512], F32, 2)
                m2r = Ring(st, "p5m2", [128, 512], F32, 2)
                stgb = Ring(st, "p5stg", [128, 512], BF16, 4)
                ntile = S // T
                bl_up = [wblk(w_ssd_up, 0, KC, j * BW, BW) for j in range(NB)]
                bl_pu = [wblk(w_pool_up, 0, 16, j * BW, BW) for j in range(NB)]
                wsu = WStream(st, "p5wu", KC, BW, 2, bl_up * ntile)
                wsp = WStream(st, "p5wp", 16, BW, 2, bl_pu * ntile)
                kl16 = [(k, k) for k in range(16)]
                for tt in range(ntile):
                    t0 = tt * T
                    wsu.prefetch(tt * NB + 2)
                    wsp.prefetch(tt * NB + 2)
                    y_, yb_ = yT.next()
                    p_, pb_ = pbT.next()
                    kb.dma(sp, y_[:], yT_s[:, t0:t0 + T].rearrange("(kc p) t -> p kc t", p=128), writes=[yb_])
                    kb.dma(sp, p_[:], pbT_s[:, t0:t0 + T].rearrange("(kc p) t -> p kc t", p=128), writes=[pb_])
                    for j in range(NB):
                        wu, wub = wsu.get(tt * NB + j)
                        wp, wpb = wsp.get(tt * NB + j)
                        ga, gab = gar.next()
                        gb, gbb = gbr.next()
                        kb.dma(sp, ga[:], gT_s[j * BW:(j + 1) * BW, t0:t0 + T].rearrange("(c p) t -> p c t", p=128),
                               writes=[gab])
                        kb.dma(sp, gb[:], gT_s[D + j * BW:D + (j + 1) * BW, t0:t0 + T].rearrange("(c p) t -> p c t", p=128),
                               writes=[gbb])
                        for sub in range(nsb):
                            for th in range(nth):
                                tk = th * 512
                                pa, pab = mm_ws(wu, wub, sub, klist32, y_, yb_, tk, 512)
                                pb2, pbb2 = mm_ws(wp, wpb, sub, kl16, p_, pb_, tk, 512)
                                m1, m1b = m1r.next()
                                m2, m2b = m2r.next()
                                kb.op(dve, lambda pa=pa, ga=ga, m1=m1, sub=sub, tk=tk: V.tensor_tensor(
                                    out=m1[:], in0=pa[:], in1=ga[:, sub, tk:tk + 512], op=ALU.mult),
                                      reads=[pab, gab], writes=[m1b])
                                kb.op(dve, lambda pb2=pb2, gb=gb, m2=m2, sub=sub, tk=tk: V.tensor_tensor(
                                    out=m2[:], in0=pb2[:], in1=gb[:, sub, tk:tk + 512], op=ALU.mult),
                                      reads=[pbb2, gbb], writes=[m2b])
                                so, sob = stgb.next()
                                kb.op(pool, lambda m1=m1, m2=m2, so=so: G.tensor_tensor(out=so[:], in0=m1[:], in1=m2[:],
                                                                                        op=ALU.add),
                                      reads=[m1b, m2b], writes=[sob])
                                f0 = j * BW + sub * 128
                                kb.dma(sp, mixT_s[f0:f0 + 128, t0 + tk:t0 + tk + 512], so[:], reads=[sob])
                kb.barrier()

        def phase6():
            T = min(1024, S)
            nsub = T // 128
            with ExitStack() as st:
                mT = Ring(st, "p6mT", [128, KC, T], BF16, 1)
                xr = Ring(st, "p6x", [128, 512], F32, 4)
                stg = Ring(st, "p6stg", [128, 512], F32, 4)
                ntile = S // T
                bl = [wblk(w_out, 0, KC, j * 512, 512) for j in range(8)]
                ws = WStream(st, "p6w", KC, 512, 2, bl * ntile)
                for tt in range(ntile):
                    t0 = tt * T
                    ws.prefetch(tt * 8 + 2)
                    m_, mb_ = mT.next()
                    kb.dma(sp, m_[:], mixT_s[:, t0:t0 + T].rearrange("(kc p) t -> p kc t", p=128), writes=[mb_])
                    for j in range(8):
                        wt, wtb = ws.get(tt * 8 + j)
                        for ts in range(nsub):
                            r = t0 + ts * 128
                            xb, xbb = xr.next()
                            kb.dma(sp, xb[:], x[r:r + 128, j * 512:(j + 1) * 512], writes=[xbb])
                            pt, ptb = mm_as(wt, wtb, 512, KC, m_, mb_, ts)
                            so, sob = stg.next()
                            kb.op(dve, lambda pt=pt, xb=xb, so=so: V.tensor_tensor(out=so[:], in0=pt[:], in1=xb[:],
                                                                                   op=ALU.add),
                                  reads=[ptb, xbb], writes=[sob])
                            kb.dma(sp, x1_s[r:r + 128, j * 512:(j + 1) * 512], so[:], reads=[sob])
                kb.barrier()

        def phase7():
            T = min(1024, S)
            nsub = T // 128
            nth = T // 512
            with ExitStack() as st:
                hT = sb(st, "p7hT", [128, KC, T], BF16)
                hTb = kb.buf("hT")
                nt = NormT(st, "p7", nsub, 2)
                stgb = Ring(st, "p7stg", [128, 512], BF16, 6)
                rlr = Ring(st, "p7rl", [128, 512], F32, 3)
                ntile = S // T
                bl = [wblk(w_ff1, 0, KC, j * 512, 512) for j in range(32)]
                ws = WStream(st, "p7w", KC, 512, 2, bl * ntile)
                for tt in range(ntile):
                    t0 = tt * T
                    ws.prefetch(tt * 32 + 2)
                    nt.run(x1_s, t0, 1, hT, hTb)
                    for j in range(32):
                        wt, wtb = ws.get(tt * 32 + j)
                        for sub in range(4):
                            for th in range(nth):
                                pt, ptb = mm_ws(wt, wtb, sub, klist32, hT, hTb, th * 512, 512)
                                so, sob = stgb.next()
                                rl, rlb = rlr.next()
                                kb.op(act, lambda pt=pt, rl=rl: A_.activation(out=rl[:], in_=pt[:], func=AF.Relu),
                                      reads=[ptb], writes=[rlb])
                                kb.op(dve, lambda rl=rl, so=so: V.tensor_tensor(out=so[:], in0=rl[:], in1=rl[:],
                                                                                op=ALU.mult),
                                      reads=[rlb], writes=[sob])
                                f0 = j * 512 + sub * 128
                                tk = t0 + th * 512
                                kb.dma(sp, uT_s[f0:f0 + 128, tk:tk + 512], so[:], reads=[sob])
                kb.barrier()

        def phase8():
            T = min(1024, S)
            nsub = T // 128
            KG = 16
            NKG = DFF // (KG * 128)
            CH = 2048
            with ExitStack() as st:
                acc = sb(st, "p8acc", [128, nsub, CH], F32)
                accb = [kb.buf(f"acc{ts}") for ts in range(nsub)]
                ur = Ring(st, "p8u", [128, KG, T], BF16, 2)
                x1r = Ring(st, "p8x1", [128, 512], F32, 4)
                ntile = S // T
                bl = []
                for tt in range(ntile):
                    for ch in range(D // CH):
                        for kg in range(NKG):
                            for j in range(CH // 512):
                                bl.append(wblk(w_ff2, kg * KG * 128, KG, ch * CH + j * 512, 512))
                ws = WStream(st, "p8w", KG, 512, 3, bl)
                bi = 0
                for tt in range(ntile):
                    t0 = tt * T
                    for ch in range(D // CH):
                        c0 = ch * CH
                        for kg in range(NKG):
                            u_, ub_ = ur.next()
                            k0 = kg * KG * 128
                            kb.dma(sp, u_[:], uT_s[k0:k0 + KG * 128, t0:t0 + T].rearrange("(kc p) t -> p kc t", p=128),
                                   writes=[ub_])
                            for j in range(CH // 512):
                                wt, wtb = ws.get(bi)
                                bi += 1
                                for ts in range(nsub):
                                    if kg == 0:
                                        r = t0 + ts * 128
                                        xb, xbb = x1r.next()
                                        kb.dma(sp, xb[:], x1_s[r:r + 128, c0 + j * 512:c0 + (j + 1) * 512], writes=[xbb])
                                    pt, ptb = mm_as(wt, wtb, 512, KG, u_, ub_, ts)
                                    if kg == 0:
                                        kb.op(dve, lambda pt=pt, ts=ts, j=j, xb=xb: V.tensor_tensor(
                                            out=acc[:, ts, j * 512:(j + 1) * 512], in0=pt[:], in1=xb[:], op=ALU.add),
                                              reads=[ptb, xbb], writes=[accb[ts]])
                                    else:
                                        kb.op(dve, lambda pt=pt, ts=ts, j=j: V.tensor_tensor(
                                            out=acc[:, ts, j * 512:(j + 1) * 512], in0=acc[:, ts, j * 512:(j + 1) * 512],
                                            in1=pt[:], op=ALU.add), reads=[ptb], writes=[accb[ts]])
                        for ts in range(nsub):
                            r = t0 + ts * 128
                            kb.dma(sp, x2_s[r:r + 128, c0:c0 + CH], acc[:, ts, :], reads=[accb[ts]])
                kb.barrier()

        def phase9():
            T = 512
            nsub = T // 128
            with ExitStack() as st:
                hT = sb(st, "p9hT", [128, KC, T], BF16)
                hTb = kb.buf("hT")
                nt = NormT(st, "p9", nsub, 2)
                pT = sb(st, "p9pT", [128, 2, T], BF16)
                pTb = kb.buf("pT")
                wp = sb(st, "p9wp", [128, 2, D], BF16)
                wpb = kb.buf("wp")
                kb.dma(pool, wp[:], w_ple_proj.rearrange("(kc p) n -> p kc n", p=128), writes=[wpb])
                plr = Ring(st, "p9pl", [128, PLE], F32, 2)
                pbr = Ring(st, "p9pb", [128, PLE], BF16, 2)
                gr = Ring(st, "p9g", [128, 512], F32, 2)
                tr = Ring(st, "p9t", [128, 512], F32, 2)
                xr = Ring(st, "p9x", [128, 512], F32, 4)
                stg = Ring(st, "p9stg", [128, 512], F32, 4)
                ntile = S // T
                bl = [wblk(w_ple_gate, 0, KC, j * 512, 512) for j in range(8)]
                ws = WStream(st, "p9w", KC, 512, 2, bl * ntile)
                for tt in range(ntile):
                    t0 = tt * T
                    ws.prefetch(tt * 8 + 2)
                    nt.run(x2_s, t0, 2, hT, hTb)
                    for ts in range(nsub):
                        r = t0 + ts * 128
                        pl, plb = plr.next()
                        pb_, pbb = pbr.next()
                        kb.dma(sp, pl[:], pin[r:r + 128, :], writes=[plb])
                        kb.op(act, lambda pl=pl, pb_=pb_: A_.copy(out=pb_[:], in_=pl[:]), reads=[plb], writes=[pbb])
                        pt, ptb = nextbank()
                        ptv = pt[:].bitcast(BF16)
                        fns = [lambda k=k, ptv=ptv, pb_=pb_: T_.transpose(out=ptv[:, k * 128:(k + 1) * 128],
                                                                         in_=pb_[:, k * 128:(k + 1) * 128],
                                                                         identity=ident_b[:]) for k in range(2)]
                        kb.grp(pe, fns, reads=[pbb], writes=[ptb])
                        kb.op(dve, lambda ptv=ptv, ts=ts: V.tensor_copy(
                            out=pT[:, :, ts * 128:(ts + 1) * 128], in_=ptv[:, 0:256].rearrange("p (k t) -> p k t", k=2)),
                              reads=[ptb], writes=[pTb])
                    for j in range(8):
                        wt, wtb = ws.get(tt * 8 + j)
                        for ts in range(nsub):
                            r = t0 + ts * 128
                            xb, xbb = xr.next()
                            kb.dma(sp, xb[:], x2_s[r:r + 128, j * 512:(j + 1) * 512], writes=[xbb])
                            pg, pgb = mm_as(wt, wtb, 512, KC, hT, hTb, ts)
                            pp, ppb = nextbank()
                            fns = [lambda k=k, pp=pp, ts=ts, j=j: T_.matmul(
                                pp[:], lhsT=pT[:, k, ts * 128:(ts + 1) * 128], rhs=wp[:, k, j * 512:(j + 1) * 512],
                                start=(k == 0), stop=(k == 1)) for k in range(2)]
                            kb.grp(pe, fns, reads=[pTb, wpb], writes=[ppb])
                            g_, gb_ = gr.next()
                            kb.op(act, lambda pg=pg, g_=g_: A_.activation(out=g_[:], in_=pg[:], func=AF.Sigmoid),
                                  reads=[pgb], writes=[gb_])
                            t_, tb_ = tr.next()
                            kb.op(dve, lambda pp=pp, g_=g_, t_=t_: V.tensor_tensor(out=t_[:], in0=pp[:], in1=g_[:],
                                                                                   op=ALU.mult),
                                  reads=[ppb, gb_], writes=[tb_])
                            so, sob = stg.next()
                            kb.op(dve, lambda t_=t_, xb=xb, so=so: V.tensor_tensor(out=so[:], in0=t_[:], in1=xb[:],
                                                                                   op=ALU.add),
                                  reads=[tb_, xbb], writes=[sob])
                            kb.dma(sp, x3_s[r:r + 128, j * 512:(j + 1) * 512], so[:], reads=[sob])
                kb.barrier()

        def phase10():
            with ExitStack() as st:
                gf = sb(st, "gf", [128, D], F32)
                gfb = kb.buf("gf")
                kb.dma(sp, gf[:], g_final.partition_broadcast(128).rearrange("p o h -> p (o h)"), writes=[gfb])
                xl = Ring(st, "p10x", [128, D], F32, 4)
                jk = Ring(st, "p10j", [128, D], BF16, 1)
                ssr = Ring(st, "p10s", [128, 2], F32, 4)
                def p10load(c):
                    xt, xtb = xl.next()
                    kb.dma(sp, xt[:], x3_s[c * 128:(c + 1) * 128, :], writes=[xtb])
                    return xt, xtb

                lds = {c: p10load(c) for c in range(min(2, NCH))}
                for c in range(NCH):
                    r = c * 128
                    if c + 2 < NCH:
                        lds[c + 2] = p10load(c + 2)
                    xt, xtb = lds.pop(c)
                    j_, jb_ = jk.next()
                    ss, ssb = ssr.next()
                    kb.op(act, lambda xt=xt, j_=j_, ss=ss: A_.activation(out=j_[:], in_=xt[:], func=AF.Square,
                                                                         accum_out=ss[:, 0:1]),
                          reads=[xtb], writes=[jb_, ssb])
                    kb.op(act, lambda ss=ss: A_.activation(out=ss[:, 1:2], in_=ss[:, 0:1], func=AF.Sqrt,
                                                           scale=1.0 / D, bias=EPS), writes=[ssb])
                    kb.op(dve, lambda ss=ss: V.reciprocal(out=ss[:, 1:2], in_=ss[:, 1:2]), writes=[ssb])
                    kb.op(dve, lambda xt=xt, ss=ss: V.scalar_tensor_tensor(out=xt[:], in0=xt[:], scalar=ss[:, 1:2],
                                                                           in1=gf[:], op0=ALU.mult, op1=ALU.mult),
                          reads=[ssb, gfb], writes=[xtb])
                    kb.dma(sp, out[r:r + 128, :], xt[:], reads=[xtb])
                kb.barrier()

        plist = [phase1, phase2, phase3, phase3c, phase4, phase5, phase6, phase7, phase8, phase9, phase10]
        for i, ph in enumerate(plist):
            if i < phases:
                ph()
    return nc


def _host_params(inp):
    f = lambda a: np.ascontiguousarray(np.asarray(a, dtype=np.float32))
    fm = lambda v: f(np.asarray(v).reshape(KC, 128).T)
    gains = np.stack([fm(inp["norm_mix_g"][0]), fm(inp["norm_mlp_g"][0]), fm(inp["norm_ple_g"][0]),
                      fm(inp["ssd_norm_g"][0])], axis=1)
    cw = np.asarray(inp["conv_w"][0])
    cb = np.asarray(inp["conv_b"][0])
    cwb = np.concatenate([cw, cb[None, :]], axis=0)
    cwb = cwb.reshape(6, 48, 128).transpose(2, 1, 0)
    return {
        "w_in": f(inp["w_in"][0]),
        "w_ssd_up": f(inp["w_ssd_up"][0]),
        "pool_w": f(np.asarray(inp["pool_w"][0]).reshape(2048, 512)),
        "w_pool_up": f(inp["w_pool_up"][0]),
        "w_out": f(inp["w_out"][0]),
        "w_ff1": f(inp["w_ff1"][0]),
        "w_ff2": f(inp["w_ff2"][0]),
        "w_ple_gate": f(inp["w_ple_gate"][0]),
        "w_ple_proj": f(inp["w_ple_proj"][0]),
        "gains": f(gains),
        "g_final": f(np.asarray(inp["norm_final_g"]).reshape(1, D)),
        "conv_wb": f(cwb),
        "dtb": f(np.concatenate([np.asarray(inp["dt_bias_f"][0]), np.asarray(inp["dt_bias_b"][0])]).reshape(1, 128)),
        "alog": f(np.concatenate([np.asarray(inp["a_log_f"][0]), np.asarray(inp["a_log_b"][0])]).reshape(1, 128)),
        "dskip": f(np.asarray(inp["d_skip"][0]).reshape(1, NH)),
        "pscale": f(np.asarray(inp["pool_scale"][0]).reshape(16, 128).T),
    }


def kernel(**inp):
    xp = np.asarray(inp["x_prompt"])
    xs = np.asarray(inp["x_sample"])
    pp = np.asarray(inp["p_prompt"])[0]
    psm = np.asarray(inp["p_sample"])[0]
    seqs = [(xp[i], pp[i]) for i in range(xp.shape[0])] + [(xs[i], psm[i]) for i in range(xs.shape[0])]
    S = xp.shape[1]
    params = _host_params(inp)
    nc = build_program(S)
    in_maps = []
    for c in range(8):
        xx, pq = seqs[c] if c < len(seqs) else seqs[c - len(seqs)]
        m = dict(params)
        m["x"] = np.ascontiguousarray(xx, dtype=np.float32)
        m["p"] = np.ascontiguousarray(pq, dtype=np.float32)
        in_maps.append(m)
    res = run_bass_kernel_spmd(nc, in_maps, core_ids=list(range(8)))
    outs = [np.asarray(res.results[c]["out"], dtype=np.float32) for c in range(len(seqs))]
    nb = xp.shape[0]
    y_prompt = np.stack(outs[:nb], axis=0)
    y_sample = np.stack(outs[nb:], axis=0)
    return (y_prompt, y_sample)
```

```python
import numpy as np
from contextlib import ExitStack
import concourse.bass as bass
import concourse.mybir as mybir
from concourse.bass_utils import run_bass_kernel_spmd

F32 = mybir.dt.float32
BF16 = mybir.dt.bfloat16
AF = mybir.ActivationFunctionType
ALU = mybir.AluOpType

D = 4096
NH = 64
HD = 64
NG = 8
XBC = 6144
POOLD = 2048
DFF = 16384
PLE = 256
IN_COLS = 20608
C_Z, C_XBC, C_DT, C_UP, C_GA, C_GB = 0, 4096, 10240, 10368, 12416, 16512
EPS = 1e-6
KC = D // 128
SEQ = 4096


class Buf:
    __slots__ = ("name", "lw", "rd", "lsem", "ssem")

    def __init__(self, name):
        self.name = name
        self.lw = None
        self.rd = {}
        self.lsem = None
        self.ssem = None


class Sem:
    __slots__ = ("h", "cnt", "idx")

    def __init__(self, h, idx):
        self.h = h
        self.cnt = 0
        self.idx = idx


class Eng:
    def __init__(self, h, sem):
        self.h = h
        self.sem = sem
        self.seen = {}


class KB:
    def __init__(self, nc, es, nsem=100):
        self.nc = nc
        self.sems = [Sem(es.enter_context(nc.semaphore(f"s{i}")), i) for i in range(nsem)]
        self.pe = Eng(nc.tensor, self.sems[0])
        self.act = Eng(nc.scalar, self.sems[1])
        self.dve = Eng(nc.vector, self.sems[2])
        self.pool = Eng(nc.gpsimd, self.sems[3])
        self.sp = Eng(nc.sync, None)
        self.engs = [self.pe, self.act, self.dve, self.pool, self.sp]
        self.free = list(self.sems[4:])
        self.used = []
        self.bufs = []

    def buf(self, name):
        b = Buf(name)
        self.bufs.append(b)
        return b

    def _dsem(self):
        s = self.free.pop()
        self.used.append(s)
        return s

    def _deps(self, reads, writes):
        deps = {}

        def add(s, v):
            if s.idx not in deps or deps[s.idx][1] < v:
                deps[s.idx] = (s, v)

        for b in reads:
            if b.lw is not None:
                add(*b.lw)
        for b in writes:
            if b.lw is not None:
                add(*b.lw)
            for s, v in b.rd.values():
                add(s, v)
        return deps

    def _wait(self, eng, deps):
        for s, v in deps.values():
            if eng.seen.get(s.idx, 0) < v:
                eng.h.wait_ge(s.h, v)
                eng.seen[s.idx] = v

    def _record(self, s, v, reads, writes):
        for b in writes:
            b.lw = (s, v)
            b.rd = {}
        for b in reads:
            if b.rd.get(s.idx, (None, 0))[1] < v:
                b.rd[s.idx] = (s, v)

    def op(self, eng, fn, reads=(), writes=()):
        self._wait(eng, self._deps(reads, writes))
        ins = fn()
        eng.sem.cnt += 1
        ins.then_inc(eng.sem.h, 1)
        self._record(eng.sem, eng.sem.cnt, reads, writes)

    def grp(self, eng, fns, reads=(), writes=()):
        self._wait(eng, self._deps(reads, writes))
        ins = None
        for fn in fns:
            ins = fn()
        eng.sem.cnt += 1
        ins.then_inc(eng.sem.h, 1)
        self._record(eng.sem, eng.sem.cnt, reads, writes)

    def dma(self, eng, out, in_, reads=(), writes=()):
        self._wait(eng, self._deps(reads, writes))
        ins = eng.h.dma_start(out=out, in_=in_)
        if writes:
            b = writes[0]
            if b.lsem is None:
                b.lsem = self._dsem()
            s = b.lsem
        else:
            b = reads[0]
            if b.ssem is None:
                b.ssem = self._dsem()
            s = b.ssem
        s.cnt += 16
        ins.then_inc(s.h, 16)
        self._record(s, s.cnt, reads, writes)

    def barrier(self):
        allsems = [e.sem for e in self.engs if e.sem is not None] + self.used
        for e in self.engs:
            for s in allsems:
                if s.cnt > 0 and e.seen.get(s.idx, 0) < s.cnt:
                    e.h.wait_ge(s.h, s.cnt)
                    e.seen[s.idx] = s.cnt
        for b in self.bufs:
            b.lw = None
            b.rd = {}
            b.lsem = None
            b.ssem = None
        self.bufs = []
        self.free.extend(self.used)
        self.used = []


def build_program(S, dbg=False, phases=99):
    nc = bass.Bass("TRN2", target_bir_lowering=False)
    NCH = S // 128
    skind = "ExternalOutput" if dbg else "Internal"

    def din(name, shape, dt=F32):
        return nc.dram_tensor(name, shape, dt, kind="ExternalInput").ap()

    def dscr(name, shape, dt):
        return nc.dram_tensor(name, shape, dt, kind=skind).ap()

    x = din("x", [S, D])
    pin = din("p", [S, PLE])
    w_in = din("w_in", [D, IN_COLS])
    w_ssd_up = din("w_ssd_up", [D, D])
    pool_w = din("pool_w", [2048, 512])
    w_pool_up = din("w_pool_up", [POOLD, D])
    w_out = din("w_out", [D, D])
    w_ff1 = din("w_ff1", [D, DFF])
    w_ff2 = din("w_ff2", [DFF, D])
    w_ple_gate = din("w_ple_gate", [D, D])
    w_ple_proj = din("w_ple_proj", [PLE, D])
    gains = din("gains", [128, 4, KC])
    g_final = din("g_final", [1, D])
    conv_wb = din("conv_wb", [128, 48, 6])
    dtb = din("dtb", [1, 128])
    alog = din("alog", [1, 128])
    dskip = din("dskip", [1, NH])
    pscale = din("pscale", [128, 16])
    out = nc.dram_tensor("out", [S, D], F32, kind="ExternalOutput").ap()

    z_s = dscr("z_s", [S, D], F32)
    xbcT_s = dscr("xbcT_s", [XBC, S], F32)
    dt_s = dscr("dt_s", [S, 128], F32)
    upT_s = dscr("upT_s", [POOLD, S], F32)
    gT_s = dscr("gT_s", [2 * D, S], BF16)
    xs_s = dscr("xs_s", [S, D], F32)
    btok_s = dscr("btok_s", [S, 1024], BF16)
    bT_s = dscr("bT_s", [NCH, 128, 8, 128], BF16)
    cT_s = dscr("cT_s", [NCH, 128, 8, 128], BF16)
    yd_s = [dscr("yf_s", [S, D], F32), dscr("yb_s", [S, D], F32)]
    yT_s = dscr("yT_s", [D, S], BF16)
    pbT_s = dscr("pbT_s", [POOLD, S], BF16)
    mixT_s = dscr("mixT_s", [D, S], BF16)
    x1_s = dscr("x1_s", [S, D], F32)
    uT_s = dscr("uT_s", [DFF, S], BF16)
    x2_s = dscr("x2_s", [S, D], F32)
    x3_s = dscr("x3_s", [S, D], F32)

    with ExitStack() as es:
        kb = KB(nc, es)
        pe, act, dve, pool, sp = kb.pe, kb.act, kb.dve, kb.pool, kb.sp
        V, A_, G, T_ = nc.vector, nc.scalar, nc.gpsimd, nc.tensor

        def sb(st, name, shape, dt):
            return st.enter_context(nc.sbuf_tensor(name, shape, dt))

        ps = [es.enter_context(nc.psum_tensor(f"ps{i}", [128, 512], F32)) for i in range(8)]
        psb = [Buf(f"ps{i}") for i in range(8)]
        pctr = [0]

        def nextbank():
            i = pctr[0] % 8
            pctr[0] += 1
            return ps[i], psb[i]

        ident_b = sb(es, "ident_b", [128, 128], BF16)
        ident_f = sb(es, "ident_f", [128, 128], F32)
        triU = sb(es, "triU", [128, 128], F32)
        triLs = sb(es, "triLs", [128, 128], F32)
        triL = sb(es, "triL", [128, 128], F32)
        triUs = sb(es, "triUs", [128, 128], F32)
        ones = sb(es, "ones", [128, 128], F32)
        gains_t = sb(es, "gains_t", [128, 4, KC], F32)
        cwb_t = sb(es, "cwb_t", [128, 48, 6], F32)
        pscale_t = sb(es, "pscale_t", [128, 16], F32)

        def mk(t, cmp, sgn=1):
            b = kb.buf("c")
            kb.op(pool, lambda: G.memset(t[:], 1.0), writes=[b])
            if cmp is not None:
                kb.op(pool, lambda: G.affine_select(out=t[:], in_=t[:], pattern=[[-sgn, 128]], compare_op=cmp,
                                                    fill=0.0, base=0, channel_multiplier=sgn), writes=[b])

        mk(ident_b, ALU.is_equal)
        mk(ident_f, ALU.is_equal)
        mk(triU, ALU.is_ge, -1)
        mk(triLs, ALU.is_gt, 1)
        mk(triL, ALU.is_ge, 1)
        mk(triUs, ALU.is_gt, -1)
        mk(ones, None)
        kb.dma(sp, gains_t[:], gains, writes=[kb.buf("c1")])
        kb.dma(sp, cwb_t[:], conv_wb, writes=[kb.buf("c2")])
        kb.dma(sp, pscale_t[:], pscale, writes=[kb.buf("c3")])
        kb.barrier()

        class WStream:
            def __init__(self, st, name, kc, ncols, nbuf, blocks):
                self.t = [sb(st, f"{name}{i}", [128, kc, ncols], BF16) for i in range(nbuf)]
                self.b = [kb.buf(f"{name}{i}") for i in range(nbuf)]
                self.blocks = blocks
                self.nbuf = nbuf
                self.issued = 0

            def prefetch(self, upto):
                while self.issued < min(upto, len(self.blocks)):
                    i = self.issued
                    src = self.blocks[i]
                    dst = self.t[i % self.nbuf]
                    kcn, ncn = src.shape[1], src.shape[2]
                    kb.dma(pool, dst[:, 0:kcn, 0:ncn], src, writes=[self.b[i % self.nbuf]])
                    self.issued += 1

            def get(self, i):
                self.prefetch(i + self.nbuf)
                return self.t[i % self.nbuf], self.b[i % self.nbuf]

        def wblk(w, r0, nk, c0, ncols):
            return w[r0:r0 + nk * 128, c0:c0 + ncols].rearrange("(kc p) n -> p kc n", p=128)

        class Ring:
            def __init__(self, st, name, shape, dt, n):
                self.t = [sb(st, f"{name}{i}", shape, dt) for i in range(n)]
                self.b = [kb.buf(f"{name}{i}") for i in range(n)]
                self.i = 0
                self.n = n

            def next(self):
                i = self.i % self.n
                self.i += 1
                return self.t[i], self.b[i]

        class NormT:
            def __init__(self, st, pref, nsub, nxl=2):
                self.xl = Ring(st, pref + "xl", [128, D], F32, nxl)
                self.xn = Ring(st, pref + "xn", [128, D], BF16, 1)
                self.ss = sb(st, pref + "ss", [128, 2 * nsub], F32)
                self.ssb = [kb.buf(f"ss{i}") for i in range(nsub)]
                self.nsub = nsub

            def run(self, src, row0, gidx, hT, hTb):
                nsub, ss = self.nsub, self.ss
                for ts in range(nsub):
                    r = row0 + ts * 128
                    xt, xtb = self.xl.next()
                    xo, xob = self.xn.next()
                    sb_ = self.ssb[ts]
                    kb.dma(sp, xt[:], src[r:r + 128, :], writes=[xtb])
                    kb.op(act, lambda xt=xt, xo=xo, ts=ts: A_.activation(out=xo[:], in_=xt[:], func=AF.Square,
                                                                         accum_out=ss[:, ts:ts + 1]),
                          reads=[xtb], writes=[xob, sb_])
                    kb.op(act, lambda ts=ts: A_.activation(out=ss[:, nsub + ts:nsub + ts + 1], in_=ss[:, ts:ts + 1],
                                                           func=AF.Sqrt, scale=1.0 / D, bias=EPS),
                          reads=[sb_], writes=[sb_])
                    kb.op(dve, lambda ts=ts: V.reciprocal(out=ss[:, nsub + ts:nsub + ts + 1],
                                                          in_=ss[:, nsub + ts:nsub + ts + 1]),
                          reads=[sb_], writes=[sb_])
                    kb.op(dve, lambda xt=xt, xo=xo, ts=ts: V.tensor_scalar(
                        out=xo[:], in0=xt[:], scalar1=ss[:, nsub + ts:nsub + ts + 1], scalar2=None,
                        op0=ALU.mult), reads=[xtb, sb_], writes=[xob])
                    for q in range(4):
                        pt, ptb = nextbank()
                        ptv = pt[:].bitcast(BF16)
                        fns = []
                        for k in range(8):
                            kc = q * 8 + k
                            fns.append(lambda k=k, kc=kc, ptv=ptv, xo=xo: T_.transpose(
                                out=ptv[:, k * 128:(k + 1) * 128], in_=xo[:, kc * 128:(kc + 1) * 128],
                                identity=ident_b[:]))
                        kb.grp(pe, fns, reads=[xob], writes=[ptb])
                        kb.op(dve, lambda q=q, ptv=ptv, ts=ts: V.tensor_tensor(
                            out=hT[:, q * 8:(q + 1) * 8, ts * 128:(ts + 1) * 128],
                            in0=ptv.rearrange("p (k t) -> p k t", k=8),
                            in1=gains_t[:, gidx, q * 8:(q + 1) * 8].unsqueeze(2).to_broadcast([128, 8, 128]),
                            op=ALU.mult), reads=[ptb], writes=[hTb])

        def mm_ws(wt, wtb, sub, klist, actT, actTb, tok0, ntok, extra_reads=()):
            pt, ptb = nextbank()
            fns = []
            nk = len(klist)
            for i, (wk, ak) in enumerate(klist):
                fns.append(lambda i=i, wk=wk, ak=ak, pt=pt: T_.matmul(
                    pt[:, 0:ntok], lhsT=wt[:, wk, sub * 128:(sub + 1) * 128], rhs=actT[:, ak, tok0:tok0 + ntok],
                    start=(i == 0), stop=(i == nk - 1)))
            kb.grp(pe, fns, reads=[wtb, actTb] + list(extra_reads), writes=[ptb])
            return pt, ptb

        def mm_as(wt, wtb, ncols, nk, actT, actTb, ts):
            pt, ptb = nextbank()
            fns = []
            for kc in range(nk):
                fns.append(lambda kc=kc, pt=pt: T_.matmul(
                    pt[:, 0:ncols], lhsT=actT[:, kc, ts * 128:(ts + 1) * 128], rhs=wt[:, kc, 0:ncols],
                    start=(kc == 0), stop=(kc == nk - 1)))
            kb.grp(pe, fns, reads=[wtb, actTb], writes=[ptb])
            return pt, ptb

        klist32 = [(k, k) for k in range(KC)]

        def phase1():
            T = min(1024, S)
            nsub = T // 128
            nth = T // 512
            with ExitStack() as st:
                hT = sb(st, "p1hT", [128, KC, T], BF16)
                hTb = kb.buf("hT")
                nt = NormT(st, "p1", nsub, 2)
                stg = Ring(st, "p1stg", [128, 512], F32, 6)
                stgb = Ring(st, "p1stgb", [128, 512], BF16, 4)
                sig = Ring(st, "p1sig", [128, 512], F32, 2)
                blocks, kinds = [], []
                for j in range(8):
                    blocks.append(wblk(w_in, 0, KC, C_Z + j * 512, 512)); kinds.append(("z", j))
                for j in range(12):
                    blocks.append(wblk(w_in, 0, KC, C_XBC + j * 512, 512)); kinds.append(("xbc", j))
                blocks.append(wblk(w_in, 0, KC, C_DT, 128)); kinds.append(("dt", 0))
                for j in range(4):
                    blocks.append(wblk(w_in, 0, KC, C_UP + j * 512, 512)); kinds.append(("up", j))
                for j in range(16):
                    blocks.append(wblk(w_in, 0, KC, C_GA + j * 512, 512)); kinds.append(("g", j))
                nblk = len(blocks)
                ntile = S // T
                ws = WStream(st, "p1w", KC, 512, 2, blocks * ntile)
                for tt in range(ntile):
                    t0 = tt * T
                    ws.prefetch(tt * nblk + 2)
                    nt.run(x, t0, 0, hT, hTb)
                    for bi in range(nblk):
                        wt, wtb = ws.get(tt * nblk + bi)
                        kind, j = kinds[bi]
                        if kind == "z":
                            for ts in range(nsub):
                                pt, ptb = mm_as(wt, wtb, 512, KC, hT, hTb, ts)
                                sg, sgb = sig.next()
                                kb.op(act, lambda pt=pt, sg=sg: A_.activation(out=sg[:], in_=pt[:], func=AF.Sigmoid),
                                      reads=[ptb], writes=[sgb])
                                so, sob = stg.next()
                                kb.op(dve, lambda pt=pt, sg=sg, so=so: V.tensor_tensor(out=so[:], in0=pt[:], in1=sg[:],
                                                                                       op=ALU.mult),
                                      reads=[ptb, sgb], writes=[sob])
                                r = t0 + ts * 128
                                kb.dma(sp, z_s[r:r + 128, j * 512:(j + 1) * 512], so[:], reads=[sob])
                        elif kind == "dt":
                            for ts in range(nsub):
                                pt, ptb = mm_as(wt, wtb, 128, KC, hT, hTb, ts)
                                so, sob = stg.next()
                                kb.op(dve, lambda pt=pt, so=so: V.tensor_copy(out=so[:, 0:128], in_=pt[:, 0:128]),
                                      reads=[ptb], writes=[sob])
                                r = t0 + ts * 128
                                kb.dma(sp, dt_s[r:r + 128, :], so[:, 0:128], reads=[sob])
                        else:
                            for sub in range(4):
                                for th in range(nth):
                                    pt, ptb = mm_ws(wt, wtb, sub, klist32, hT, hTb, th * 512, 512)
                                    tk = t0 + th * 512
                                    f0 = j * 512 + sub * 128
                                    if kind == "g":
                                        so, sob = stgb.next()
                                        kb.op(act, lambda pt=pt, so=so: A_.activation(out=so[:], in_=pt[:],
                                                                                      func=AF.Sigmoid),
                                              reads=[ptb], writes=[sob])
                                        kb.dma(sp, gT_s[f0:f0 + 128, tk:tk + 512], so[:], reads=[sob])
                                    else:
                                        so, sob = stg.next()
                                        kb.op(dve, lambda pt=pt, so=so: V.tensor_copy(out=so[:], in_=pt[:]),
                                              reads=[ptb], writes=[sob])
                                        dst = xbcT_s if kind == "xbc" else upT_s
                                        kb.dma(sp, dst[f0:f0 + 128, tk:tk + 512], so[:], reads=[sob])
                kb.barrier()

        def phase2():
            T2 = 512
            with ExitStack() as st:
                xin = Ring(st, "p2xin", [128, 12, T2 + 4], F32, 2)
                acc = Ring(st, "p2acc", [128, 12, T2], F32, 2)
                stg = Ring(st, "p2stg", [128, 512], F32, 6)
                stgb = Ring(st, "p2stgb", [128, 512], BF16, 4)
                stgT = Ring(st, "p2stgT", [128, 512], BF16, 4)
                ptmp = Ring(st, "p2ptmp", [128, 512], F32, 3)
                def p2load(tt, grp):
                    t0 = tt * T2
                    xi, xib = xin.next()
                    lo = max(t0 - 2, 0)
                    hi = min(t0 + T2 + 2, S)
                    if t0 == 0:
                        kb.op(pool, lambda xi=xi: G.memset(xi[:, :, 0:2], 0.0), writes=[xib])
                    if t0 + T2 == S:
                        kb.op(pool, lambda xi=xi: G.memset(xi[:, :, T2 + 2:T2 + 4], 0.0), writes=[xib])
                    src = xbcT_s[grp * 1536:(grp + 1) * 1536, lo:hi].rearrange("(c p) t -> p c t", p=128)
                    kb.dma(sp, xi[:, :, lo - (t0 - 2):hi - (t0 - 2)], src, writes=[xib])
                    return xi, xib

                units = [(tt, grp) for tt in range(S // T2) for grp in range(4)]
                loaded = {0: p2load(*units[0])}
                for ui, (tt, grp) in enumerate(units):
                    if True:
                        t0 = tt * T2
                        if ui + 1 < len(units):
                            loaded[ui + 1] = p2load(*units[ui + 1])
                        xi, xib = loaded.pop(ui)
                        ac, acb = acc.next()
                        for k in range(5):
                            for j in range(12):
                                cj = grp * 12 + j
                                useP = False
                                e = pool if useP else dve
                                E_ = G if useP else V
                                if useP and k > 0:
                                    ptm, ptmb = ptmp.next()
                                    kb.op(pool, lambda j=j, cj=cj, k=k, xi=xi, ptm=ptm: G.tensor_scalar(
                                        out=ptm[:], in0=xi[:, j, k:k + T2], scalar1=cwb_t[:, cj, k:k + 1], scalar2=None,
                                        op0=ALU.mult), reads=[xib], writes=[ptmb])
                                    kb.op(pool, lambda j=j, ac=ac, ptm=ptm: G.tensor_tensor(
                                        out=ac[:, j, :], in0=ac[:, j, :], in1=ptm[:], op=ALU.add),
                                          reads=[ptmb], writes=[acb])
                                    continue
                                if k == 0:
                                    fn = lambda E_=E_, j=j, cj=cj, xi=xi, ac=ac: E_.tensor_scalar(
                                        out=ac[:, j, :], in0=xi[:, j, 0:T2], scalar1=cwb_t[:, cj, 0:1],
                                        scalar2=cwb_t[:, cj, 5:6], op0=ALU.mult, op1=ALU.add)
                                else:
                                    fn = lambda E_=E_, j=j, cj=cj, k=k, xi=xi, ac=ac: E_.scalar_tensor_tensor(
                                        out=ac[:, j, :], in0=xi[:, j, k:k + T2], scalar=cwb_t[:, cj, k:k + 1],
                                        in1=ac[:, j, :], op0=ALU.mult, op1=ALU.add)
                                kb.op(e, fn, reads=[xib], writes=[acb])
                        kb.op(act, lambda ac=ac: A_.activation(out=ac[:], in_=ac[:], func=AF.Silu),
                              reads=[], writes=[acb])
                        for q in range(3):
                            cj0 = grp * 12 + q * 4
                            if cj0 < 40:
                                for tc in range(T2 // 128):
                                    pt, ptb = nextbank()
                                    fns = []
                                    for k in range(4):
                                        fns.append(lambda k=k, pt=pt, ac=ac, q=q, tc=tc: T_.matmul(
                                            pt[:, k * 128:(k + 1) * 128], lhsT=ac[:, q * 4 + k, tc * 128:(tc + 1) * 128],
                                            rhs=ident_f[:], start=True, stop=True))
                                    kb.grp(pe, fns, reads=[acb], writes=[ptb])
                                    r = t0 + tc * 128
                                    if cj0 < 32:
                                        so, sob = stg.next()
                                        kb.op(act, lambda pt=pt, so=so: A_.copy(out=so[:], in_=pt[:]),
                                              reads=[ptb], writes=[sob])
                                        kb.dma(sp, xs_s[r:r + 128, cj0 * 128:cj0 * 128 + 512], so[:], reads=[sob])
                                    else:
                                        so, sob = stgb.next()
                                        kb.op(act, lambda pt=pt, so=so: A_.copy(out=so[:], in_=pt[:]),
                                              reads=[ptb], writes=[sob])
                                        c0 = (cj0 - 32) * 128
                                        kb.dma(sp, btok_s[r:r + 128, c0:c0 + 512], so[:], reads=[sob])
                            if cj0 >= 32:
                                for k in range(4):
                                    cj = cj0 + k
                                    so, sob = stgT.next()
                                    kb.op(dve, lambda so=so, ac=ac, q=q, k=k: V.tensor_copy(out=so[:],
                                                                                            in_=ac[:, q * 4 + k, :]),
                                          reads=[acb], writes=[sob])
                                    dstT = bT_s if cj < 40 else cT_s
                                    g = (cj - 32) % 8
                                    c0 = t0 // 128
                                    dst = dstT[c0:c0 + T2 // 128, :, g, :].rearrange("c p t -> p c t")
                                    kb.dma(sp, dst, so[:].rearrange("p (c t) -> p c t", t=128), reads=[sob])
                kb.barrier()

        def phase3():
            with ExitStack() as st:
                dtd = sb(st, "dtd", [128, NCH, 64], F32)
                a_d = sb(st, "a_d", [128, NCH, 64], F32)
                bias_t = sb(st, "bias_t", [128, 128], F32)
                A_t = sb(st, "A_t", [128, 128], F32)
                b_dt, b_a, b_bias, b_A = (kb.buf(n) for n in ("dt", "a", "bias", "A"))
                kb.dma(sp, bias_t[:], dtb.partition_broadcast(128).rearrange("p o h -> p (o h)"), writes=[b_bias])
                kb.dma(sp, A_t[:], alog.partition_broadcast(128).rearrange("p o h -> p (o h)"), writes=[b_A])
                kb.op(act, lambda: A_.activation(out=A_t[:], in_=A_t[:], func=AF.Exp), writes=[b_A])
                kb.op(dve, lambda: V.tensor_scalar(out=A_t[:], in0=A_t[:], scalar1=-1.0, scalar2=None, op0=ALU.mult),
                      writes=[b_A])

                Er = Ring(st, "p3E", [128, 192], F32, 3)
                ddr = Ring(st, "p3dd", [128, 64], F32, 3)
                aT = Ring(st, "p3aT", [128, 16, 128], F32, 2)
                xsr = Ring(st, "p3xs", [128, D], F32, 2)
                btr = Ring(st, "p3bt", [128, 1024], BF16, 2)
                bTr = Ring(st, "p3bT", [128, 8, 128], BF16, 2)
                cTr = Ring(st, "p3cT", [128, 8, 128], BF16, 2)
                cbm = Ring(st, "p3cbm", [128, 8, 128], BF16, 2)
                dec = Ring(st, "p3dec", [128, 512], F32, 6)
                MT = Ring(st, "p3MT", [128, 8, 128], BF16, 4)
                xdt = Ring(st, "p3xdt", [128, D], BF16, 2)
                xdte = Ring(st, "p3xdte", [128, D], BF16, 2)
                H = sb(st, "p3H", [128, 8, 512], F32)
                Hbf = sb(st, "p3Hbf", [128, 8, 512], BF16)
                Hb = [kb.buf(f"H{g}") for g in range(8)]
                Hbfb = [kb.buf(f"Hbf{g}") for g in range(8)]
                ystg = Ring(st, "p3y", [128, 512], F32, 4)
                tmp = Ring(st, "p3tmp", [128, 512], F32, 3)

                for d in (1, 0):
                    hs = slice(64 * d, 64 * d + 64)
                    tri_incl = triU if d == 0 else triL
                    tri_strict = triLs if d == 0 else triUs
                    ll_ = xsr.t[0][:, 0:NCH * 64].rearrange("p (c h) -> p c h", h=64)
                    b_l = xsr.b[0]
                    kb.dma(sp, dtd[:], dt_s[:, hs].rearrange("(c p) h -> p c h", p=128), writes=[b_dt])
                    bb = bias_t[:, hs].unsqueeze(1).to_broadcast([128, NCH, 64])
                    kb.op(dve, lambda bb=bb: V.tensor_tensor(out=a_d[:], in0=dtd[:], in1=bb, op=ALU.add),
                          reads=[b_dt, b_bias], writes=[b_a])
                    kb.op(act, lambda ll_=ll_: A_.activation(out=ll_, in_=a_d[:], func=AF.Abs),
                          reads=[b_a], writes=[b_l])
                    kb.op(act, lambda ll_=ll_: A_.activation(out=ll_, in_=ll_, func=AF.Exp, scale=-1.0), writes=[b_l])
                    kb.op(act, lambda ll_=ll_: A_.activation(out=ll_, in_=ll_, func=AF.Ln, bias=1.0), writes=[b_l])
                    kb.op(dve, lambda ll_=ll_: V.scalar_tensor_tensor(out=dtd[:], in0=a_d[:], scalar=0.0, in1=ll_,
                                                                      op0=ALU.max, op1=ALU.add),
                          reads=[b_a, b_l], writes=[b_dt])
                    ab = A_t[:, hs].unsqueeze(1).to_broadcast([128, NCH, 64])
                    kb.op(dve, lambda ab=ab: V.tensor_tensor(out=a_d[:], in0=dtd[:], in1=ab, op=ALU.mult),
                          reads=[b_dt, b_A], writes=[b_a])
                    for g in range(8):
                        kb.op(dve, lambda g=g: V.memset(H[:, g, :], 0.0), writes=[Hb[g]])
                        kb.op(dve, lambda g=g: V.memset(Hbf[:, g, :], 0.0), writes=[Hbfb[g]])
                    order = list(range(NCH)) if d == 0 else list(range(NCH - 1, -1, -1))

                    def prologue(c):
                        P = {"c": c}
                        r = c * 128
                        xs, xsb = xsr.next()
                        bt, btb = btr.next()
                        bTt, bTb = bTr.next()
                        cTt, cTb = cTr.next()
                        kb.dma(sp, xs[:], xs_s[r:r + 128, :], writes=[xsb])
                        kb.dma(sp, bt[:], btok_s[r:r + 128, :], writes=[btb])
                        kb.dma(sp, bTt[:], bT_s[c], writes=[bTb])
                        kb.dma(sp, cTt[:], cT_s[c], writes=[cTb])
                        E, Eb = Er.next()
                        dd, ddb = ddr.next()
                        pt, ptb = nextbank()
                        fns = []
                        for i, lt in enumerate((tri_incl, tri_strict, ones)):
                            fns.append(lambda i=i, lt=lt, pt=pt, c=c: T_.matmul(
                                pt[:, i * 64:(i + 1) * 64], lhsT=lt[:], rhs=a_d[:, c, :], start=True, stop=True))
                        kb.grp(pe, fns, reads=[b_a], writes=[ptb])
                        kb.op(act, lambda pt=pt, E=E: A_.activation(out=E[:], in_=pt[:, 0:192], func=AF.Exp),
                              reads=[ptb], writes=[Eb])
                        kb.op(dve, lambda dd=dd, E=E, c=c: V.tensor_tensor(out=dd[:], in0=dtd[:, c, :], in1=E[:, 64:128],
                                                                           op=ALU.mult),
                              reads=[b_dt, Eb], writes=[ddb])
                        xd, xdb = xdt.next()
                        xe, xeb = xdte.next()
                        kb.op(pool, lambda xs=xs, xd=xd, c=c: G.tensor_tensor(
                            out=xd[:].rearrange("p (h e) -> p h e", e=64), in0=xs[:].rearrange("p (h e) -> p h e", e=64),
                            in1=dtd[:, c, :].unsqueeze(2).to_broadcast([128, 64, 64]), op=ALU.mult),
                              reads=[xsb, b_dt], writes=[xdb])
                        kb.op(dve, lambda xs=xs, xe=xe, dd=dd: V.tensor_tensor(
                            out=xe[:].rearrange("p (h e) -> p h e", e=64), in0=xs[:].rearrange("p (h e) -> p h e", e=64),
                            in1=dd[:].unsqueeze(2).to_broadcast([128, 64, 64]), op=ALU.mult),
                              reads=[xsb, ddb], writes=[xeb])
                        cm, cmb = cbm.next()
                        for half in range(2):
                            pt, ptb = nextbank()
                            fns = []
                            for k in range(4):
                                g = half * 4 + k
                                fns.append(lambda k=k, g=g, pt=pt, bTt=bTt, cTt=cTt: T_.matmul(
                                    pt[:, k * 128:(k + 1) * 128], lhsT=bTt[:, g, :], rhs=cTt[:, g, :],
                                    start=True, stop=True))
                            kb.grp(pe, fns, reads=[bTb, cTb], writes=[ptb])
                            kb.op(dve, lambda pt=pt, cm=cm, half=half: V.tensor_tensor(
                                out=cm[:, half * 4:(half + 1) * 4, :], in0=pt[:].rearrange("p (g l) -> p g l", g=4),
                                in1=tri_incl[:].unsqueeze(1).to_broadcast([128, 4, 128]), op=ALU.mult),
                                  reads=[ptb], writes=[cmb])
                        P.update(xs=xs, xsb=xsb, bt=bt, btb=btb, cTt=cTt, cTb=cTb, E=E, Eb=Eb, xd=xd, xdb=xdb,
                                 xe=xe, xeb=xeb, cm=cm, cmb=cmb, at=None, atb=None, mt={}, mtb={})
                        return P

                    def stageA(P, g):
                        c = P["c"]
                        if g % 2 == 0:
                            at, atb = aT.next()
                            h0 = g * 8
                            kb.op(pool, lambda at=at, c=c, h0=h0: G.tensor_tensor(
                                out=at[:], in0=a_d[:, c, h0:h0 + 16].unsqueeze(2).to_broadcast([128, 16, 128]),
                                in1=tri_incl[:].unsqueeze(1).to_broadcast([128, 16, 128]), op=ALU.mult),
                                  reads=[b_a], writes=[atb])
                            P["at"], P["atb"] = at, atb
                        at, atb = P["at"], P["atb"]
                        cm, cmb = P["cm"], P["cmb"]
                        mt, mtb = MT.next()
                        for half in range(2):
                            pt, ptb = nextbank()
                            hh = (g % 2) * 8 + half * 4
                            kb.grp(pe, [lambda pt=pt, at=at, hh=hh: T_.matmul(
                                pt[:], lhsT=tri_strict[:], rhs=at[:, hh:hh + 4, :], start=True, stop=True)],
                                   reads=[atb], writes=[ptb])
                            dc, dcb = dec.next()
                            kb.op(act, lambda pt=pt, dc=dc: A_.activation(out=dc[:], in_=pt[:], func=AF.Exp),
                                  reads=[ptb], writes=[dcb])
                            kb.op(dve, lambda dc=dc, mt=mt, cm=cm, g=g, half=half: V.tensor_tensor(
                                out=mt[:, half * 4:(half + 1) * 4, :], in0=dc[:].rearrange("p (h l) -> p h l", h=4),
                                in1=cm[:, g, :].unsqueeze(1).to_broadcast([128, 4, 128]), op=ALU.mult),
                                  reads=[dcb, cmb], writes=[mtb])
                        P["mt"][g], P["mtb"][g] = mt, mtb

                    def stageB(P, g):
                        c = P["c"]
                        r = c * 128
                        mt, mtb = P["mt"][g], P["mtb"][g]
                        xd, xdb, xe, xeb = P["xd"], P["xdb"], P["xe"], P["xeb"]
                        cTt, cTb, bt, btb, E, Eb = P["cTt"], P["cTb"], P["bt"], P["btb"], P["E"], P["Eb"]
                        py, pyb = nextbank()
                        fns = []
                        for hl in range(8):
                            col = g * 512 + hl * 64
                            fns.append(lambda hl=hl, col=col, py=py, mt=mt, xd=xd: T_.matmul(
                                py[:, hl * 64:(hl + 1) * 64], lhsT=mt[:, hl, :], rhs=xd[:, col:col + 64],
                                start=True, stop=True))
                        kb.grp(pe, fns, reads=[mtb, xdb], writes=[pyb])
                        po, pob = nextbank()
                        kb.grp(pe, [lambda po=po, cTt=cTt, g=g: T_.matmul(po[:], lhsT=cTt[:, g, :], rhs=Hbf[:, g, :],
                                                                           start=True, stop=True)],
                               reads=[cTb, Hbfb[g]], writes=[pob])
                        pst, pstb = nextbank()
                        kb.grp(pe, [lambda pst=pst, bt=bt, xe=xe, g=g: T_.matmul(
                            pst[:], lhsT=bt[:, g * 128:(g + 1) * 128], rhs=xe[:, g * 512:(g + 1) * 512],
                            start=True, stop=True)], reads=[btb, xeb], writes=[pstb])
                        tm, tmb = tmp.next()
                        kb.op(dve, lambda po=po, tm=tm, E=E, g=g: V.tensor_tensor(
                            out=tm[:].rearrange("p (h e) -> p h e", e=64), in0=po[:].rearrange("p (h e) -> p h e", e=64),
                            in1=E[:, g * 8:(g + 1) * 8].unsqueeze(2).to_broadcast([128, 8, 64]), op=ALU.mult),
                              reads=[pob, Eb], writes=[tmb])
                        yo, yob = ystg.next()
                        kb.op(dve, lambda py=py, tm=tm, yo=yo: V.tensor_tensor(out=yo[:], in0=py[:], in1=tm[:],
                                                                               op=ALU.add),
                              reads=[pyb, tmb], writes=[yob])
                        kb.dma(sp, yd_s[d][r:r + 128, g * 512:(g + 1) * 512], yo[:], reads=[yob])
                        kb.op(pool, lambda E=E, g=g: G.tensor_tensor(
                            out=H[:, g, :].rearrange("p (h e) -> p h e", e=64),
                            in0=H[:, g, :].rearrange("p (h e) -> p h e", e=64),
                            in1=E[:, 128 + g * 8:128 + (g + 1) * 8].unsqueeze(2).to_broadcast([128, 8, 64]),
                            op=ALU.mult), reads=[Eb], writes=[Hb[g]])
                        kb.op(dve, lambda pst=pst, g=g: V.tensor_tensor(out=H[:, g, :], in0=H[:, g, :], in1=pst[:],
                                                                        op=ALU.add),
                              reads=[pstb], writes=[Hb[g]])
                        kb.op(act, lambda g=g: A_.copy(out=Hbf[:, g, :], in_=H[:, g, :]),
                              reads=[Hb[g]], writes=[Hbfb[g]])

                    steps = [(ci, g) for ci in range(len(order)) for g in range(8)]
                    Ps = {0: prologue(order[0])}
                    stageA(Ps[0], 0)
                    stageA(Ps[0], 1)
                    for k, (ci, g) in enumerate(steps):
                        if g == 2 and ci + 1 < len(order):
                            Ps[ci + 1] = prologue(order[ci + 1])
                        stageB(Ps[ci], g)
                        if k + 2 < len(steps):
                            ci2, g2 = steps[k + 2]
                            stageA(Ps[ci2], g2)
                        if g == 7:
                            del Ps[ci]
                kb.barrier()

        def phase3c():
            PW = 1024
            with ExitStack() as st:
                dsk = sb(st, "dsk", [128, NH], F32)
                b_dsk = kb.buf("dsk")
                kb.dma(sp, dsk[:], dskip.partition_broadcast(128).rearrange("p o h -> p (o h)"), writes=[b_dsk])
                yfr = Ring(st, "c_yf", [128, PW], F32, 2)
                ybr = Ring(st, "c_yb", [128, PW], F32, 2)
                xsr = Ring(st, "c_xs", [128, PW], F32, 2)
                zsr = Ring(st, "c_zs", [128, PW], F32, 2)
                junk = Ring(st, "c_junk", [128, 512], BF16, 2)
                ssr = Ring(st, "c_ss", [128, 4], F32, 4)
                ynr = Ring(st, "c_yn", [128, PW], BF16, 2)
                yTst = Ring(st, "c_yT", [128, KC, 512], BF16, 2)
                yts, ytb = None, None
                for c in range(NCH):
                    r = c * 128
                    if c % 4 == 0:
                        yts, ytb = yTst.next()
                    for pc in range(D // PW):
                        cs = slice(pc * PW, (pc + 1) * PW)
                        yf, yfb = yfr.next()
                        yb, ybb = ybr.next()
                        xs, xsb = xsr.next()
                        zs, zsb = zsr.next()
                        kb.dma(sp, yf[:], yd_s[0][r:r + 128, cs], writes=[yfb])
                        kb.dma(sp, yb[:], yd_s[1][r:r + 128, cs], writes=[ybb])
                        kb.dma(sp, xs[:], xs_s[r:r + 128, cs], writes=[xsb])
                        kb.dma(sp, zs[:], z_s[r:r + 128, cs], writes=[zsb])
                        nh = PW // 64
                        kb.op(pool, lambda yf=yf, yb=yb: G.tensor_tensor(out=yf[:], in0=yf[:], in1=yb[:], op=ALU.add),
                              reads=[ybb], writes=[yfb])
                        kb.op(pool, lambda xs=xs, pc=pc: G.tensor_tensor(
                            out=xs[:].rearrange("p (h e) -> p h e", e=64), in0=xs[:].rearrange("p (h e) -> p h e", e=64),
                            in1=dsk[:, pc * nh:(pc + 1) * nh].unsqueeze(2).to_broadcast([128, nh, 64]), op=ALU.mult),
                              reads=[b_dsk], writes=[xsb])
                        kb.op(dve, lambda yf=yf, xs=xs: V.tensor_tensor(out=yf[:], in0=yf[:], in1=xs[:], op=ALU.add),
                              reads=[xsb], writes=[yfb])
                        kb.op(dve, lambda yf=yf, zs=zs: V.tensor_tensor(out=yf[:], in0=yf[:], in1=zs[:], op=ALU.mult),
                              reads=[zsb], writes=[yfb])
                        ss, ssb = ssr.next()
                        for gq in range(PW // 512):
                            jk, jkb = junk.next()
                            kb.op(act, lambda yf=yf, jk=jk, ss=ss, gq=gq: A_.activation(
                                out=jk[:], in_=yf[:, gq * 512:(gq + 1) * 512], func=AF.Square,
                                accum_out=ss[:, gq:gq + 1]), reads=[yfb], writes=[jkb, ssb])
                        kb.op(act, lambda ss=ss: A_.activation(out=ss[:, 2:4], in_=ss[:, 0:2], func=AF.Sqrt,
                                                               scale=1.0 / 512, bias=EPS), writes=[ssb])
                        kb.op(dve, lambda ss=ss: V.reciprocal(out=ss[:, 2:4], in_=ss[:, 2:4]), writes=[ssb])
                        yn, ynb = ynr.next()
                        kb.op(dve, lambda yf=yf, yn=yn, ss=ss: V.tensor_tensor(
                            out=yn[:].rearrange("p (g e) -> p g e", e=512), in0=yf[:].rearrange("p (g e) -> p g e", e=512),
                            in1=ss[:, 2:4].unsqueeze(2).to_broadcast([128, 2, 512]),
                            op=ALU.mult), reads=[yfb, ssb], writes=[ynb])
                        pt, ptb = nextbank()
                        ptv = pt[:].bitcast(BF16)
                        fns = []
                        for k in range(8):
                            fns.append(lambda k=k, ptv=ptv, yn=yn: T_.transpose(
                                out=ptv[:, k * 128:(k + 1) * 128], in_=yn[:, k * 128:(k + 1) * 128], identity=ident_b[:]))
                        kb.grp(pe, fns, reads=[ynb], writes=[ptb])
                        kb.op(dve, lambda ptv=ptv, yts=yts, pc=pc, c=c: V.tensor_tensor(
                            out=yts[:, pc * 8:(pc + 1) * 8, (c % 4) * 128:(c % 4 + 1) * 128],
                            in0=ptv.rearrange("p (k t) -> p k t", k=8),
                            in1=gains_t[:, 3, pc * 8:(pc + 1) * 8].unsqueeze(2).to_broadcast([128, 8, 128]),
                            op=ALU.mult), reads=[ptb], writes=[ytb])
                    if c % 4 == 3:
                        t0 = (c - 3) * 128
                        kb.dma(sp, yT_s[:, t0:t0 + 512].rearrange("(kc p) t -> p kc t", p=128), yts[:], reads=[ytb])
                kb.barrier()

        def phase4():
            T4 = 512
            HL = 8
            W4 = T4 + 2 * HL
            with ExitStack() as st:
                pw = sb(st, "p4pw", [128, 16, 512], BF16)
                b_pw = kb.buf("pw")
                kb.dma(pool, pw[:], pool_w.rearrange("(kc p) n -> p kc n", p=128), writes=[b_pw])
                ur = Ring(st, "p4u", [128, 16, W4], F32, 2)
                w1 = sb(st, "p4w1", [128, 4, W4], F32)
                w2 = sb(st, "p4w2", [128, 4, W4], F32)
                b_w1, b_w2 = kb.buf("w1"), kb.buf("w2")
                pdr = Ring(st, "p4pd", [128, 16, T4], BF16, 2)
                stgb = Ring(st, "p4stg", [128, 512], BF16, 4)
                def p4load(tt):
                    t0 = tt * T4
                    u, ub = ur.next()
                    lo = max(t0 - HL, 0)
                    hi = min(t0 + T4 + HL, S)
                    if t0 == 0:
                        kb.op(pool, lambda u=u: G.memset(u[:, :, 0:HL], 0.0), writes=[ub])
                    if t0 + T4 == S:
                        kb.op(pool, lambda u=u: G.memset(u[:, :, T4 + HL:W4], 0.0), writes=[ub])
                    kb.dma(sp, u[:, :, lo - (t0 - HL):hi - (t0 - HL)],
                           upT_s[:, lo:hi].rearrange("(c p) t -> p c t", p=128), writes=[ub])
                    return u, ub

                uld = {0: p4load(0)}
                for tt in range(S // T4):
                    t0 = tt * T4
                    if tt + 1 < S // T4:
                        uld[tt + 1] = p4load(tt + 1)
                    u, ub = uld.pop(tt)
                    pd, pdb = pdr.next()
                    for gi, w in enumerate((2, 4, 8, 16)):
                        us = u[:, gi * 4:(gi + 1) * 4, :]
                        kb.op(dve, lambda us=us: V.tensor_tensor(out=w1[:, :, 1:W4], in0=us[:, :, 0:W4 - 1],
                                                                 in1=us[:, :, 1:W4], op=ALU.add),
                              reads=[ub], writes=[b_w1])
                        cur, curb, oth, othb = w1, b_w1, w2, b_w2
                        lo_v, hi_v = 1, W4
                        sh = 1
                        ww = 2
                        while ww < w:
                            nlo, nhi = lo_v + sh, hi_v - sh
                            kb.op(dve, lambda cur=cur, oth=oth, nlo=nlo, nhi=nhi, sh=sh: V.tensor_tensor(
                                out=oth[:, :, nlo:nhi], in0=cur[:, :, nlo - sh:nhi - sh], in1=cur[:, :, nlo + sh:nhi + sh],
                                op=ALU.add), reads=[curb], writes=[othb])
                            cur, curb, oth, othb = oth, othb, cur, curb
                            lo_v, hi_v = nlo, nhi
                            sh *= 2
                            ww *= 2
                        h = w // 2
                        if t0 == 0:
                            for t in range(h):
                                cnt = (t + (w - h)) - 0
                                kb.op(dve, lambda cur=cur, t=t, cnt=cnt, w=w: V.tensor_scalar(
                                    out=cur[:, :, HL + t:HL + t + 1], in0=cur[:, :, HL + t:HL + t + 1],
                                    scalar1=float(w) / cnt, scalar2=None, op0=ALU.mult), writes=[curb])
                        if t0 + T4 == S:
                            for t in range(S - (w - h) + 1, S):
                                cnt = S - (t - h)
                                jj = HL + (t - t0)
                                kb.op(dve, lambda cur=cur, jj=jj, cnt=cnt, w=w: V.tensor_scalar(
                                    out=cur[:, :, jj:jj + 1], in0=cur[:, :, jj:jj + 1],
                                    scalar1=float(w) / cnt, scalar2=None, op0=ALU.mult), writes=[curb])
                        kb.op(dve, lambda cur=cur, us=us, pd=pd, gi=gi, w=w: V.scalar_tensor_tensor(
                            out=pd[:, gi * 4:(gi + 1) * 4, :], in0=cur[:, :, HL:HL + T4], scalar=1.0 / w,
                            in1=us[:, :, HL:HL + T4], op0=ALU.mult, op1=ALU.subtract),
                              reads=[curb, ub], writes=[pdb])
                    for gi in range(4):
                        for sub in range(4):
                            klist = [(gi * 4 + k, gi * 4 + k) for k in range(4)]
                            pt, ptb = mm_ws(pw, b_pw, sub, klist, pd, pdb, 0, T4)
                            so, sob = stgb.next()
                            fi = gi * 4 + sub
                            kb.op(act, lambda pt=pt, so=so, fi=fi: A_.activation(out=so[:], in_=pt[:], func=AF.Identity,
                                                                                 scale=pscale_t[:, fi:fi + 1]),
                                  reads=[ptb], writes=[sob])
                            kb.dma(sp, pbT_s[fi * 128:(fi + 1) * 128, t0:t0 + T4], so[:], reads=[sob])
                kb.barrier()

        def phase5():
            T = min(1024, S)
            BW = 256
            NB = D // BW
            nsb = BW // 128
            nth = T // 512
            with ExitStack() as st:
                yT = Ring(st, "p5yT", [128, KC, T], BF16, 1)
                pbT = Ring(st, "p5pbT", [128, 16, T], BF16, 1)
                gar = Ring(st, "p5ga", [128, nsb, T], BF16, 2)
                gbr = Ring(st, "p5gb", [128, nsb, T], BF16, 2)
                m1r = Ring(st, "p5m1", [128, 512], F32, 2)
                m2r = Ring(st, "p5m2", [128, 512], F32, 2)
                stgb = Ring(st, "p5stg", [128, 512], BF16, 4)
                ntile = S // T
                bl_up = [wblk(w_ssd_up, 0, KC, j * BW, BW) for j in range(NB)]
                bl_pu = [wblk(w_pool_up, 0, 16, j * BW, BW) for j in range(NB)]
                wsu = WStream(st, "p5wu", KC, BW, 2, bl_up * ntile)
                wsp = WStream(st, "p5wp", 16, BW, 2, bl_pu * ntile)
                kl16 = [(k, k) for k in range(16)]
                for tt in range(ntile):
                    t0 = tt * T
                    wsu.prefetch(tt * NB + 2)
                    wsp.prefetch(tt * NB + 2)
                    y_, yb_ = yT.next()
                    p_, pb_ = pbT.next()
                    kb.dma(sp, y_[:], yT_s[:, t0:t0 + T].rearrange("(kc p) t -> p kc t", p=128), writes=[yb_])
                    kb.dma(sp, p_[:], pbT_s[:, t0:t0 + T].rearrange("(kc p) t -> p kc t", p=128), writes=[pb_])
                    for j in range(NB):
                        wu, wub = wsu.get(tt * NB + j)
                        wp, wpb = wsp.get(tt * NB + j)
                        ga, gab = gar.next()
                        gb, gbb = gbr.next()
                        kb.dma(sp, ga[:], gT_s[j * BW:(j + 1) * BW, t0:t0 + T].rearrange("(c p) t -> p c t", p=128),
                               writes=[gab])
                        kb.dma(sp, gb[:], gT_s[D + j * BW:D + (j + 1) * BW, t0:t0 + T].rearrange("(c p) t -> p c t", p=128),
                               writes=[gbb])
                        for sub in range(nsb):
                            for th in range(nth):
                                tk = th * 512
                                pa, pab = mm_ws(wu, wub, sub, klist32, y_, yb_, tk, 512)
                                pb2, pbb2 = mm_ws(wp, wpb, sub, kl16, p_, pb_, tk, 512)
                                m1, m1b = m1r.next()
                                m2, m2b = m2r.next()
                                kb.op(dve, lambda pa=pa, ga=ga, m1=m1, sub=sub, tk=tk: V.tensor_tensor(
                                    out=m1[:], in0=pa[:], in1=ga[:, sub, tk:tk + 512], op=ALU.mult),
                                      reads=[pab, gab], writes=[m1b])
                                kb.op(dve, lambda pb2=pb2, gb=gb, m2=m2, sub=sub, tk=tk: V.tensor_tensor(
                                    out=m2[:], in0=pb2[:], in1=gb[:, sub, tk:tk + 512], op=ALU.mult),
                                      reads=[pbb2, gbb], writes=[m2b])
                                so, sob = stgb.next()
                                kb.op(pool, lambda m1=m1, m2=m2, so=so: G.tensor_tensor(out=so[:], in0=m1[:], in1=m2[:],
                                                                                        op=ALU.add),
                                      reads=[m1b, m2b], writes=[sob])
                                f0 = j * BW + sub * 128
                                kb.dma(sp, mixT_s[f0:f0 + 128, t0 + tk:t0 + tk + 512], so[:], reads=[sob])
                kb.barrier()

        def phase6():
            T = min(1024, S)
            nsub = T // 128
            with ExitStack() as st:
                mT = Ring(st, "p6mT", [128, KC, T], BF16, 1)
                xr = Ring(st, "p6x", [128, 512], F32, 4)
                stg = Ring(st, "p6stg", [128, 512], F32, 4)
                ntile = S // T
                bl = [wblk(w_out, 0, KC, j * 512, 512) for j in range(8)]
                ws = WStream(st, "p6w", KC, 512, 2, bl * ntile)
                for tt in range(ntile):
                    t0 = tt * T
                    ws.prefetch(tt * 8 + 2)
                    m_, mb_ = mT.next()
                    kb.dma(sp, m_[:], mixT_s[:, t0:t0 + T].rearrange("(kc p) t -> p kc t", p=128), writes=[mb_])
                    for j in range(8):
                        wt, wtb = ws.get(tt * 8 + j)
                        for ts in range(nsub):
                            r = t0 + ts * 128
                            xb, xbb = xr.next()
                            kb.dma(sp, xb[:], x[r:r + 128, j * 512:(j + 1) * 512], writes=[xbb])
                            pt, ptb = mm_as(wt, wtb, 512, KC, m_, mb_, ts)
                            so, sob = stg.next()
                            kb.op(dve, lambda pt=pt, xb=xb, so=so: V.tensor_tensor(out=so[:], in0=pt[:], in1=xb[:],
                                                                                   op=ALU.add),
                                  reads=[ptb, xbb], writes=[sob])
                            kb.dma(sp, x1_s[r:r + 128, j * 512:(j + 1) * 512], so[:], reads=[sob])
                kb.barrier()

        def phase7():
            T = min(1024, S)
            nsub = T // 128
            nth = T // 512
            with ExitStack() as st:
                hT = sb(st, "p7hT", [128, KC, T], BF16)
                hTb = kb.buf("hT")
                nt = NormT(st, "p7", nsub, 2)
                stgb = Ring(st, "p7stg", [128, 512], BF16, 6)
                rlr = Ring(st, "p7rl", [128, 512], F32, 3)
                ntile = S // T
                bl = [wblk(w_ff1, 0, KC, j * 512, 512) for j in range(32)]
                ws = WStream(st, "p7w", KC, 512, 2, bl * ntile)
                for tt in range(ntile):
                    t0 = tt * T
                    ws.prefetch(tt * 32 + 2)
                    nt.run(x1_s, t0, 1, hT, hTb)
                    for j in range(32):
                        wt, wtb = ws.get(tt * 32 + j)
                        for sub in range(4):
                            for th in range(nth):
                                pt, ptb = mm_ws(wt, wtb, sub, klist32, hT, hTb, th * 512, 512)
                                so, sob = stgb.next()
                                rl, rlb = rlr.next()
                                kb.op(act, lambda pt=pt, rl=rl: A_.activation(out=rl[:], in_=pt[:], func=AF.Relu),
                                      reads=[ptb], writes=[rlb])
                                kb.op(dve, lambda rl=rl, so=so: V.tensor_tensor(out=so[:], in0=rl[:], in1=rl[:],
                                                                                op=ALU.mult),
                                      reads=[rlb], writes=[sob])
                                f0 = j * 512 + sub * 128
                                tk = t0 + th * 512
                                kb.dma(sp, uT_s[f0:f0 + 128, tk:tk + 512], so[:], reads=[sob])
                kb.barrier()

        def phase8():
            T = min(1024, S)
            nsub = T // 128
            KG = 16
            NKG = DFF // (KG * 128)
            CH = 2048
            with ExitStack() as st:
                acc = sb(st, "p8acc", [128, nsub, CH], F32)
                accb = [kb.buf(f"acc{ts}") for ts in range(nsub)]
                ur = Ring(st, "p8u", [128, KG, T], BF16, 2)
                x1r = Ring(st, "p8x1", [128, 512], F32, 4)
                ntile = S // T
                bl = []
                for tt in range(ntile):
                    for ch in range(D // CH):
                        for kg in range(NKG):
                            for j in range(CH // 512):
                                bl.append(wblk(w_ff2, kg * KG * 128, KG, ch * CH + j * 512, 512))
                ws = WStream(st, "p8w", KG, 512, 3, bl)
                bi = 0
                for tt in range(ntile):
                    t0 = tt * T
                    for ch in range(D // CH):
                        c0 = ch * CH
                        for kg in range(NKG):
                            u_, ub_ = ur.next()
                            k0 = kg * KG * 128
                            kb.dma(sp, u_[:], uT_s[k0:k0 + KG * 128, t0:t0 + T].rearrange("(kc p) t -> p kc t", p=128),
                                   writes=[ub_])
                            for j in range(CH // 512):
                                wt, wtb = ws.get(bi)
                                bi += 1
                                for ts in range(nsub):
                                    if kg == 0:
                                        r = t0 + ts * 128
                                        xb, xbb = x1r.next()
                                        kb.dma(sp, xb[:], x1_s[r:r + 128, c0 + j * 512:c0 + (j + 1) * 512], writes=[xbb])
                                    pt, ptb = mm_as(wt, wtb, 512, KG, u_, ub_, ts)
                                    if kg == 0:
                                        kb.op(dve, lambda pt=pt, ts=ts, j=j, xb=xb: V.tensor_tensor(
                                            out=acc[:, ts, j * 512:(j + 1) * 512], in0=pt[:], in1=xb[:], op=ALU.add),
                                              reads=[ptb, xbb], writes=[accb[ts]])
                                    else:
                                        kb.op(dve, lambda pt=pt, ts=ts, j=j: V.tensor_tensor(
                                            out=acc[:, ts, j * 512:(j + 1) * 512], in0=acc[:, ts, j * 512:(j + 1) * 512],
                                            in1=pt[:], op=ALU.add), reads=[ptb], writes=[accb[ts]])
                        for ts in range(nsub):
                            r = t0 + ts * 128
                            kb.dma(sp, x2_s[r:r + 128, c0:c0 + CH], acc[:, ts, :], reads=[accb[ts]])
                kb.barrier()

        def phase9():
            T = 512
            nsub = T // 128
            with ExitStack() as st:
                hT = sb(st, "p9hT", [128, KC, T], BF16)
                hTb = kb.buf("hT")
                nt = NormT(st, "p9", nsub, 2)
                pT = sb(st, "p9pT", [128, 2, T], BF16)
                pTb = kb.buf("pT")
                wp = sb(st, "p9wp", [128, 2, D], BF16)
                wpb = kb.buf("wp")
                kb.dma(pool, wp[:], w_ple_proj.rearrange("(kc p) n -> p kc n", p=128), writes=[wpb])
                plr = Ring(st, "p9pl", [128, PLE], F32, 2)
                pbr = Ring(st, "p9pb", [128, PLE], BF16, 2)
                gr = Ring(st, "p9g", [128, 512], F32, 2)
                tr = Ring(st, "p9t", [128, 512], F32, 2)
                xr = Ring(st, "p9x", [128, 512], F32, 4)
                stg = Ring(st, "p9stg", [128, 512], F32, 4)
                ntile = S // T
                bl = [wblk(w_ple_gate, 0, KC, j * 512, 512) for j in range(8)]
                ws = WStream(st, "p9w", KC, 512, 2, bl * ntile)
                for tt in range(ntile):
                    t0 = tt * T
                    ws.prefetch(tt * 8 + 2)
                    nt.run(x2_s, t0, 2, hT, hTb)
                    for ts in range(nsub):
                        r = t0 + ts * 128
                        pl, plb = plr.next()
                        pb_, pbb = pbr.next()
                        kb.dma(sp, pl[:], pin[r:r + 128, :], writes=[plb])
                        kb.op(act, lambda pl=pl, pb_=pb_: A_.copy(out=pb_[:], in_=pl[:]), reads=[plb], writes=[pbb])
                        pt, ptb = nextbank()
                        ptv = pt[:].bitcast(BF16)
                        fns = [lambda k=k, ptv=ptv, pb_=pb_: T_.transpose(out=ptv[:, k * 128:(k + 1) * 128],
                                                                         in_=pb_[:, k * 128:(k + 1) * 128],
                                                                         identity=ident_b[:]) for k in range(2)]
                        kb.grp(pe, fns, reads=[pbb], writes=[ptb])
                        kb.op(dve, lambda ptv=ptv, ts=ts: V.tensor_copy(
                            out=pT[:, :, ts * 128:(ts + 1) * 128], in_=ptv[:, 0:256].rearrange("p (k t) -> p k t", k=2)),
                              reads=[ptb], writes=[pTb])
                    for j in range(8):
                        wt, wtb = ws.get(tt * 8 + j)
                        for ts in range(nsub):
                            r = t0 + ts * 128
                            xb, xbb = xr.next()
                            kb.dma(sp, xb[:], x2_s[r:r + 128, j * 512:(j + 1) * 512], writes=[xbb])
                            pg, pgb = mm_as(wt, wtb, 512, KC, hT, hTb, ts)
                            pp, ppb = nextbank()
                            fns = [lambda k=k, pp=pp, ts=ts, j=j: T_.matmul(
                                pp[:], lhsT=pT[:, k, ts * 128:(ts + 1) * 128], rhs=wp[:, k, j * 512:(j + 1) * 512],
                                start=(k == 0), stop=(k == 1)) for k in range(2)]
                            kb.grp(pe, fns, reads=[pTb, wpb], writes=[ppb])
                            g_, gb_ = gr.next()
                            kb.op(act, lambda pg=pg, g_=g_: A_.activation(out=g_[:], in_=pg[:], func=AF.Sigmoid),
                                  reads=[pgb], writes=[gb_])
                            t_, tb_ = tr.next()
                            kb.op(dve, lambda pp=pp, g_=g_, t_=t_: V.tensor_tensor(out=t_[:], in0=pp[:], in1=g_[:],
                                                                                   op=ALU.mult),
                                  reads=[ppb, gb_], writes=[tb_])
                            so, sob = stg.next()
                            kb.op(dve, lambda t_=t_, xb=xb, so=so: V.tensor_tensor(out=so[:], in0=t_[:], in1=xb[:],
                                                                                   op=ALU.add),
                                  reads=[tb_, xbb], writes=[sob])
                            kb.dma(sp, x3_s[r:r + 128, j * 512:(j + 1) * 512], so[:], reads=[sob])
                kb.barrier()

        def phase10():
            with ExitStack() as st:
                gf = sb(st, "gf", [128, D], F32)
                gfb = kb.buf("gf")
                kb.dma(sp, gf[:], g_final.partition_broadcast(128).rearrange("p o h -> p (o h)"), writes=[gfb])
                xl = Ring(st, "p10x", [128, D], F32, 4)
                jk = Ring(st, "p10j", [128, D], BF16, 1)
                ssr = Ring(st, "p10s", [128, 2], F32, 4)
                def p10load(c):
                    xt, xtb = xl.next()
                    kb.dma(sp, xt[:], x3_s[c * 128:(c + 1) * 128, :], writes=[xtb])
                    return xt, xtb

                lds = {c: p10load(c) for c in range(min(2, NCH))}
                for c in range(NCH):
                    r = c * 128
                    if c + 2 < NCH:
                        lds[c + 2] = p10load(c + 2)
                    xt, xtb = lds.pop(c)
                    j_, jb_ = jk.next()
                    ss, ssb = ssr.next()
                    kb.op(act, lambda xt=xt, j_=j_, ss=ss: A_.activation(out=j_[:], in_=xt[:], func=AF.Square,
                                                                         accum_out=ss[:, 0:1]),
                          reads=[xtb], writes=[jb_, ssb])
                    kb.op(act, lambda ss=ss: A_.activation(out=ss[:, 1:2], in_=ss[:, 0:1], func=AF.Sqrt,
                                                           scale=1.0 / D, bias=EPS), writes=[ssb])
                    kb.op(dve, lambda ss=ss: V.reciprocal(out=ss[:, 1:2], in_=ss[:, 1:2]), writes=[ssb])
                    kb.op(dve, lambda xt=xt, ss=ss: V.scalar_tensor_tensor(out=xt[:], in0=xt[:], scalar=ss[:, 1:2],
                                                                           in1=gf[:], op0=ALU.mult, op1=ALU.mult),
                          reads=[ssb, gfb], writes=[xtb])
                    kb.dma(sp, out[r:r + 128, :], xt[:], reads=[xtb])
                kb.barrier()

        plist = [phase1, phase2, phase3, phase3c, phase4, phase5, phase6, phase7, phase8, phase9, phase10]
        for i, ph in enumerate(plist):
            if i < phases:
                ph()
    return nc


def _host_params(inp):
    f = lambda a: np.ascontiguousarray(np.asarray(a, dtype=np.float32))
    fm = lambda v: f(np.asarray(v).reshape(KC, 128).T)
    gains = np.stack([fm(inp["norm_mix_g"][0]), fm(inp["norm_mlp_g"][0]), fm(inp["norm_ple_g"][0]),
                      fm(inp["ssd_norm_g"][0])], axis=1)
    cw = np.asarray(inp["conv_w"][0])
    cb = np.asarray(inp["conv_b"][0])
    cwb = np.concatenate([cw, cb[None, :]], axis=0)
    cwb = cwb.reshape(6, 48, 128).transpose(2, 1, 0)
    return {
        "w_in": f(inp["w_in"][0]),
        "w_ssd_up": f(inp["w_ssd_up"][0]),
        "pool_w": f(np.asarray(inp["pool_w"][0]).reshape(2048, 512)),
        "w_pool_up": f(inp["w_pool_up"][0]),
        "w_out": f(inp["w_out"][0]),
        "w_ff1": f(inp["w_ff1"][0]),
        "w_ff2": f(inp["w_ff2"][0]),
        "w_ple_gate": f(inp["w_ple_gate"][0]),
        "w_ple_proj": f(inp["w_ple_proj"][0]),
        "gains": f(gains),
        "g_final": f(np.asarray(inp["norm_final_g"]).reshape(1, D)),
        "conv_wb": f(cwb),
        "dtb": f(np.concatenate([np.asarray(inp["dt_bias_f"][0]), np.asarray(inp["dt_bias_b"][0])]).reshape(1, 128)),
        "alog": f(np.concatenate([np.asarray(inp["a_log_f"][0]), np.asarray(inp["a_log_b"][0])]).reshape(1, 128)),
        "dskip": f(np.asarray(inp["d_skip"][0]).reshape(1, NH)),
        "pscale": f(np.asarray(inp["pool_scale"][0]).reshape(16, 128).T),
    }


def kernel(**inp):
    xp = np.asarray(inp["x_prompt"])
    xs = np.asarray(inp["x_sample"])
    pp = np.asarray(inp["p_prompt"])[0]
    psm = np.asarray(inp["p_sample"])[0]
    seqs = [(xp[i], pp[i]) for i in range(xp.shape[0])] + [(xs[i], psm[i]) for i in range(xs.shape[0])]
    S = xp.shape[1]
    params = _host_params(inp)
    nc = build_program(S)
    core_of_seq = [0, 1, 2, 4, 5, 6]
    seq_of_core = {c: i for i, c in enumerate(core_of_seq)}
    zx = np.zeros((S, D), np.float32)
    zp = np.zeros((S, PLE), np.float32)
    in_maps = []
    for c in range(8):
        m = dict(params)
        if c in seq_of_core:
            xx, pq = seqs[seq_of_core[c]]
            m["x"] = np.ascontiguousarray(xx, dtype=np.float32)
            m["p"] = np.ascontiguousarray(pq, dtype=np.float32)
        else:
            m["x"] = zx
            m["p"] = zp
        in_maps.append(m)
    res = run_bass_kernel_spmd(nc, in_maps, core_ids=list(range(8)))
    outs = [np.asarray(res.results[core_of_seq[i]]["out"], dtype=np.float32) for i in range(len(seqs))]
    nb = xp.shape[0]
    y_prompt = np.stack(outs[:nb], axis=0)
    y_sample = np.stack(outs[nb:], axis=0)
    return (y_prompt, y_sample)
```

```python
import numpy as np
from contextlib import ExitStack
import concourse.bass as bass
import concourse.mybir as mybir
from concourse.bass_utils import run_bass_kernel_spmd

F32 = mybir.dt.float32
BF16 = mybir.dt.bfloat16
AF = mybir.ActivationFunctionType
ALU = mybir.AluOpType

D = 4096
NH = 64
HD = 64
NG = 8
XBC = 6144
POOLD = 2048
DFF = 16384
PLE = 256
IN_COLS = 20608
C_Z, C_XBC, C_DT, C_UP, C_GA, C_GB = 0, 4096, 10240, 10368, 12416, 16512
EPS = 1e-6
KC = D // 128
SEQ = 4096


class Buf:
    __slots__ = ("name", "lw", "rd", "lsem", "ssem")

    def __init__(self, name):
        self.name = name
        self.lw = None
        self.rd = {}
        self.lsem = None
        self.ssem = None


class Sem:
    __slots__ = ("h", "cnt", "idx")

    def __init__(self, h, idx):
        self.h = h
        self.cnt = 0
        self.idx = idx


class Eng:
    def __init__(self, h, sem):
        self.h = h
        self.sem = sem
        self.seen = {}


class KB:
    def __init__(self, nc, es, nsem=100):
        self.nc = nc
        self.sems = [Sem(es.enter_context(nc.semaphore(f"s{i}")), i) for i in range(nsem)]
        self.pe = Eng(nc.tensor, self.sems[0])
        self.act = Eng(nc.scalar, self.sems[1])
        self.dve = Eng(nc.vector, self.sems[2])
        self.pool = Eng(nc.gpsimd, self.sems[3])
        self.sp = Eng(nc.sync, None)
        self.engs = [self.pe, self.act, self.dve, self.pool, self.sp]
        self.free = list(self.sems[4:])
        self.used = []
        self.bufs = []

    def buf(self, name):
        b = Buf(name)
        self.bufs.append(b)
        return b

    def _dsem(self):
        s = self.free.pop()
        self.used.append(s)
        return s

    def _deps(self, reads, writes):
        deps = {}

        def add(s, v):
            if s.idx not in deps or deps[s.idx][1] < v:
                deps[s.idx] = (s, v)

        for b in reads:
            if b.lw is not None:
                add(*b.lw)
        for b in writes:
            if b.lw is not None:
                add(*b.lw)
            for s, v in b.rd.values():
                add(s, v)
        return deps

    def _wait(self, eng, deps):
        for s, v in deps.values():
            if eng.seen.get(s.idx, 0) < v:
                eng.h.wait_ge(s.h, v)
                eng.seen[s.idx] = v

    def _record(self, s, v, reads, writes):
        for b in writes:
            b.lw = (s, v)
            b.rd = {}
        for b in reads:
            if b.rd.get(s.idx, (None, 0))[1] < v:
                b.rd[s.idx] = (s, v)

    def op(self, eng, fn, reads=(), writes=()):
        self._wait(eng, self._deps(reads, writes))
        ins = fn()
        eng.sem.cnt += 1
        ins.then_inc(eng.sem.h, 1)
        self._record(eng.sem, eng.sem.cnt, reads, writes)

    def grp(self, eng, fns, reads=(), writes=()):
        self._wait(eng, self._deps(reads, writes))
        ins = None
        for fn in fns:
            ins = fn()
        eng.sem.cnt += 1
        ins.then_inc(eng.sem.h, 1)
        self._record(eng.sem, eng.sem.cnt, reads, writes)

    def dma(self, eng, out, in_, reads=(), writes=()):
        self._wait(eng, self._deps(reads, writes))
        ins = eng.h.dma_start(out=out, in_=in_)
        if writes:
            b = writes[0]
            if b.lsem is None:
                b.lsem = self._dsem()
            s = b.lsem
        else:
            b = reads[0]
            if b.ssem is None:
                b.ssem = self._dsem()
            s = b.ssem
        s.cnt += 16
        ins.then_inc(s.h, 16)
        self._record(s, s.cnt, reads, writes)

    def barrier(self):
        allsems = [e.sem for e in self.engs if e.sem is not None] + self.used
        for e in self.engs:
            for s in allsems:
                if s.cnt > 0 and e.seen.get(s.idx, 0) < s.cnt:
                    e.h.wait_ge(s.h, s.cnt)
                    e.seen[s.idx] = s.cnt
        for b in self.bufs:
            b.lw = None
            b.rd = {}
            b.lsem = None
            b.ssem = None
        self.bufs = []
        self.free.extend(self.used)
        self.used = []


def build_program(S, dbg=False, phases=99):
    nc = bass.Bass("TRN2", target_bir_lowering=False)
    NCH = S // 128
    skind = "ExternalOutput" if dbg else "Internal"

    def din(name, shape, dt=F32):
        return nc.dram_tensor(name, shape, dt, kind="ExternalInput").ap()

    def dscr(name, shape, dt):
        return nc.dram_tensor(name, shape, dt, kind=skind).ap()

    x = din("x", [S, D])
    pin = din("p", [S, PLE])
    w_in = din("w_in", [D, IN_COLS])
    w_ssd_up = din("w_ssd_up", [D, D])
    pool_w = din("pool_w", [2048, 512])
    w_pool_up = din("w_pool_up", [POOLD, D])
    w_out = din("w_out", [D, D])
    w_ff1 = din("w_ff1", [D, DFF])
    w_ff2 = din("w_ff2", [DFF, D])
    w_ple_gate = din("w_ple_gate", [D, D])
    w_ple_proj = din("w_ple_proj", [PLE, D])
    gains = din("gains", [128, 4, KC])
    g_final = din("g_final", [1, D])
    conv_wb = din("conv_wb", [128, 48, 6])
    dtb = din("dtb", [1, 128])
    alog = din("alog", [1, 128])
    dskip = din("dskip", [1, NH])
    pscale = din("pscale", [128, 16])
    out = nc.dram_tensor("out", [S, D], F32, kind="ExternalOutput").ap()

    z_s = dscr("z_s", [S, D], F32)
    xbcT_s = dscr("xbcT_s", [XBC, S], F32)
    dt_s = dscr("dt_s", [S, 128], F32)
    upT_s = dscr("upT_s", [POOLD, S], F32)
    gT_s = dscr("gT_s", [2 * D, S], BF16)
    xs_s = dscr("xs_s", [S, D], F32)
    btok_s = dscr("btok_s", [S, 1024], BF16)
    bT_s = dscr("bT_s", [NCH, 128, 8, 128], BF16)
    cT_s = dscr("cT_s", [NCH, 128, 8, 128], BF16)
    yd_s = [dscr("yf_s", [S, D], F32), dscr("yb_s", [S, D], F32)]
    yT_s = dscr("yT_s", [D, S], BF16)
    pbT_s = dscr("pbT_s", [POOLD, S], BF16)
    mixT_s = dscr("mixT_s", [D, S], BF16)
    x1_s = dscr("x1_s", [S, D], F32)
    uT_s = dscr("uT_s", [DFF, S], BF16)
    x2_s = dscr("x2_s", [S, D], F32)
    x3_s = dscr("x3_s", [S, D], F32)

    with ExitStack() as es:
        kb = KB(nc, es)
        pe, act, dve, pool, sp = kb.pe, kb.act, kb.dve, kb.pool, kb.sp
        V, A_, G, T_ = nc.vector, nc.scalar, nc.gpsimd, nc.tensor

        def sb(st, name, shape, dt):
            return st.enter_context(nc.sbuf_tensor(name, shape, dt))

        ps = [es.enter_context(nc.psum_tensor(f"ps{i}", [128, 512], F32)) for i in range(8)]
        psb = [Buf(f"ps{i}") for i in range(8)]
        pctr = [0]

        def nextbank():
            i = pctr[0] % 8
            pctr[0] += 1
            return ps[i], psb[i]

        ident_b = sb(es, "ident_b", [128, 128], BF16)
        ident_f = sb(es, "ident_f", [128, 128], F32)
        triU = sb(es, "triU", [128, 128], F32)
        triLs = sb(es, "triLs", [128, 128], F32)
        triL = sb(es, "triL", [128, 128], F32)
        triUs = sb(es, "triUs", [128, 128], F32)
        ones = sb(es, "ones", [128, 128], F32)
        gains_t = sb(es, "gains_t", [128, 4, KC], F32)
        cwb_t = sb(es, "cwb_t", [128, 48, 6], F32)
        pscale_t = sb(es, "pscale_t", [128, 16], F32)

        def mk(t, cmp, sgn=1):
            b = kb.buf("c")
            kb.op(pool, lambda: G.memset(t[:], 1.0), writes=[b])
            if cmp is not None:
                kb.op(pool, lambda: G.affine_select(out=t[:], in_=t[:], pattern=[[-sgn, 128]], compare_op=cmp,
                                                    fill=0.0, base=0, channel_multiplier=sgn), writes=[b])

        mk(ident_b, ALU.is_equal)
        mk(ident_f, ALU.is_equal)
        mk(triU, ALU.is_ge, -1)
        mk(triLs, ALU.is_gt, 1)
        mk(triL, ALU.is_ge, 1)
        mk(triUs, ALU.is_gt, -1)
        mk(ones, None)
        kb.dma(sp, gains_t[:], gains, writes=[kb.buf("c1")])
        kb.dma(sp, cwb_t[:], conv_wb, writes=[kb.buf("c2")])
        kb.dma(sp, pscale_t[:], pscale, writes=[kb.buf("c3")])
        kb.barrier()

        class WStream:
            def __init__(self, st, name, kc, ncols, nbuf, blocks):
                self.t = [sb(st, f"{name}{i}", [128, kc, ncols], BF16) for i in range(nbuf)]
                self.b = [kb.buf(f"{name}{i}") for i in range(nbuf)]
                self.blocks = blocks
                self.nbuf = nbuf
                self.issued = 0

            def prefetch(self, upto):
                while self.issued < min(upto, len(self.blocks)):
                    i = self.issued
                    src = self.blocks[i]
                    dst = self.t[i % self.nbuf]
                    kcn, ncn = src.shape[1], src.shape[2]
                    kb.dma(pool, dst[:, 0:kcn, 0:ncn], src, writes=[self.b[i % self.nbuf]])
                    self.issued += 1

            def get(self, i):
                self.prefetch(i + self.nbuf)
                return self.t[i % self.nbuf], self.b[i % self.nbuf]

        def wblk(w, r0, nk, c0, ncols):
            return w[r0:r0 + nk * 128, c0:c0 + ncols].rearrange("(kc p) n -> p kc n", p=128)

        class Ring:
            def __init__(self, st, name, shape, dt, n):
                self.t = [sb(st, f"{name}{i}", shape, dt) for i in range(n)]
                self.b = [kb.buf(f"{name}{i}") for i in range(n)]
                self.i = 0
                self.n = n

            def next(self):
                i = self.i % self.n
                self.i += 1
                return self.t[i], self.b[i]

        class NormT:
            def __init__(self, st, pref, nsub, nxl=2):
                self.xl = Ring(st, pref + "xl", [128, D], F32, nxl)
                self.xn = Ring(st, pref + "xn", [128, D], BF16, 1)
                self.ss = sb(st, pref + "ss", [128, 2 * nsub], F32)
                self.ssb = [kb.buf(f"ss{i}") for i in range(nsub)]
                self.nsub = nsub

            def run(self, src, row0, gidx, hT, hTb):
                nsub, ss = self.nsub, self.ss
                for ts in range(nsub):
                    r = row0 + ts * 128
                    xt, xtb = self.xl.next()
                    xo, xob = self.xn.next()
                    sb_ = self.ssb[ts]
                    kb.dma(sp, xt[:], src[r:r + 128, :], writes=[xtb])
                    kb.op(act, lambda xt=xt, xo=xo, ts=ts: A_.activation(out=xo[:], in_=xt[:], func=AF.Square,
                                                                         accum_out=ss[:, ts:ts + 1]),
                          reads=[xtb], writes=[xob, sb_])
                    kb.op(act, lambda ts=ts: A_.activation(out=ss[:, nsub + ts:nsub + ts + 1], in_=ss[:, ts:ts + 1],
                                                           func=AF.Sqrt, scale=1.0 / D, bias=EPS),
                          reads=[sb_], writes=[sb_])
                    kb.op(dve, lambda ts=ts: V.reciprocal(out=ss[:, nsub + ts:nsub + ts + 1],
                                                          in_=ss[:, nsub + ts:nsub + ts + 1]),
                          reads=[sb_], writes=[sb_])
                    kb.op(dve, lambda xt=xt, xo=xo, ts=ts: V.tensor_scalar(
                        out=xo[:], in0=xt[:], scalar1=ss[:, nsub + ts:nsub + ts + 1], scalar2=None,
                        op0=ALU.mult), reads=[xtb, sb_], writes=[xob])
                    for q in range(4):
                        pt, ptb = nextbank()
                        ptv = pt[:].bitcast(BF16)
                        fns = []
                        for k in range(8):
                            kc = q * 8 + k
                            fns.append(lambda k=k, kc=kc, ptv=ptv, xo=xo: T_.transpose(
                                out=ptv[:, k * 128:(k + 1) * 128], in_=xo[:, kc * 128:(kc + 1) * 128],
                                identity=ident_b[:]))
                        kb.grp(pe, fns, reads=[xob], writes=[ptb])
                        kb.op(dve, lambda q=q, ptv=ptv, ts=ts: V.tensor_tensor(
                            out=hT[:, q * 8:(q + 1) * 8, ts * 128:(ts + 1) * 128],
                            in0=ptv.rearrange("p (k t) -> p k t", k=8),
                            in1=gains_t[:, gidx, q * 8:(q + 1) * 8].unsqueeze(2).to_broadcast([128, 8, 128]),
                            op=ALU.mult), reads=[ptb], writes=[hTb])

        def mm_ws(wt, wtb, sub, klist, actT, actTb, tok0, ntok, extra_reads=()):
            pt, ptb = nextbank()
            fns = []
            nk = len(klist)
            for i, (wk, ak) in enumerate(klist):
                fns.append(lambda i=i, wk=wk, ak=ak, pt=pt: T_.matmul(
                    pt[:, 0:ntok], lhsT=wt[:, wk, sub * 128:(sub + 1) * 128], rhs=actT[:, ak, tok0:tok0 + ntok],
                    start=(i == 0), stop=(i == nk - 1)))
            kb.grp(pe, fns, reads=[wtb, actTb] + list(extra_reads), writes=[ptb])
            return pt, ptb

        def mm_as(wt, wtb, ncols, nk, actT, actTb, ts):
            pt, ptb = nextbank()
            fns = []
            for kc in range(nk):
                fns.append(lambda kc=kc, pt=pt: T_.matmul(
                    pt[:, 0:ncols], lhsT=actT[:, kc, ts * 128:(ts + 1) * 128], rhs=wt[:, kc, 0:ncols],
                    start=(kc == 0), stop=(kc == nk - 1)))
            kb.grp(pe, fns, reads=[wtb, actTb], writes=[ptb])
            return pt, ptb

        klist32 = [(k, k) for k in range(KC)]

        def phase1():
            T = min(1024, S)
            nsub = T // 128
            nth = T // 512
            with ExitStack() as st:
                hT = sb(st, "p1hT", [128, KC, T], BF16)
                hTb = kb.buf("hT")
                nt = NormT(st, "p1", nsub, 2)
                stg = Ring(st, "p1stg", [128, 512], F32, 6)
                stgb = Ring(st, "p1stgb", [128, 512], BF16, 4)
                sig = Ring(st, "p1sig", [128, 512], F32, 2)
                blocks, kinds = [], []
                for j in range(8):
                    blocks.append(wblk(w_in, 0, KC, C_Z + j * 512, 512)); kinds.append(("z", j))
                for j in range(12):
                    blocks.append(wblk(w_in, 0, KC, C_XBC + j * 512, 512)); kinds.append(("xbc", j))
                blocks.append(wblk(w_in, 0, KC, C_DT, 128)); kinds.append(("dt", 0))
                for j in range(4):
                    blocks.append(wblk(w_in, 0, KC, C_UP + j * 512, 512)); kinds.append(("up", j))
                for j in range(16):
                    blocks.append(wblk(w_in, 0, KC, C_GA + j * 512, 512)); kinds.append(("g", j))
                nblk = len(blocks)
                ntile = S // T
                ws = WStream(st, "p1w", KC, 512, 2, blocks * ntile)
                for tt in range(ntile):
                    t0 = tt * T
                    ws.prefetch(tt * nblk + 2)
                    nt.run(x, t0, 0, hT, hTb)
                    for bi in range(nblk):
                        wt, wtb = ws.get(tt * nblk + bi)
                        kind, j = kinds[bi]
                        if kind == "z":
                            for ts in range(nsub):
                                pt, ptb = mm_as(wt, wtb, 512, KC, hT, hTb, ts)
                                sg, sgb = sig.next()
                                kb.op(act, lambda pt=pt, sg=sg: A_.activation(out=sg[:], in_=pt[:], func=AF.Sigmoid),
                                      reads=[ptb], writes=[sgb])
                                so, sob = stg.next()
                                kb.op(dve, lambda pt=pt, sg=sg, so=so: V.tensor_tensor(out=so[:], in0=pt[:], in1=sg[:],
                                                                                       op=ALU.mult),
                                      reads=[ptb, sgb], writes=[sob])
                                r = t0 + ts * 128
                                kb.dma(sp, z_s[r:r + 128, j * 512:(j + 1) * 512], so[:], reads=[sob])
                        elif kind == "dt":
                            for ts in range(nsub):
                                pt, ptb = mm_as(wt, wtb, 128, KC, hT, hTb, ts)
                                so, sob = stg.next()
                                kb.op(dve, lambda pt=pt, so=so: V.tensor_copy(out=so[:, 0:128], in_=pt[:, 0:128]),
                                      reads=[ptb], writes=[sob])
                                r = t0 + ts * 128
                                kb.dma(sp, dt_s[r:r + 128, :], so[:, 0:128], reads=[sob])
                        else:
                            for sub in range(4):
                                for th in range(nth):
                                    pt, ptb = mm_ws(wt, wtb, sub, klist32, hT, hTb, th * 512, 512)
                                    tk = t0 + th * 512
                                    f0 = j * 512 + sub * 128
                                    if kind == "g":
                                        so, sob = stgb.next()
                                        kb.op(act, lambda pt=pt, so=so: A_.activation(out=so[:], in_=pt[:],
                                                                                      func=AF.Sigmoid),
                                              reads=[ptb], writes=[sob])
                                        kb.dma(sp, gT_s[f0:f0 + 128, tk:tk + 512], so[:], reads=[sob])
                                    else:
                                        so, sob = stg.next()
                                        kb.op(dve, lambda pt=pt, so=so: V.tensor_copy(out=so[:], in_=pt[:]),
                                              reads=[ptb], writes=[sob])
                                        dst = xbcT_s if kind == "xbc" else upT_s
                                        kb.dma(sp, dst[f0:f0 + 128, tk:tk + 512], so[:], reads=[sob])
                kb.barrier()

        def phase2():
            T2 = 512
            with ExitStack() as st:
                xin = Ring(st, "p2xin", [128, 12, T2 + 4], F32, 2)
                acc = Ring(st, "p2acc", [128, 12, T2], F32, 2)
                stg = Ring(st, "p2stg", [128, 512], F32, 6)
                stgb = Ring(st, "p2stgb", [128, 512], BF16, 4)
                stgT = Ring(st, "p2stgT", [128, 512], BF16, 4)
                ptmp = Ring(st, "p2ptmp", [128, 512], F32, 3)
                def p2load(tt, grp):
                    t0 = tt * T2
                    xi, xib = xin.next()
                    lo = max(t0 - 2, 0)
                    hi = min(t0 + T2 + 2, S)
                    if t0 == 0:
                        kb.op(pool, lambda xi=xi: G.memset(xi[:, :, 0:2], 0.0), writes=[xib])
                    if t0 + T2 == S:
                        kb.op(pool, lambda xi=xi: G.memset(xi[:, :, T2 + 2:T2 + 4], 0.0), writes=[xib])
                    src = xbcT_s[grp * 1536:(grp + 1) * 1536, lo:hi].rearrange("(c p) t -> p c t", p=128)
                    kb.dma(sp, xi[:, :, lo - (t0 - 2):hi - (t0 - 2)], src, writes=[xib])
                    return xi, xib

                units = [(tt, grp) for tt in range(S // T2) for grp in range(4)]
                loaded = {0: p2load(*units[0])}
                for ui, (tt, grp) in enumerate(units):
                    if True:
                        t0 = tt * T2
                        if ui + 1 < len(units):
                            loaded[ui + 1] = p2load(*units[ui + 1])
                        xi, xib = loaded.pop(ui)
                        ac, acb = acc.next()
                        for k in range(5):
                            for j in range(12):
                                cj = grp * 12 + j
                                useP = False
                                e = pool if useP else dve
                                E_ = G if useP else V
                                if useP and k > 0:
                                    ptm, ptmb = ptmp.next()
                                    kb.op(pool, lambda j=j, cj=cj, k=k, xi=xi, ptm=ptm: G.tensor_scalar(
                                        out=ptm[:], in0=xi[:, j, k:k + T2], scalar1=cwb_t[:, cj, k:k + 1], scalar2=None,
                                        op0=ALU.mult), reads=[xib], writes=[ptmb])
                                    kb.op(pool, lambda j=j, ac=ac, ptm=ptm: G.tensor_tensor(
                                        out=ac[:, j, :], in0=ac[:, j, :], in1=ptm[:], op=ALU.add),
                                          reads=[ptmb], writes=[acb])
                                    continue
                                if k == 0:
                                    kb.op(act, lambda j=j, cj=cj, xi=xi, ac=ac: A_.activation(
                                        out=ac[:, j, :], in_=xi[:, j, 0:T2], func=AF.Identity,
                                        scale=cwb_t[:, cj, 0:1], bias=cwb_t[:, cj, 5:6]), reads=[xib], writes=[acb])
                                    continue
                                else:
                                    fn = lambda E_=E_, j=j, cj=cj, k=k, xi=xi, ac=ac: E_.scalar_tensor_tensor(
                                        out=ac[:, j, :], in0=xi[:, j, k:k + T2], scalar=cwb_t[:, cj, k:k + 1],
                                        in1=ac[:, j, :], op0=ALU.mult, op1=ALU.add)
                                kb.op(e, fn, reads=[xib], writes=[acb])
                        kb.op(act, lambda ac=ac: A_.activation(out=ac[:], in_=ac[:], func=AF.Silu),
                              reads=[], writes=[acb])
                        for q in range(3):
                            cj0 = grp * 12 + q * 4
                            if cj0 < 40:
                                for tc in range(T2 // 128):
                                    pt, ptb = nextbank()
                                    fns = []
                                    for k in range(4):
                                        fns.append(lambda k=k, pt=pt, ac=ac, q=q, tc=tc: T_.matmul(
                                            pt[:, k * 128:(k + 1) * 128], lhsT=ac[:, q * 4 + k, tc * 128:(tc + 1) * 128],
                                            rhs=ident_f[:], start=True, stop=True))
                                    kb.grp(pe, fns, reads=[acb], writes=[ptb])
                                    r = t0 + tc * 128
                                    if cj0 < 32:
                                        so, sob = stg.next()
                                        kb.op(act, lambda pt=pt, so=so: A_.copy(out=so[:], in_=pt[:]),
                                              reads=[ptb], writes=[sob])
                                        kb.dma(sp, xs_s[r:r + 128, cj0 * 128:cj0 * 128 + 512], so[:], reads=[sob])
                                    else:
                                        so, sob = stgb.next()
                                        kb.op(act, lambda pt=pt, so=so: A_.copy(out=so[:], in_=pt[:]),
                                              reads=[ptb], writes=[sob])
                                        c0 = (cj0 - 32) * 128
                                        kb.dma(sp, btok_s[r:r + 128, c0:c0 + 512], so[:], reads=[sob])
                            if cj0 >= 32:
                                for k in range(4):
                                    cj = cj0 + k
                                    so, sob = stgT.next()
                                    kb.op(dve, lambda so=so, ac=ac, q=q, k=k: V.tensor_copy(out=so[:],
                                                                                            in_=ac[:, q * 4 + k, :]),
                                          reads=[acb], writes=[sob])
                                    dstT = bT_s if cj < 40 else cT_s
                                    g = (cj - 32) % 8
                                    c0 = t0 // 128
                                    dst = dstT[c0:c0 + T2 // 128, :, g, :].rearrange("c p t -> p c t")
                                    kb.dma(sp, dst, so[:].rearrange("p (c t) -> p c t", t=128), reads=[sob])
                kb.barrier()

        def phase3():
            with ExitStack() as st:
                dtd = sb(st, "dtd", [128, NCH, 64], F32)
                a_d = sb(st, "a_d", [128, NCH, 64], F32)
                bias_t = sb(st, "bias_t", [128, 128], F32)
                A_t = sb(st, "A_t", [128, 128], F32)
                b_dt, b_a, b_bias, b_A = (kb.buf(n) for n in ("dt", "a", "bias", "A"))
                kb.dma(sp, bias_t[:], dtb.partition_broadcast(128).rearrange("p o h -> p (o h)"), writes=[b_bias])
                kb.dma(sp, A_t[:], alog.partition_broadcast(128).rearrange("p o h -> p (o h)"), writes=[b_A])
                kb.op(act, lambda: A_.activation(out=A_t[:], in_=A_t[:], func=AF.Exp), writes=[b_A])
                kb.op(dve, lambda: V.tensor_scalar(out=A_t[:], in0=A_t[:], scalar1=-1.0, scalar2=None, op0=ALU.mult),
                      writes=[b_A])

                Er = Ring(st, "p3E", [128, 192], F32, 3)
                ddr = Ring(st, "p3dd", [128, 64], F32, 3)
                aT = Ring(st, "p3aT", [128, 16, 128], F32, 2)
                xsr = Ring(st, "p3xs", [128, D], F32, 2)
                btr = Ring(st, "p3bt", [128, 1024], BF16, 2)
                bTr = Ring(st, "p3bT", [128, 8, 128], BF16, 2)
                cTr = Ring(st, "p3cT", [128, 8, 128], BF16, 2)
                cbm = Ring(st, "p3cbm", [128, 8, 128], BF16, 2)
                dec = Ring(st, "p3dec", [128, 512], F32, 6)
                MT = Ring(st, "p3MT", [128, 8, 128], BF16, 4)
                xdt = Ring(st, "p3xdt", [128, D], BF16, 2)
                xdte = Ring(st, "p3xdte", [128, D], BF16, 2)
                H = sb(st, "p3H", [128, 8, 512], F32)
                Hbf = sb(st, "p3Hbf", [128, 8, 512], BF16)
                Hb = [kb.buf(f"H{g}") for g in range(8)]
                Hbfb = [kb.buf(f"Hbf{g}") for g in range(8)]
                ystg = Ring(st, "p3y", [128, 512], F32, 4)
                tmp = Ring(st, "p3tmp", [128, 512], F32, 3)

                for d in (1, 0):
                    hs = slice(64 * d, 64 * d + 64)
                    tri_incl = triU if d == 0 else triL
                    tri_strict = triLs if d == 0 else triUs
                    ll_ = xsr.t[0][:, 0:NCH * 64].rearrange("p (c h) -> p c h", h=64)
                    b_l = xsr.b[0]
                    kb.dma(sp, dtd[:], dt_s[:, hs].rearrange("(c p) h -> p c h", p=128), writes=[b_dt])
                    bb = bias_t[:, hs].unsqueeze(1).to_broadcast([128, NCH, 64])
                    kb.op(dve, lambda bb=bb: V.tensor_tensor(out=a_d[:], in0=dtd[:], in1=bb, op=ALU.add),
                          reads=[b_dt, b_bias], writes=[b_a])
                    kb.op(act, lambda ll_=ll_: A_.activation(out=ll_, in_=a_d[:], func=AF.Abs),
                          reads=[b_a], writes=[b_l])
                    kb.op(act, lambda ll_=ll_: A_.activation(out=ll_, in_=ll_, func=AF.Exp, scale=-1.0), writes=[b_l])
                    kb.op(act, lambda ll_=ll_: A_.activation(out=ll_, in_=ll_, func=AF.Ln, bias=1.0), writes=[b_l])
                    kb.op(dve, lambda ll_=ll_: V.scalar_tensor_tensor(out=dtd[:], in0=a_d[:], scalar=0.0, in1=ll_,
                                                                      op0=ALU.max, op1=ALU.add),
                          reads=[b_a, b_l], writes=[b_dt])
                    ab = A_t[:, hs].unsqueeze(1).to_broadcast([128, NCH, 64])
                    kb.op(dve, lambda ab=ab: V.tensor_tensor(out=a_d[:], in0=dtd[:], in1=ab, op=ALU.mult),
                          reads=[b_dt, b_A], writes=[b_a])
                    for g in range(8):
                        kb.op(dve, lambda g=g: V.memset(H[:, g, :], 0.0), writes=[Hb[g]])
                        kb.op(dve, lambda g=g: V.memset(Hbf[:, g, :], 0.0), writes=[Hbfb[g]])
                    order = list(range(NCH)) if d == 0 else list(range(NCH - 1, -1, -1))

                    def prologue(c):
                        P = {"c": c}
                        r = c * 128
                        xs, xsb = xsr.next()
                        bt, btb = btr.next()
                        bTt, bTb = bTr.next()
                        cTt, cTb = cTr.next()
                        kb.dma(sp, xs[:], xs_s[r:r + 128, :], writes=[xsb])
                        kb.dma(sp, bt[:], btok_s[r:r + 128, :], writes=[btb])
                        kb.dma(sp, bTt[:], bT_s[c], writes=[bTb])
                        kb.dma(sp, cTt[:], cT_s[c], writes=[cTb])
                        E, Eb = Er.next()
                        dd, ddb = ddr.next()
                        pt, ptb = nextbank()
                        fns = []
                        for i, lt in enumerate((tri_incl, tri_strict, ones)):
                            fns.append(lambda i=i, lt=lt, pt=pt, c=c: T_.matmul(
                                pt[:, i * 64:(i + 1) * 64], lhsT=lt[:], rhs=a_d[:, c, :], start=True, stop=True))
                        kb.grp(pe, fns, reads=[b_a], writes=[ptb])
                        kb.op(act, lambda pt=pt, E=E: A_.activation(out=E[:], in_=pt[:, 0:192], func=AF.Exp),
                              reads=[ptb], writes=[Eb])
                        kb.op(dve, lambda dd=dd, E=E, c=c: V.tensor_tensor(out=dd[:], in0=dtd[:, c, :], in1=E[:, 64:128],
                                                                           op=ALU.mult),
                              reads=[b_dt, Eb], writes=[ddb])
                        xd, xdb = xdt.next()
                        xe, xeb = xdte.next()
                        kb.op(pool, lambda xs=xs, xd=xd, c=c: G.tensor_tensor(
                            out=xd[:].rearrange("p (h e) -> p h e", e=64), in0=xs[:].rearrange("p (h e) -> p h e", e=64),
                            in1=dtd[:, c, :].unsqueeze(2).to_broadcast([128, 64, 64]), op=ALU.mult),
                              reads=[xsb, b_dt], writes=[xdb])
                        kb.op(dve, lambda xs=xs, xe=xe, dd=dd: V.tensor_tensor(
                            out=xe[:].rearrange("p (h e) -> p h e", e=64), in0=xs[:].rearrange("p (h e) -> p h e", e=64),
                            in1=dd[:].unsqueeze(2).to_broadcast([128, 64, 64]), op=ALU.mult),
                              reads=[xsb, ddb], writes=[xeb])
                        cm, cmb = cbm.next()
                        for half in range(2):
                            pt, ptb = nextbank()
                            fns = []
                            for k in range(4):
                                g = half * 4 + k
                                fns.append(lambda k=k, g=g, pt=pt, bTt=bTt, cTt=cTt: T_.matmul(
                                    pt[:, k * 128:(k + 1) * 128], lhsT=bTt[:, g, :], rhs=cTt[:, g, :],
                                    start=True, stop=True))
                            kb.grp(pe, fns, reads=[bTb, cTb], writes=[ptb])
                            kb.op(dve, lambda pt=pt, cm=cm, half=half: V.tensor_tensor(
                                out=cm[:, half * 4:(half + 1) * 4, :], in0=pt[:].rearrange("p (g l) -> p g l", g=4),
                                in1=tri_incl[:].unsqueeze(1).to_broadcast([128, 4, 128]), op=ALU.mult),
                                  reads=[ptb], writes=[cmb])
                        P.update(xs=xs, xsb=xsb, bt=bt, btb=btb, cTt=cTt, cTb=cTb, E=E, Eb=Eb, xd=xd, xdb=xdb,
                                 xe=xe, xeb=xeb, cm=cm, cmb=cmb, at=None, atb=None, mt={}, mtb={})
                        return P

                    def stageA(P, g):
                        c = P["c"]
                        if g % 2 == 0:
                            at, atb = aT.next()
                            h0 = g * 8
                            kb.op(pool, lambda at=at, c=c, h0=h0: G.tensor_tensor(
                                out=at[:], in0=a_d[:, c, h0:h0 + 16].unsqueeze(2).to_broadcast([128, 16, 128]),
                                in1=tri_incl[:].unsqueeze(1).to_broadcast([128, 16, 128]), op=ALU.mult),
                                  reads=[b_a], writes=[atb])
                            P["at"], P["atb"] = at, atb
                        at, atb = P["at"], P["atb"]
                        cm, cmb = P["cm"], P["cmb"]
                        mt, mtb = MT.next()
                        for half in range(2):
                            pt, ptb = nextbank()
                            hh = (g % 2) * 8 + half * 4
                            kb.grp(pe, [lambda pt=pt, at=at, hh=hh: T_.matmul(
                                pt[:], lhsT=tri_strict[:], rhs=at[:, hh:hh + 4, :], start=True, stop=True)],
                                   reads=[atb], writes=[ptb])
                            dc, dcb = dec.next()
                            kb.op(act, lambda pt=pt, dc=dc: A_.activation(out=dc[:], in_=pt[:], func=AF.Exp),
                                  reads=[ptb], writes=[dcb])
                            kb.op(dve, lambda dc=dc, mt=mt, cm=cm, g=g, half=half: V.tensor_tensor(
                                out=mt[:, half * 4:(half + 1) * 4, :], in0=dc[:].rearrange("p (h l) -> p h l", h=4),
                                in1=cm[:, g, :].unsqueeze(1).to_broadcast([128, 4, 128]), op=ALU.mult),
                                  reads=[dcb, cmb], writes=[mtb])
                        P["mt"][g], P["mtb"][g] = mt, mtb

                    def stageB(P, g):
                        c = P["c"]
                        r = c * 128
                        mt, mtb = P["mt"][g], P["mtb"][g]
                        xd, xdb, xe, xeb = P["xd"], P["xdb"], P["xe"], P["xeb"]
                        cTt, cTb, bt, btb, E, Eb = P["cTt"], P["cTb"], P["bt"], P["btb"], P["E"], P["Eb"]
                        py, pyb = nextbank()
                        fns = []
                        for hl in range(8):
                            col = g * 512 + hl * 64
                            fns.append(lambda hl=hl, col=col, py=py, mt=mt, xd=xd: T_.matmul(
                                py[:, hl * 64:(hl + 1) * 64], lhsT=mt[:, hl, :], rhs=xd[:, col:col + 64],
                                start=True, stop=True))
                        kb.grp(pe, fns, reads=[mtb, xdb], writes=[pyb])
                        po, pob = nextbank()
                        kb.grp(pe, [lambda po=po, cTt=cTt, g=g: T_.matmul(po[:], lhsT=cTt[:, g, :], rhs=Hbf[:, g, :],
                                                                           start=True, stop=True)],
                               reads=[cTb, Hbfb[g]], writes=[pob])
                        pst, pstb = nextbank()
                        kb.grp(pe, [lambda pst=pst, bt=bt, xe=xe, g=g: T_.matmul(
                            pst[:], lhsT=bt[:, g * 128:(g + 1) * 128], rhs=xe[:, g * 512:(g + 1) * 512],
                            start=True, stop=True)], reads=[btb, xeb], writes=[pstb])
                        tm, tmb = tmp.next()
                        kb.op(dve, lambda po=po, tm=tm, E=E, g=g: V.tensor_tensor(
                            out=tm[:].rearrange("p (h e) -> p h e", e=64), in0=po[:].rearrange("p (h e) -> p h e", e=64),
                            in1=E[:, g * 8:(g + 1) * 8].unsqueeze(2).to_broadcast([128, 8, 64]), op=ALU.mult),
                              reads=[pob, Eb], writes=[tmb])
                        yo, yob = ystg.next()
                        kb.op(dve, lambda py=py, tm=tm, yo=yo: V.tensor_tensor(out=yo[:], in0=py[:], in1=tm[:],
                                                                               op=ALU.add),
                              reads=[pyb, tmb], writes=[yob])
                        kb.dma(sp, yd_s[d][r:r + 128, g * 512:(g + 1) * 512], yo[:], reads=[yob])
                        kb.op(pool, lambda E=E, g=g: G.tensor_tensor(
                            out=H[:, g, :].rearrange("p (h e) -> p h e", e=64),
                            in0=H[:, g, :].rearrange("p (h e) -> p h e", e=64),
                            in1=E[:, 128 + g * 8:128 + (g + 1) * 8].unsqueeze(2).to_broadcast([128, 8, 64]),
                            op=ALU.mult), reads=[Eb], writes=[Hb[g]])
                        kb.op(dve, lambda pst=pst, g=g: V.tensor_tensor(out=H[:, g, :], in0=H[:, g, :], in1=pst[:],
                                                                        op=ALU.add),
                              reads=[pstb], writes=[Hb[g]])
                        kb.op(act, lambda g=g: A_.copy(out=Hbf[:, g, :], in_=H[:, g, :]),
                              reads=[Hb[g]], writes=[Hbfb[g]])

                    steps = [(ci, g) for ci in range(len(order)) for g in range(8)]
                    Ps = {0: prologue(order[0])}
                    stageA(Ps[0], 0)
                    stageA(Ps[0], 1)
                    for k, (ci, g) in enumerate(steps):
                        if g == 2 and ci + 1 < len(order):
                            Ps[ci + 1] = prologue(order[ci + 1])
                        stageB(Ps[ci], g)
                        if k + 2 < len(steps):
                            ci2, g2 = steps[k + 2]
                            stageA(Ps[ci2], g2)
                        if g == 7:
                            del Ps[ci]
                kb.barrier()

        def phase3c():
            PW = 1024
            with ExitStack() as st:
                dsk = sb(st, "dsk", [128, NH], F32)
                b_dsk = kb.buf("dsk")
                kb.dma(sp, dsk[:], dskip.partition_broadcast(128).rearrange("p o h -> p (o h)"), writes=[b_dsk])
                yfr = Ring(st, "c_yf", [128, PW], F32, 2)
                ybr = Ring(st, "c_yb", [128, PW], F32, 2)
                xsr = Ring(st, "c_xs", [128, PW], F32, 2)
                zsr = Ring(st, "c_zs", [128, PW], F32, 2)
                junk = Ring(st, "c_junk", [128, 512], BF16, 2)
                ssr = Ring(st, "c_ss", [128, 4], F32, 4)
                ynr = Ring(st, "c_yn", [128, PW], BF16, 2)
                yTst = Ring(st, "c_yT", [128, KC, 512], BF16, 2)
                yts, ytb = None, None
                for c in range(NCH):
                    r = c * 128
                    if c % 4 == 0:
                        yts, ytb = yTst.next()
                    for pc in range(D // PW):
                        cs = slice(pc * PW, (pc + 1) * PW)
                        yf, yfb = yfr.next()
                        yb, ybb = ybr.next()
                        xs, xsb = xsr.next()
                        zs, zsb = zsr.next()
                        kb.dma(sp, yf[:], yd_s[0][r:r + 128, cs], writes=[yfb])
                        kb.dma(sp, yb[:], yd_s[1][r:r + 128, cs], writes=[ybb])
                        kb.dma(sp, xs[:], xs_s[r:r + 128, cs], writes=[xsb])
                        kb.dma(sp, zs[:], z_s[r:r + 128, cs], writes=[zsb])
                        nh = PW // 64
                        kb.op(pool, lambda yf=yf, yb=yb: G.tensor_tensor(out=yf[:], in0=yf[:], in1=yb[:], op=ALU.add),
                              reads=[ybb], writes=[yfb])
                        kb.op(pool, lambda xs=xs, pc=pc: G.tensor_tensor(
                            out=xs[:].rearrange("p (h e) -> p h e", e=64), in0=xs[:].rearrange("p (h e) -> p h e", e=64),
                            in1=dsk[:, pc * nh:(pc + 1) * nh].unsqueeze(2).to_broadcast([128, nh, 64]), op=ALU.mult),
                              reads=[b_dsk], writes=[xsb])
                        kb.op(dve, lambda yf=yf, xs=xs: V.tensor_tensor(out=yf[:], in0=yf[:], in1=xs[:], op=ALU.add),
                              reads=[xsb], writes=[yfb])
                        kb.op(dve, lambda yf=yf, zs=zs: V.tensor_tensor(out=yf[:], in0=yf[:], in1=zs[:], op=ALU.mult),
                              reads=[zsb], writes=[yfb])
                        ss, ssb = ssr.next()
                        for gq in range(PW // 512):
                            jk, jkb = junk.next()
                            kb.op(act, lambda yf=yf, jk=jk, ss=ss, gq=gq: A_.activation(
                                out=jk[:], in_=yf[:, gq * 512:(gq + 1) * 512], func=AF.Square,
                                accum_out=ss[:, gq:gq + 1]), reads=[yfb], writes=[jkb, ssb])
                        kb.op(act, lambda ss=ss: A_.activation(out=ss[:, 2:4], in_=ss[:, 0:2], func=AF.Sqrt,
                                                               scale=1.0 / 512, bias=EPS), writes=[ssb])
                        kb.op(dve, lambda ss=ss: V.reciprocal(out=ss[:, 2:4], in_=ss[:, 2:4]), writes=[ssb])
                        yn, ynb = ynr.next()
                        kb.op(dve, lambda yf=yf, yn=yn, ss=ss: V.tensor_tensor(
                            out=yn[:].rearrange("p (g e) -> p g e", e=512), in0=yf[:].rearrange("p (g e) -> p g e", e=512),
                            in1=ss[:, 2:4].unsqueeze(2).to_broadcast([128, 2, 512]),
                            op=ALU.mult), reads=[yfb, ssb], writes=[ynb])
                        pt, ptb = nextbank()
                        ptv = pt[:].bitcast(BF16)
                        fns = []
                        for k in range(8):
                            fns.append(lambda k=k, ptv=ptv, yn=yn: T_.transpose(
                                out=ptv[:, k * 128:(k + 1) * 128], in_=yn[:, k * 128:(k + 1) * 128], identity=ident_b[:]))
                        kb.grp(pe, fns, reads=[ynb], writes=[ptb])
                        kb.op(dve, lambda ptv=ptv, yts=yts, pc=pc, c=c: V.tensor_tensor(
                            out=yts[:, pc * 8:(pc + 1) * 8, (c % 4) * 128:(c % 4 + 1) * 128],
                            in0=ptv.rearrange("p (k t) -> p k t", k=8),
                            in1=gains_t[:, 3, pc * 8:(pc + 1) * 8].unsqueeze(2).to_broadcast([128, 8, 128]),
                            op=ALU.mult), reads=[ptb], writes=[ytb])
                    if c % 4 == 3:
                        t0 = (c - 3) * 128
                        kb.dma(sp, yT_s[:, t0:t0 + 512].rearrange("(kc p) t -> p kc t", p=128), yts[:], reads=[ytb])
                kb.barrier()

        def phase4():
            T4 = 512
            HL = 8
            W4 = T4 + 2 * HL
            with ExitStack() as st:
                pw = sb(st, "p4pw", [128, 16, 512], BF16)
                b_pw = kb.buf("pw")
                kb.dma(pool, pw[:], pool_w.rearrange("(kc p) n -> p kc n", p=128), writes=[b_pw])
                ur = Ring(st, "p4u", [128, 16, W4], F32, 2)
                w1 = sb(st, "p4w1", [128, 4, W4], F32)
                w2 = sb(st, "p4w2", [128, 4, W4], F32)
                b_w1, b_w2 = kb.buf("w1"), kb.buf("w2")
                pdr = Ring(st, "p4pd", [128, 16, T4], BF16, 2)
                stgb = Ring(st, "p4stg", [128, 512], BF16, 4)
                def p4load(tt):
                    t0 = tt * T4
                    u, ub = ur.next()
                    lo = max(t0 - HL, 0)
                    hi = min(t0 + T4 + HL, S)
                    if t0 == 0:
                        kb.op(pool, lambda u=u: G.memset(u[:, :, 0:HL], 0.0), writes=[ub])
                    if t0 + T4 == S:
                        kb.op(pool, lambda u=u: G.memset(u[:, :, T4 + HL:W4], 0.0), writes=[ub])
                    kb.dma(sp, u[:, :, lo - (t0 - HL):hi - (t0 - HL)],
                           upT_s[:, lo:hi].rearrange("(c p) t -> p c t", p=128), writes=[ub])
                    return u, ub

                uld = {0: p4load(0)}
                for tt in range(S // T4):
                    t0 = tt * T4
                    if tt + 1 < S // T4:
                        uld[tt + 1] = p4load(tt + 1)
                    u, ub = uld.pop(tt)
                    pd, pdb = pdr.next()
                    for gi, w in enumerate((2, 4, 8, 16)):
                        us = u[:, gi * 4:(gi + 1) * 4, :]
                        kb.op(dve, lambda us=us: V.tensor_tensor(out=w1[:, :, 1:W4], in0=us[:, :, 0:W4 - 1],
                                                                 in1=us[:, :, 1:W4], op=ALU.add),
                              reads=[ub], writes=[b_w1])
                        cur, curb, oth, othb = w1, b_w1, w2, b_w2
                        lo_v, hi_v = 1, W4
                        sh = 1
                        ww = 2
                        while ww < w:
                            nlo, nhi = lo_v + sh, hi_v - sh
                            kb.op(dve, lambda cur=cur, oth=oth, nlo=nlo, nhi=nhi, sh=sh: V.tensor_tensor(
                                out=oth[:, :, nlo:nhi], in0=cur[:, :, nlo - sh:nhi - sh], in1=cur[:, :, nlo + sh:nhi + sh],
                                op=ALU.add), reads=[curb], writes=[othb])
                            cur, curb, oth, othb = oth, othb, cur, curb
                            lo_v, hi_v = nlo, nhi
                            sh *= 2
                            ww *= 2
                        h = w // 2
                        if t0 == 0:
                            for t in range(h):
                                cnt = (t + (w - h)) - 0
                                kb.op(dve, lambda cur=cur, t=t, cnt=cnt, w=w: V.tensor_scalar(
                                    out=cur[:, :, HL + t:HL + t + 1], in0=cur[:, :, HL + t:HL + t + 1],
                                    scalar1=float(w) / cnt, scalar2=None, op0=ALU.mult), writes=[curb])
                        if t0 + T4 == S:
                            for t in range(S - (w - h) + 1, S):
                                cnt = S - (t - h)
                                jj = HL + (t - t0)
                                kb.op(dve, lambda cur=cur, jj=jj, cnt=cnt, w=w: V.tensor_scalar(
                                    out=cur[:, :, jj:jj + 1], in0=cur[:, :, jj:jj + 1],
                                    scalar1=float(w) / cnt, scalar2=None, op0=ALU.mult), writes=[curb])
                        kb.op(dve, lambda cur=cur, us=us, pd=pd, gi=gi, w=w: V.scalar_tensor_tensor(
                            out=pd[:, gi * 4:(gi + 1) * 4, :], in0=cur[:, :, HL:HL + T4], scalar=1.0 / w,
                            in1=us[:, :, HL:HL + T4], op0=ALU.mult, op1=ALU.subtract),
                              reads=[curb, ub], writes=[pdb])
                    for gi in range(4):
                        for sub in range(4):
                            klist = [(gi * 4 + k, gi * 4 + k) for k in range(4)]
                            pt, ptb = mm_ws(pw, b_pw, sub, klist, pd, pdb, 0, T4)
                            so, sob = stgb.next()
                            fi = gi * 4 + sub
                            kb.op(act, lambda pt=pt, so=so, fi=fi: A_.activation(out=so[:], in_=pt[:], func=AF.Identity,
                                                                                 scale=pscale_t[:, fi:fi + 1]),
                                  reads=[ptb], writes=[sob])
                            kb.dma(sp, pbT_s[fi * 128:(fi + 1) * 128, t0:t0 + T4], so[:], reads=[sob])
                kb.barrier()

        def phase5():
            T = min(1024, S)
            BW = 256
            NB = D // BW
            nsb = BW // 128
            nth = T // 512
            with ExitStack() as st:
                yT = Ring(st, "p5yT", [128, KC, T], BF16, 1)
                pbT = Ring(st, "p5pbT", [128, 16, T], BF16, 1)
                gar = Ring(st, "p5ga", [128, nsb, T], BF16, 2)
                gbr = Ring(st, "p5gb", [128, nsb, T], BF16, 2)
                m1r = Ring(st, "p5m1", [128, 512], F32, 2)
                m2r = Ring(st, "p5m2", [128, 512], F32, 2)
                stgb = Ring(st, "p5stg", [128, 512], BF16, 4)
                ntile = S // T
                bl_up = [wblk(w_ssd_up, 0, KC, j * BW, BW) for j in range(NB)]
                bl_pu = [wblk(w_pool_up, 0, 16, j * BW, BW) for j in range(NB)]
                wsu = WStream(st, "p5wu", KC, BW, 2, bl_up * ntile)
                wsp = WStream(st, "p5wp", 16, BW, 2, bl_pu * ntile)
                kl16 = [(k, k) for k in range(16)]
                for tt in range(ntile):
                    t0 = tt * T
                    wsu.prefetch(tt * NB + 2)
                    wsp.prefetch(tt * NB + 2)
                    y_, yb_ = yT.next()
                    p_, pb_ = pbT.next()
                    kb.dma(sp, y_[:], yT_s[:, t0:t0 + T].rearrange("(kc p) t -> p kc t", p=128), writes=[yb_])
                    kb.dma(sp, p_[:], pbT_s[:, t0:t0 + T].rearrange("(kc p) t -> p kc t", p=128), writes=[pb_])
                    for j in range(NB):
                        wu, wub = wsu.get(tt * NB + j)
                        wp, wpb = wsp.get(tt * NB + j)
                        ga, gab = gar.next()
                        gb, gbb = gbr.next()
                        kb.dma(sp, ga[:], gT_s[j * BW:(j + 1) * BW, t0:t0 + T].rearrange("(c p) t -> p c t", p=128),
                               writes=[gab])
                        kb.dma(sp, gb[:], gT_s[D + j * BW:D + (j + 1) * BW, t0:t0 + T].rearrange("(c p) t -> p c t", p=128),
                               writes=[gbb])
                        for sub in range(nsb):
                            for th in range(nth):
                                tk = th * 512
                                pa, pab = mm_ws(wu, wub, sub, klist32, y_, yb_, tk, 512)
                                pb2, pbb2 = mm_ws(wp, wpb, sub, kl16, p_, pb_, tk, 512)
                                m1, m1b = m1r.next()
                                m2, m2b = m2r.next()
                                kb.op(dve, lambda pa=pa, ga=ga, m1=m1, sub=sub, tk=tk: V.tensor_tensor(
                                    out=m1[:], in0=pa[:], in1=ga[:, sub, tk:tk + 512], op=ALU.mult),
                                      reads=[pab, gab], writes=[m1b])
                                kb.op(dve, lambda pb2=pb2, gb=gb, m2=m2, sub=sub, tk=tk: V.tensor_tensor(
                                    out=m2[:], in0=pb2[:], in1=gb[:, sub, tk:tk + 512], op=ALU.mult),
                                      reads=[pbb2, gbb], writes=[m2b])
                                so, sob = stgb.next()
                                kb.op(pool, lambda m1=m1, m2=m2, so=so: G.tensor_tensor(out=so[:], in0=m1[:], in1=m2[:],
                                                                                        op=ALU.add),
                                      reads=[m1b, m2b], writes=[sob])
                                f0 = j * BW + sub * 128
                                kb.dma(sp, mixT_s[f0:f0 + 128, t0 + tk:t0 + tk + 512], so[:], reads=[sob])
                kb.barrier()

        def phase6():
            T = min(1024, S)
            nsub = T // 128
            with ExitStack() as st:
                mT = Ring(st, "p6mT", [128, KC, T], BF16, 1)
                xr = Ring(st, "p6x", [128, 512], F32, 4)
                stg = Ring(st, "p6stg", [128, 512], F32, 4)
                ntile = S // T
                bl = [wblk(w_out, 0, KC, j * 512, 512) for j in range(8)]
                ws = WStream(st, "p6w", KC, 512, 2, bl * ntile)
                for tt in range(ntile):
                    t0 = tt * T
                    ws.prefetch(tt * 8 + 2)
                    m_, mb_ = mT.next()
                    kb.dma(sp, m_[:], mixT_s[:, t0:t0 + T].rearrange("(kc p) t -> p kc t", p=128), writes=[mb_])
                    for j in range(8):
                        wt, wtb = ws.get(tt * 8 + j)
                        for ts in range(nsub):
                            r = t0 + ts * 128
                            xb, xbb = xr.next()
                            kb.dma(sp, xb[:], x[r:r + 128, j * 512:(j + 1) * 512], writes=[xbb])
                            pt, ptb = mm_as(wt, wtb, 512, KC, m_, mb_, ts)
                            so, sob = stg.next()
                            kb.op(dve, lambda pt=pt, xb=xb, so=so: V.tensor_tensor(out=so[:], in0=pt[:], in1=xb[:],
                                                                                   op=ALU.add),
                                  reads=[ptb, xbb], writes=[sob])
                            kb.dma(sp, x1_s[r:r + 128, j * 512:(j + 1) * 512], so[:], reads=[sob])
                kb.barrier()

        def phase7():
            T = min(1024, S)
            nsub = T // 128
            nth = T // 512
            with ExitStack() as st:
                hT = sb(st, "p7hT", [128, KC, T], BF16)
                hTb = kb.buf("hT")
                nt = NormT(st, "p7", nsub, 2)
                stgb = Ring(st, "p7stg", [128, 512], BF16, 6)
                rlr = Ring(st, "p7rl", [128, 512], F32, 3)
                ntile = S // T
                bl = [wblk(w_ff1, 0, KC, j * 512, 512) for j in range(32)]
                ws = WStream(st, "p7w", KC, 512, 2, bl * ntile)
                for tt in range(ntile):
                    t0 = tt * T
                    ws.prefetch(tt * 32 + 2)
                    nt.run(x1_s, t0, 1, hT, hTb)
                    for j in range(32):
                        wt, wtb = ws.get(tt * 32 + j)
                        for sub in range(4):
                            for th in range(nth):
                                pt, ptb = mm_ws(wt, wtb, sub, klist32, hT, hTb, th * 512, 512)
                                so, sob = stgb.next()
                                rl, rlb = rlr.next()
                                kb.op(act, lambda pt=pt, rl=rl: A_.activation(out=rl[:], in_=pt[:], func=AF.Relu),
                                      reads=[ptb], writes=[rlb])
                                kb.op(dve, lambda rl=rl, so=so: V.tensor_tensor(out=so[:], in0=rl[:], in1=rl[:],
                                                                                op=ALU.mult),
                                      reads=[rlb], writes=[sob])
                                f0 = j * 512 + sub * 128
                                tk = t0 + th * 512
                                kb.dma(sp, uT_s[f0:f0 + 128, tk:tk + 512], so[:], reads=[sob])
                kb.barrier()

        def phase8():
            T = min(1024, S)
            nsub = T // 128
            KG = 16
            NKG = DFF // (KG * 128)
            CH = 2048
            with ExitStack() as st:
                acc = sb(st, "p8acc", [128, nsub, CH], F32)
                accb = [kb.buf(f"acc{ts}") for ts in range(nsub)]
                ur = Ring(st, "p8u", [128, KG, T], BF16, 2)
                x1r = Ring(st, "p8x1", [128, 512], F32, 4)
                ntile = S // T
                bl = []
                for tt in range(ntile):
                    for ch in range(D // CH):
                        for kg in range(NKG):
                            for j in range(CH // 512):
                                bl.append(wblk(w_ff2, kg * KG * 128, KG, ch * CH + j * 512, 512))
                ws = WStream(st, "p8w", KG, 512, 3, bl)
                bi = 0
                for tt in range(ntile):
                    t0 = tt * T
                    for ch in range(D // CH):
                        c0 = ch * CH
                        for kg in range(NKG):
                            u_, ub_ = ur.next()
                            k0 = kg * KG * 128
                            kb.dma(sp, u_[:], uT_s[k0:k0 + KG * 128, t0:t0 + T].rearrange("(kc p) t -> p kc t", p=128),
                                   writes=[ub_])
                            for j in range(CH // 512):
                                wt, wtb = ws.get(bi)
                                bi += 1
                                for ts in range(nsub):
                                    if kg == 0:
                                        r = t0 + ts * 128
                                        xb, xbb = x1r.next()
                                        kb.dma(sp, xb[:], x1_s[r:r + 128, c0 + j * 512:c0 + (j + 1) * 512], writes=[xbb])
                                    pt, ptb = mm_as(wt, wtb, 512, KG, u_, ub_, ts)
                                    if kg == 0:
                                        kb.op(dve, lambda pt=pt, ts=ts, j=j, xb=xb: V.tensor_tensor(
                                            out=acc[:, ts, j * 512:(j + 1) * 512], in0=pt[:], in1=xb[:], op=ALU.add),
                                              reads=[ptb, xbb], writes=[accb[ts]])
                                    else:
                                        kb.op(dve, lambda pt=pt, ts=ts, j=j: V.tensor_tensor(
                                            out=acc[:, ts, j * 512:(j + 1) * 512], in0=acc[:, ts, j * 512:(j + 1) * 512],
                                            in1=pt[:], op=ALU.add), reads=[ptb], writes=[accb[ts]])
                        for ts in range(nsub):
                            r = t0 + ts * 128
                            kb.dma(sp, x2_s[r:r + 128, c0:c0 + CH], acc[:, ts, :], reads=[accb[ts]])
                kb.barrier()

        def phase9():
            T = 512
            nsub = T // 128
            with ExitStack() as st:
                hT = sb(st, "p9hT", [128, KC, T], BF16)
                hTb = kb.buf("hT")
                nt = NormT(st, "p9", nsub, 2)
                pT = sb(st, "p9pT", [128, 2, T], BF16)
                pTb = kb.buf("pT")
                wp = sb(st, "p9wp", [128, 2, D], BF16)
                wpb = kb.buf("wp")
                kb.dma(pool, wp[:], w_ple_proj.rearrange("(kc p) n -> p kc n", p=128), writes=[wpb])
                plr = Ring(st, "p9pl", [128, PLE], F32, 2)
                pbr = Ring(st, "p9pb", [128, PLE], BF16, 2)
                gr = Ring(st, "p9g", [128, 512], F32, 2)
                tr = Ring(st, "p9t", [128, 512], F32, 2)
                xr = Ring(st, "p9x", [128, 512], F32, 4)
                stg = Ring(st, "p9stg", [128, 512], F32, 4)
                ntile = S // T
                bl = [wblk(w_ple_gate, 0, KC, j * 512, 512) for j in range(8)]
                ws = WStream(st, "p9w", KC, 512, 2, bl * ntile)
                for tt in range(ntile):
                    t0 = tt * T
                    ws.prefetch(tt * 8 + 2)
                    nt.run(x2_s, t0, 2, hT, hTb)
                    for ts in range(nsub):
                        r = t0 + ts * 128
                        pl, plb = plr.next()
                        pb_, pbb = pbr.next()
                        kb.dma(sp, pl[:], pin[r:r + 128, :], writes=[plb])
                        kb.op(act, lambda pl=pl, pb_=pb_: A_.copy(out=pb_[:], in_=pl[:]), reads=[plb], writes=[pbb])
                        pt, ptb = nextbank()
                        ptv = pt[:].bitcast(BF16)
                        fns = [lambda k=k, ptv=ptv, pb_=pb_: T_.transpose(out=ptv[:, k * 128:(k + 1) * 128],
                                                                         in_=pb_[:, k * 128:(k + 1) * 128],
                                                                         identity=ident_b[:]) for k in range(2)]
                        kb.grp(pe, fns, reads=[pbb], writes=[ptb])
                        kb.op(dve, lambda ptv=ptv, ts=ts: V.tensor_copy(
                            out=pT[:, :, ts * 128:(ts + 1) * 128], in_=ptv[:, 0:256].rearrange("p (k t) -> p k t", k=2)),
                              reads=[ptb], writes=[pTb])
                    for j in range(8):
                        wt, wtb = ws.get(tt * 8 + j)
                        for ts in range(nsub):
                            r = t0 + ts * 128
                            xb, xbb = xr.next()
                            kb.dma(sp, xb[:], x2_s[r:r + 128, j * 512:(j + 1) * 512], writes=[xbb])
                            pg, pgb = mm_as(wt, wtb, 512, KC, hT, hTb, ts)
                            pp, ppb = nextbank()
                            fns = [lambda k=k, pp=pp, ts=ts, j=j: T_.matmul(
                                pp[:], lhsT=pT[:, k, ts * 128:(ts + 1) * 128], rhs=wp[:, k, j * 512:(j + 1) * 512],
                                start=(k == 0), stop=(k == 1)) for k in range(2)]
                            kb.grp(pe, fns, reads=[pTb, wpb], writes=[ppb])
                            g_, gb_ = gr.next()
                            kb.op(act, lambda pg=pg, g_=g_: A_.activation(out=g_[:], in_=pg[:], func=AF.Sigmoid),
                                  reads=[pgb], writes=[gb_])
                            t_, tb_ = tr.next()
                            kb.op(dve, lambda pp=pp, g_=g_, t_=t_: V.tensor_tensor(out=t_[:], in0=pp[:], in1=g_[:],
                                                                                   op=ALU.mult),
                                  reads=[ppb, gb_], writes=[tb_])
                            so, sob = stg.next()
                            kb.op(dve, lambda t_=t_, xb=xb, so=so: V.tensor_tensor(out=so[:], in0=t_[:], in1=xb[:],
                                                                                   op=ALU.add),
                                  reads=[tb_, xbb], writes=[sob])
                            kb.dma(sp, x3_s[r:r + 128, j * 512:(j + 1) * 512], so[:], reads=[sob])
                kb.barrier()

        def phase10():
            with ExitStack() as st:
                gf = sb(st, "gf", [128, D], F32)
                gfb = kb.buf("gf")
                kb.dma(sp, gf[:], g_final.partition_broadcast(128).rearrange("p o h -> p (o h)"), writes=[gfb])
                xl = Ring(st, "p10x", [128, D], F32, 4)
                jk = Ring(st, "p10j", [128, D], BF16, 1)
                ssr = Ring(st, "p10s", [128, 2], F32, 4)
                def p10load(c):
                    xt, xtb = xl.next()
                    kb.dma(sp, xt[:], x3_s[c * 128:(c + 1) * 128, :], writes=[xtb])
                    return xt, xtb

                lds = {c: p10load(c) for c in range(min(2, NCH))}
                for c in range(NCH):
                    r = c * 128
                    if c + 2 < NCH:
                        lds[c + 2] = p10load(c + 2)
                    xt, xtb = lds.pop(c)
                    j_, jb_ = jk.next()
                    ss, ssb = ssr.next()
                    kb.op(act, lambda xt=xt, j_=j_, ss=ss: A_.activation(out=j_[:], in_=xt[:], func=AF.Square,
                                                                         accum_out=ss[:, 0:1]),
                          reads=[xtb], writes=[jb_, ssb])
                    kb.op(act, lambda ss=ss: A_.activation(out=ss[:, 1:2], in_=ss[:, 0:1], func=AF.Sqrt,
                                                           scale=1.0 / D, bias=EPS), writes=[ssb])
                    kb.op(dve, lambda ss=ss: V.reciprocal(out=ss[:, 1:2], in_=ss[:, 1:2]), writes=[ssb])
                    kb.op(dve, lambda xt=xt, ss=ss: V.scalar_tensor_tensor(out=xt[:], in0=xt[:], scalar=ss[:, 1:2],
                                                                           in1=gf[:], op0=ALU.mult, op1=ALU.mult),
                          reads=[ssb, gfb], writes=[xtb])
                    kb.dma(sp, out[r:r + 128, :], xt[:], reads=[xtb])
                kb.barrier()

        plist = [phase1, phase2, phase3, phase3c, phase4, phase5, phase6, phase7, phase8, phase9, phase10]
        for i, ph in enumerate(plist):
            if i < phases:
                ph()
    return nc


def _host_params(inp):
    f = lambda a: np.ascontiguousarray(np.asarray(a, dtype=np.float32))
    fm = lambda v: f(np.asarray(v).reshape(KC, 128).T)
    gains = np.stack([fm(inp["norm_mix_g"][0]), fm(inp["norm_mlp_g"][0]), fm(inp["norm_ple_g"][0]),
                      fm(inp["ssd_norm_g"][0])], axis=1)
    cw = np.asarray(inp["conv_w"][0])
    cb = np.asarray(inp["conv_b"][0])
    cwb = np.concatenate([cw, cb[None, :]], axis=0)
    cwb = cwb.reshape(6, 48, 128).transpose(2, 1, 0)
    return {
        "w_in": f(inp["w_in"][0]),
        "w_ssd_up": f(inp["w_ssd_up"][0]),
        "pool_w": f(np.asarray(inp["pool_w"][0]).reshape(2048, 512)),
        "w_pool_up": f(inp["w_pool_up"][0]),
        "w_out": f(inp["w_out"][0]),
        "w_ff1": f(inp["w_ff1"][0]),
        "w_ff2": f(inp["w_ff2"][0]),
        "w_ple_gate": f(inp["w_ple_gate"][0]),
        "w_ple_proj": f(inp["w_ple_proj"][0]),
        "gains": f(gains),
        "g_final": f(np.asarray(inp["norm_final_g"]).reshape(1, D)),
        "conv_wb": f(cwb),
        "dtb": f(np.concatenate([np.asarray(inp["dt_bias_f"][0]), np.asarray(inp["dt_bias_b"][0])]).reshape(1, 128)),
        "alog": f(np.concatenate([np.asarray(inp["a_log_f"][0]), np.asarray(inp["a_log_b"][0])]).reshape(1, 128)),
        "dskip": f(np.asarray(inp["d_skip"][0]).reshape(1, NH)),
        "pscale": f(np.asarray(inp["pool_scale"][0]).reshape(16, 128).T),
    }


def kernel(**inp):
    xp = np.asarray(inp["x_prompt"])
    xs = np.asarray(inp["x_sample"])
    pp = np.asarray(inp["p_prompt"])[0]
    psm = np.asarray(inp["p_sample"])[0]
    seqs = [(xp[i], pp[i]) for i in range(xp.shape[0])] + [(xs[i], psm[i]) for i in range(xs.shape[0])]
    S = xp.shape[1]
    params = _host_params(inp)
    nc = build_program(S)
    core_of_seq = [0, 1, 2, 4, 5, 6]
    seq_of_core = {c: i for i, c in enumerate(core_of_seq)}
    zx = np.zeros((S, D), np.float32)
    zp = np.zeros((S, PLE), np.float32)
    in_maps = []
    for c in range(8):
        m = dict(params)
        if c in seq_of_core:
            xx, pq = seqs[seq_of_core[c]]
            m["x"] = np.ascontiguousarray(xx, dtype=np.float32)
            m["p"] = np.ascontiguousarray(pq, dtype=np.float32)
        else:
            m["x"] = zx
            m["p"] = zp
        in_maps.append(m)
    res = run_bass_kernel_spmd(nc, in_maps, core_ids=list(range(8)))
    outs = [np.asarray(res.results[core_of_seq[i]]["out"], dtype=np.float32) for i in range(len(seqs))]
    nb = xp.shape[0]
    y_prompt = np.stack(outs[:nb], axis=0)
    y_sample = np.stack(outs[nb:], axis=0)
    return (y_prompt, y_sample)
```
